# Optimizing a Trainium2 kernel written in Bass

```python
import math
import jax, jax.numpy as jnp
from jax import lax
import numpy as np

D_MODEL = 1024
BATCH = 4
SEQ = 4096
DEPTH = 1

HEAD_DIM = 64
A_HEADS = 8
A_KV_HEADS = 2
A_WINDOW = 128
B_HEADS = 8
B_KV_GROUPS = 2
CMP_BLOCK = 32
CMP_STRIDE = 16
CMP_HIDDEN = 256
SLC_BLOCK = 64
SLC_TOPK = 16
NSA_WINDOW = 512
NUM_BUCKETS = 32
MAX_DISTANCE = 128
D_FF = 2816
CONV_WIDTH = 3
Q_BLOCK = 128
EPS = 1e-6
NEG = -1e30
BIG = 1e30

A_Q_DIM = A_HEADS * HEAD_DIM
A_KV_DIM = A_KV_HEADS * HEAD_DIM
B_Q_DIM = B_HEADS * HEAD_DIM
B_KV_DIM = B_KV_GROUPS * HEAD_DIM
B_GATE_DIM = 3 * B_HEADS
IN_SPLITS = (A_Q_DIM, A_KV_DIM, A_KV_DIM, B_Q_DIM, B_KV_DIM, B_KV_DIM, B_KV_DIM, B_KV_DIM, B_KV_DIM, B_KV_DIM, B_GATE_DIM, D_MODEL, D_MODEL)
IN_DIM = sum(IN_SPLITS)
IN_OFFSETS = tuple(int(o) for o in np.cumsum(IN_SPLITS)[:-1])

kernel_name = 'hybrid_swa_nsa_convffn'


def rmsnorm(x, g):
    xf = x.astype(jnp.float32)
    y = xf * lax.rsqrt(jnp.mean(xf * xf, axis=-1, keepdims=True) + EPS)
    return (y * g.astype(jnp.float32)).astype(x.dtype)


def t5_bucket(dist):
    dist = jnp.maximum(dist, 0)
    max_exact = NUM_BUCKETS // 2
    d = jnp.maximum(dist, 1).astype(jnp.float32)
    large = max_exact + (jnp.log(d / max_exact) / math.log(MAX_DISTANCE / max_exact)
                         * (NUM_BUCKETS - max_exact)).astype(jnp.int32)
    large = jnp.minimum(large, NUM_BUCKETS - 1)
    return jnp.where(dist < max_exact, dist, large)


def head_bias(table_gr, dist):
    return jnp.transpose(table_gr[t5_bucket(dist)], (2, 3, 0, 1))


def selection_overlap(n_cmp, n_slc):
    r = SLC_BLOCK // CMP_STRIDE
    c = CMP_BLOCK // CMP_STRIDE
    j, m, n = np.meshgrid(np.arange(n_slc), np.arange(r), np.arange(c), indexing='ij')
    i = r * j + m - n
    ok = (i >= 0) & (i < n_cmp)
    mat = np.zeros((n_cmp, n_slc), np.float32)
    np.add.at(mat, (i[ok], j[ok]), 1.0)
    return mat


def causal_dwconv(u, w, b):
    c = u.shape[-1]
    out = lax.conv_general_dilated(u, w[:, None, :].astype(u.dtype), window_strides=(1,),
                                   padding=[(CONV_WIDTH - 1, 0)],
                                   dimension_numbers=('NWC', 'WIO', 'NWC'),
                                   feature_group_count=c)
    return out + b


def swa_sink_attention(q, k, v, sinks, table):
    bsz, seq, n_heads, dh = q.shape
    n_groups = k.shape[2]
    rep = n_heads // n_groups
    span = A_WINDOW + Q_BLOCK
    qg = q.reshape(bsz, seq, n_groups, rep, dh)
    kp = jnp.pad(k, ((0, 0), (A_WINDOW, 0), (0, 0), (0, 0)))
    vp = jnp.pad(v, ((0, 0), (A_WINDOW, 0), (0, 0), (0, 0)))
    qi = jnp.arange(Q_BLOCK)[:, None]
    ki = jnp.arange(span)[None, :]
    dist = qi + A_WINDOW - ki
    in_window = (dist >= 0) & (dist < A_WINDOW)
    bias = head_bias(table.reshape(NUM_BUCKETS, n_groups, rep), dist)
    sink = sinks.astype(jnp.float32).reshape(n_groups, rep)
    scale = dh ** -0.5

    def block(b):
        start = b * Q_BLOCK
        qb = lax.dynamic_slice_in_dim(qg, start, Q_BLOCK, axis=1)
        kb = lax.dynamic_slice_in_dim(kp, start, span, axis=1)
        vb = lax.dynamic_slice_in_dim(vp, start, span, axis=1)
        valid = in_window & (start - A_WINDOW + ki >= 0)
        logits = jnp.einsum('bqgrd,bkgd->bgrqk', qb, kb, preferred_element_type=jnp.float32) * scale + bias
        logits = jnp.where(valid, logits, NEG)
        sink_col = jnp.broadcast_to(sink[None, :, :, None, None], logits.shape[:-1] + (1,))
        probs = jax.nn.softmax(jnp.concatenate([logits, sink_col], axis=-1), axis=-1)[..., :span]
        out = jnp.einsum('bgrqk,bkgd->bqgrd', probs.astype(v.dtype), vb)
        return out.reshape(bsz, Q_BLOCK, n_heads * dh)

    out = lax.map(block, jnp.arange(seq // Q_BLOCK))
    return jnp.transpose(out, (1, 0, 2, 3)).reshape(bsz, seq, n_heads * dh)


def compress(kv, pos, w1, w2):
    bsz, seq, n_groups, dh = kv.shape
    n_cmp = (seq - CMP_BLOCK) // CMP_STRIDE + 1
    idx = jnp.arange(n_cmp)[:, None] * CMP_STRIDE + jnp.arange(CMP_BLOCK)[None, :]
    blocks = kv[:, idx] + pos[None, None, :, None, :]
    blocks = jnp.transpose(blocks, (0, 1, 3, 2, 4)).reshape(bsz, n_cmp, n_groups, CMP_BLOCK * dh)
    return jax.nn.gelu(blocks @ w1) @ w2


def nsa_attention(q, k_cmp, v_cmp, k_slc, v_slc, k_win, v_win, gates, table):
    bsz, seq, n_heads, dh = q.shape
    n_groups = k_slc.shape[2]
    rep = n_heads // n_groups
    n_cmp = k_cmp.shape[1]
    n_slc = seq // SLC_BLOCK
    topk = min(SLC_TOPK, n_slc)
    n_tok = topk * SLC_BLOCK
    scale = dh ** -0.5
    qg = q.reshape(bsz, seq, n_groups, rep, dh)
    table_gr = table.reshape(NUM_BUCKETS, n_groups, rep)
    table_g = jnp.transpose(table_gr, (1, 0, 2))
    cmp_end = jnp.arange(n_cmp) * CMP_STRIDE + CMP_BLOCK - 1
    overlap = jnp.asarray(selection_overlap(n_cmp, n_slc))
    k_slc_g = jnp.transpose(k_slc, (0, 2, 1, 3))
    v_slc_g = jnp.transpose(v_slc, (0, 2, 1, 3))
    span = NSA_WINDOW + Q_BLOCK
    kwp = jnp.pad(k_win, ((0, 0), (NSA_WINDOW, 0), (0, 0), (0, 0)))
    vwp = jnp.pad(v_win, ((0, 0), (NSA_WINDOW, 0), (0, 0), (0, 0)))
    qi = jnp.arange(Q_BLOCK)[:, None]
    ki = jnp.arange(span)[None, :]
    dist_w = qi + NSA_WINDOW - ki
    in_window = (dist_w >= 0) & (dist_w < NSA_WINDOW)
    bias_w = head_bias(table_gr, dist_w)
    bidx = jnp.arange(bsz)[:, None, None, None]
    gidx = jnp.arange(n_groups)[None, :, None, None]
    slc_ids = jnp.arange(n_slc)[None, :]

    def block(b):
        start = b * Q_BLOCK
        qpos = start + jnp.arange(Q_BLOCK)
        qb = lax.dynamic_slice_in_dim(qg, start, Q_BLOCK, axis=1)
        dist_c = qpos[:, None] - cmp_end[None, :]
        valid_c = dist_c >= 0
        lc = jnp.einsum('bqgrd,bcgd->bgrqc', qb, k_cmp, preferred_element_type=jnp.float32) * scale
        lc = jnp.where(valid_c, lc + head_bias(table_gr, dist_c), NEG)
        p_c = jax.nn.softmax(lc, axis=-1) * valid_c
        o_c = jnp.einsum('bgrqc,bcgd->bqgrd', p_c.astype(v_cmp.dtype), v_cmp)
        imp = jnp.einsum('bgrqc,cj->bgqj', p_c, overlap)
        qblk = (qpos // SLC_BLOCK)[:, None]
        forced = (slc_ids == 0) | (slc_ids == qblk) | (slc_ids == qblk - 1)
        score = jnp.where(forced, BIG, jnp.where(slc_ids > qblk, NEG, imp))
        _, sel = lax.top_k(score, topk)
        tok = (sel[..., None] * SLC_BLOCK + jnp.arange(SLC_BLOCK)).reshape(bsz, n_groups, Q_BLOCK, n_tok)
        ks = k_slc_g[bidx, gidx, tok]
        vs = v_slc_g[bidx, gidx, tok]
        dist_s = qpos[None, None, :, None] - tok
        bias_s = jnp.transpose(table_g[gidx, t5_bucket(dist_s)], (0, 1, 4, 2, 3))
        ls = jnp.einsum('bqgrd,bgqtd->bgrqt', qb, ks, preferred_element_type=jnp.float32) * scale + bias_s
        ls = jnp.where((dist_s >= 0)[:, :, None], ls, NEG)
        p_s = jax.nn.softmax(ls, axis=-1)
        o_s = jnp.einsum('bgrqt,bgqtd->bqgrd', p_s.astype(v_slc.dtype), vs)
        kb = lax.dynamic_slice_in_dim(kwp, start, span, axis=1)
        vb = lax.dynamic_slice_in_dim(vwp, start, span, axis=1)
        valid_w = in_window & (start - NSA_WINDOW + ki >= 0)
        lw = jnp.einsum('bqgrd,bkgd->bgrqk', qb, kb, preferred_element_type=jnp.float32) * scale + bias_w
        p_w = jax.nn.softmax(jnp.where(valid_w, lw, NEG), axis=-1)
        o_w = jnp.einsum('bgrqk,bkgd->bqgrd', p_w.astype(v_win.dtype), vb)
        gb = lax.dynamic_slice_in_dim(gates, start, Q_BLOCK, axis=1)
        out = gb[..., 0:1] * o_c + gb[..., 1:2] * o_s + gb[..., 2:3] * o_w
        return out.reshape(bsz, Q_BLOCK, n_heads * dh)

    out = lax.map(block, jnp.arange(seq // Q_BLOCK))
    return jnp.transpose(out, (1, 0, 2, 3)).reshape(bsz, seq, n_heads * dh)


def setup_inputs(seed: int = 0) -> dict:
    key = jax.random.key(seed)
    ks = jax.random.split(key, 20)
    f32 = jnp.float32

    def nrm(k, shape, scale):
        return jax.random.normal(k, shape, f32) * scale

    L, D = DEPTH, D_MODEL
    cmp_in = CMP_BLOCK * HEAD_DIM
    return {
        'x': nrm(ks[0], (BATCH, SEQ, D), 1.0),
        'norm_mix': 1.0 + nrm(ks[1], (L, D), 0.02),
        'w_in': nrm(ks[2], (L, D, IN_DIM), D ** -0.5),
        'attn_sinks': nrm(ks[3], (L, A_HEADS), 0.5),
        'cmp_pos_k': nrm(ks[4], (L, CMP_BLOCK, HEAD_DIM), 0.1),
        'cmp_w1_k': nrm(ks[5], (L, cmp_in, CMP_HIDDEN), cmp_in ** -0.5),
        'cmp_w2_k': nrm(ks[6], (L, CMP_HIDDEN, HEAD_DIM), CMP_HIDDEN ** -0.5),
        'cmp_pos_v': nrm(ks[7], (L, CMP_BLOCK, HEAD_DIM), 0.1),
        'cmp_w1_v': nrm(ks[8], (L, cmp_in, CMP_HIDDEN), cmp_in ** -0.5),
        'cmp_w2_v': nrm(ks[9], (L, CMP_HIDDEN, HEAD_DIM), CMP_HIDDEN ** -0.5),
        'w_up_a': nrm(ks[10], (L, A_Q_DIM, D), A_Q_DIM ** -0.5),
        'w_up_b': nrm(ks[11], (L, B_Q_DIM, D), B_Q_DIM ** -0.5),
        'w_out': nrm(ks[12], (L, D, D), D ** -0.5),
        'norm_ffn': 1.0 + nrm(ks[13], (L, D), 0.02),
        'w_ffn_in': nrm(ks[14], (L, D, 2 * D_FF), D ** -0.5),
        'conv_w': nrm(ks[15], (L, CONV_WIDTH, 2 * D_FF), CONV_WIDTH ** -0.5),
        'conv_b': nrm(ks[16], (L, 2 * D_FF), 0.01),
        'w_ffn_out': nrm(ks[17], (L, D_FF, D), D_FF ** -0.5),
        'rel_bias_table': nrm(ks[18], (NUM_BUCKETS, A_HEADS + B_HEADS), 0.5),
        'norm_final': 1.0 + nrm(ks[19], (D,), 0.02),
    }


def reference(x, norm_mix, w_in, attn_sinks, cmp_pos_k, cmp_w1_k, cmp_w2_k, cmp_pos_v, cmp_w1_v, cmp_w2_v,
              w_up_a, w_up_b, w_out, norm_ffn, w_ffn_in, conv_w, conv_b, w_ffn_out, rel_bias_table, norm_final):
    bsz, seq, _ = x.shape
    table_a = rel_bias_table[:, :A_HEADS]
    table_b = rel_bias_table[:, A_HEADS:]

    def heads(t, n):
        return t.reshape(bsz, seq, n, HEAD_DIM)

    for l in range(DEPTH):
        h = rmsnorm(x, norm_mix[l])
        proj = h @ w_in[l]
        (qa, ka, va, qb, kc, vc, ksl, vsl, kw, vw, g_nsa, g_a, g_b) = jnp.split(proj, IN_OFFSETS, axis=-1)
        y_a = swa_sink_attention(heads(qa, A_HEADS), heads(ka, A_KV_HEADS), heads(va, A_KV_HEADS),
                                 attn_sinks[l], table_a)
        k_cmp = compress(heads(kc, B_KV_GROUPS), cmp_pos_k[l], cmp_w1_k[l], cmp_w2_k[l])
        v_cmp = compress(heads(vc, B_KV_GROUPS), cmp_pos_v[l], cmp_w1_v[l], cmp_w2_v[l])
        nsa_gates = jax.nn.sigmoid(g_nsa).reshape(bsz, seq, B_KV_GROUPS, B_HEADS // B_KV_GROUPS, 3)
        y_b = nsa_attention(heads(qb, B_HEADS), k_cmp, v_cmp, heads(ksl, B_KV_GROUPS), heads(vsl, B_KV_GROUPS),
                            heads(kw, B_KV_GROUPS), heads(vw, B_KV_GROUPS), nsa_gates, table_b)
        merged = jax.nn.sigmoid(g_a) * (y_a @ w_up_a[l]) + jax.nn.sigmoid(g_b) * (y_b @ w_up_b[l])
        x = x + merged @ w_out[l]
        h = rmsnorm(x, norm_ffn[l])
        ug = causal_dwconv(h @ w_ffn_in[l], conv_w[l], conv_b[l])
        u, g = jnp.split(ug, 2, axis=-1)
        x = x + (jax.nn.silu(g) * u) @ w_ffn_out[l]
    return rmsnorm(x, norm_final)
```

```python
import math
from contextlib import ExitStack
import numpy as np
import ml_dtypes
BF_NP = ml_dtypes.bfloat16
import concourse.bass as bass
import concourse.mybir as mybir
from concourse.bass_utils import run_bass_kernel_spmd

F32 = mybir.dt.float32
BF = mybir.dt.bfloat16
AF = mybir.ActivationFunctionType
ALU = mybir.AluOpType
AX = mybir.AxisListType
NEGM = -30000.0
NQ = 16


class Buf:
    def __init__(self, name):
        self.name = name
        self.w = None
        self.r = []
        self.sem = None
        self.cnt = 0


class Sched:
    ENG = ['pe', 'act', 'dve', 'pool', 'sp']

    def __init__(self, nc, stack):
        self.nc = nc
        self.ops = []
        self.start = 0
        self.stack = stack
        self.esem = {e: stack.enter_context(nc.semaphore('sem_' + e)) for e in self.ENG}
        self.ecnt = {e: 0 for e in self.ENG}
        self.bar = stack.enter_context(nc.semaphore('sem_bar'))
        self.nphase = 0

    def add(self, eng, fn, reads=(), writes=(), dma=None, bg=False, nowaw=False, own_sem=False):
        i = len(self.ops)
        deps = set()
        for b in reads:
            if b.w is not None:
                deps.add(b.w)
        for b in writes:
            if b.w is not None and not nowaw:
                deps.add(b.w)
            deps.update(b.r)
        for b in reads:
            b.r.append(i)
        for b in writes:
            b.w = i
            b.r = []
        self.ops.append(dict(eng=eng, fn=fn, deps=deps, dma=dma, bg=bg, own_sem=own_sem))
        return i

    def emit(self):
        nc = self.nc
        ops = self.ops
        s0 = self.start
        for o in ops[s0:]:
            o['deps'] = {d for d in o['deps'] if d >= s0 or ops[d]['bg']}
            if o['eng'] == 'pe' and o['dma'] is None:
                o['deps'] = {d for d in o['deps'] if not (ops[d]['eng'] == 'pe' and ops[d]['dma'] is None)}
        need = [False] * len(ops)
        for o in ops[s0:]:
            for d in o['deps']:
                need[d] = True
        self.nphase += 1
        mine_last = {}
        for e in self.ENG:
            idxs = [i for i in range(s0, len(ops)) if ops[i]['eng'] == e and ops[i]['fn'] is not None and ops[i]['dma'] is None]
            mine_last[e] = idxs[-1] if idxs else None
            if idxs:
                need[idxs[-1]] = True
        alld = []
        for i in range(s0, len(ops)):
            o = ops[i]
            if o['dma'] is not None:
                b = o['dma']
                if b.sem is None:
                    b.sem = self.stack.enter_context(nc.semaphore('dsem_' + b.name))
                b.cnt += 16
                o['sig'] = (b.sem, b.cnt)
                if not o['bg']:
                    alld.append(i)
            elif o['own_sem']:
                o['sig'] = (self.stack.enter_context(nc.semaphore('osem_%d' % i)), 1)
            elif need[i] and o['fn'] is not None:
                self.ecnt[o['eng']] += 1
                o['sig'] = (self.esem[o['eng']], self.ecnt[o['eng']])
            else:
                o['sig'] = None
        with nc.Block() as block:
            reg = dict(pe=block.tensor, act=block.scalar, dve=block.vector, pool=block.gpsimd, sp=block.sync)
            for e in self.ENG:
                mine = [o for o in ops[s0:] if o['eng'] == e]

                def body(eh, mine=mine, e=e):
                    seen = {}

                    def wait_for(d):
                        if ops[d]['sig'] is None:
                            return
                        sem, val = ops[d]['sig']
                        k = id(sem)
                        if seen.get(k, 0) >= val:
                            return
                        eh.wait_ge(sem, val)
                        seen[k] = val
                    for o in mine:
                        for d in sorted(o['deps']):
                            wait_for(d)
                        if o['fn'] is None:
                            continue
                        ins = o['fn'](eh)
                        if o['sig'] is not None:
                            sem, val = o['sig']
                            ins.then_inc(sem, 16 if o['dma'] is not None else 1)
                    if e == 'sp':
                        last = {}
                        for d in alld:
                            sem, val = ops[d]['sig']
                            if id(sem) not in last or ops[last[id(sem)]]['sig'][1] < val:
                                last[id(sem)] = d
                        for d in sorted(last.values()):
                            wait_for(d)
                    if mine_last[e] is not None:
                        wait_for(mine_last[e])
                    eh.sem_inc(self.bar, 1)
                    eh.wait_ge(self.bar, 5 * self.nphase)
                reg[e](body)
        self.start = len(ops)


def bc_last(ap, n):
    return bass.AP(ap.tensor, ap.offset, [list(a) for a in ap.ap] + [[0, n]])


class Rot:
    def __init__(self, items):
        self.items = items
        self.i = 0

    def next(self):
        it = self.items[self.i % len(self.items)]
        self.i += 1
        return it


O_QA, O_KA, O_VA, O_QB, O_KC, O_VC, O_KSL, O_VSL, O_KW, O_VW, O_GN, O_GA, O_GB = (
    0, 512, 640, 768, 1280, 1408, 1536, 1664, 1792, 1920, 2048, 2072, 3096)


def build(debug=False):
    nc = bass.Bass("TRN2", target_bir_lowering=False)

    def di(n, s, dt=F32):
        return nc.dram_tensor(n, list(s), dt, kind="ExternalInput").ap()
    xk = di("xk", [4096, 1024])
    w_kf = di("w_kf", [1024, 640])
    w_v = di("w_v", [1024, 384])
    w_q = di("w_q", [1024, 1024])
    w_g = di("w_g", [1024, 2072])
    gam = di("gam", [3, 1024])
    sinks = di("sinks", [1, 8])
    table = di("table", [32, 16])
    posT = di("posT", [2, 64, 32])
    w1 = di("w1", [2, 2048, 256])
    w2 = di("w2", [2, 256, 64])
    w_upa = di("w_upa", [512, 1024])
    w_upb = di("w_upb", [512, 1024])
    w_out = di("w_out", [1024, 1024])
    w_fi = di("w_fi", [1024, 5632])
    cwb = di("cwb", [128, 4, 44])
    w_fo = di("w_fo", [2816, 1024])
    keymask = di("keymask", [128, 32])
    cmask = di("cmask", [128, 2])
    emat = di("emat", [64, 4096], BF)
    scoreadd = di("scoreadd", [128, NQ, 64])
    allowed = di("allowed", [128, NQ, 64])
    farlow = di("farlow", [128, 512], BF)
    shiftext = di("shiftext", [33, NQ, 256], BF)
    oha = di("oha", [33, 1024])
    ohb = di("ohb", [33, 1024])
    asel = di("asel", [128, 1])
    overlap = di("overlap", [128, 2, 64])
    out = nc.dram_tensor("out", [2048, 1024], F32, kind="ExternalOutput").ap()
    dbg = {}
    if debug:
        dbg['ya'] = nc.dram_tensor("dbg_ya", [2048, 512], F32, kind="ExternalOutput").ap()
        dbg['yb'] = nc.dram_tensor("dbg_yb", [2048, 512], F32, kind="ExternalOutput").ap()
        dbg['x1'] = nc.dram_tensor("dbg_x1", [2048, 1024], F32, kind="ExternalOutput").ap()
    biasd = nc.dram_tensor("biasd", [16, 1024], BF, kind="Internal").ap()
    x1d = nc.dram_tensor("x1d", [2048, 1024], F32, kind="Internal").ap()
    h2Td = nc.dram_tensor("h2Td", [128, 8 * 2048], BF, kind="Internal").ap()
    yabd = nc.dram_tensor("yabd", [2048, 1024], BF, kind="Internal").ap()
    cin = nc.dram_tensor("cin", [128, 256], BF, kind="Internal")
    w1b = nc.dram_tensor("w1b", [2, 64, 32, 256], BF, kind="Internal").ap()
    wqb = nc.dram_tensor("wqb", [1024, 1024], BF, kind="Internal").ap()
    wupab = nc.dram_tensor("wupab", [512, 1024], BF, kind="Internal").ap()
    wupbb = nc.dram_tensor("wupbb", [512, 1024], BF, kind="Internal").ap()
    wob = nc.dram_tensor("wob", [1024, 1024], BF, kind="Internal").ap()
    wfib = nc.dram_tensor("wfib", [1024, 5632], BF, kind="Internal").ap()
    wfob = nc.dram_tensor("wfob", [2816, 1024], BF, kind="Internal").ap()
    cout = nc.dram_tensor("cout", [256, 256], BF, kind="Internal")

    with ExitStack() as st:
        S = Sched(nc, st)

        def sbuf(stk, n, s, d):
            return stk.enter_context(nc.sbuf_tensor(n, list(s), d))
        ps = [st.enter_context(nc.psum_tensor("ps%d" % i, [128, 512], F32)) for i in range(8)]
        Bps = [Buf("ps%d" % i) for i in range(8)]

        ident = sbuf(st, "ident", [128, 128], BF)
        gam_mix = sbuf(st, "gam_mix", [128, 1024], F32)
        junk_t = sbuf(st, "junk_t", [128, 1024], BF)
        Bjunk = Buf("junk")
        arena0 = sbuf(st, "arena0", [128, 16384], BF)
        wg_t = arena0[:, :].rearrange("p (k n) -> p k n", k=8)
        Bwg = Buf("wg")
        Bconst = Buf("const")
        Bgam = Buf("gam")

        def norm_h_g(xt, Bx, gam_ap, tmp, lnexp=False, slack=0):
            ssq, rstd, h, Bt = tmp
            if lnexp:
                S.add('act', lambda e: e.activation(out=junk_t[:], in_=xt, func=AF.Square, accum_out=ssq[:]),
                      reads=[Bx], writes=[Bt, Bjunk])
                for _ in range(1 + slack):
                    yield
                S.add('dve', lambda e: e.tensor_scalar(out=rstd[:], in0=ssq[:], scalar1=1.0 / 1024, scalar2=1e-6,
                                                       op0=ALU.mult, op1=ALU.add), reads=[Bt], writes=[Bt])
                for _ in range(1 + slack):
                    yield
                S.add('act', lambda e: e.activation(out=rstd[:], in_=rstd[:], func=AF.Ln), reads=[Bt], writes=[Bt])
                S.add('act', lambda e: e.activation(out=rstd[:], in_=rstd[:], func=AF.Exp, scale=-0.5), reads=[Bt], writes=[Bt])
                for _ in range(1 + slack):
                    yield
                S.add('dve', lambda e: e.scalar_tensor_tensor(out=h[:], in0=xt, scalar=rstd[:, 0:1], in1=gam_ap,
                                                              op0=ALU.mult, op1=ALU.mult),
                      reads=[Bx, Bt, Bgam], writes=[Bt])
                yield
                return
            S.add('act', lambda e: e.activation(out=junk_t[:], in_=xt, func=AF.Square, accum_out=ssq[:]),
                  reads=[Bx], writes=[Bt, Bjunk])
            yield
            S.add('dve', lambda e: e.tensor_scalar(out=rstd[:], in0=ssq[:], scalar1=1.0 / 1024, scalar2=1e-6,
                                                   op0=ALU.mult, op1=ALU.add), reads=[Bt], writes=[Bt])
            yield
            S.add('act', lambda e: e.activation(out=rstd[:], in_=rstd[:], func=AF.Sqrt), reads=[Bt], writes=[Bt])
            yield
            S.add('dve', lambda e: e.reciprocal(out=rstd[:], in_=rstd[:]), reads=[Bt], writes=[Bt])
            yield
            S.add('dve', lambda e: e.scalar_tensor_tensor(out=h[:], in0=xt, scalar=rstd[:, 0:1], in1=gam_ap,
                                                          op0=ALU.mult, op1=ALU.mult),
                  reads=[Bx, Bt, Bgam], writes=[Bt])
            yield

        def trans_g(h, Bt, hT_out, BhT, bank, eng='act'):
            pT = ps[bank][:].bitcast(BF)
            for k in range(8):
                S.add('pe', lambda e, k=k: e.transpose(out=pT[:, k * 128:(k + 1) * 128], in_=h[:, k * 128:(k + 1) * 128],
                                                       identity=ident[:]), reads=[Bt, Bconst], writes=[Bps[bank]])
                if k % 4 == 3:
                    yield
            if eng == 'act':
                S.add('act', lambda e: e.copy(out=hT_out, in_=pT[:, 0:1024].rearrange("p (k t) -> p k t", k=8)),
                      reads=[Bps[bank]], writes=[BhT])
            else:
                S.add('dve', lambda e: e.tensor_copy(out=hT_out, in_=pT[:, 0:1024].rearrange("p (k t) -> p k t", k=8)),
                      reads=[Bps[bank]], writes=[BhT])
            yield

        def norm_T_g(xt, Bx, gam_ap, hT_out, BhT, tmp, bank):
            yield from norm_h_g(xt, Bx, gam_ap, tmp)
            yield from trans_g(tmp[2], tmp[3], hT_out, BhT, bank)

        def norm_T(xt, Bx, gam_ap, hT_out, BhT, tmp, bank):
            for _ in norm_T_g(xt, Bx, gam_ap, hT_out, BhT, tmp, bank):
                pass

        def zipgens_dyn(lst):
            while lst:
                for g in list(lst):
                    try:
                        next(g)
                    except StopIteration:
                        lst.remove(g)

        def zipgens(gens):
            gens = [g for g in gens if g is not None]
            while gens:
                alive = []
                for g in gens:
                    try:
                        next(g)
                        alive.append(g)
                    except StopIteration:
                        pass
                gens = alive

        cast_rr = [0]

        def load_cast(dst, src, stage_rot, Bdst, engs=('dve', 'act')):
            stg_t, Bst = stage_rot.next()
            n = 1
            for d_ in dst.shape[1:]:
                n *= d_
            sv = stg_t[0:dst.shape[0], 0:n]
            if len(dst.shape) == 3:
                sv = sv.rearrange("p (a b) -> p a b", a=dst.shape[1])
            S.add('sp', lambda e: e.dma_start(out=sv, in_=src), writes=[Bst], dma=Bst)
            eng = engs[cast_rr[0] % len(engs)]
            cast_rr[0] += 1
            if eng == 'act':
                S.add('act', lambda e: e.copy(out=dst, in_=sv), reads=[Bst], writes=[Bdst], nowaw=True)
            else:
                S.add(eng, lambda e: e.tensor_copy(out=dst, in_=sv), reads=[Bst], writes=[Bdst], nowaw=True)

        def mk_tmp(stk, n):
            return (sbuf(stk, "ssq" + n, [128, 1], F32),
                    sbuf(stk, "rstd" + n, [128, 1], F32), sbuf(stk, "hbf" + n, [128, 1024], BF), Buf("nt" + n))

        att = ExitStack()
        KA = sbuf(att, "KA", [128, 2, 4096], BF)
        KW = sbuf(att, "KW", [128, 2, 4096], BF)
        KS = sbuf(att, "KS", [128, 2, 4096], BF)
        Vx = sbuf(att, "Vx", [128, 32, 3, 2, 65], BF)
        KCT = sbuf(att, "KCT", [128, 2, 256], BF)
        VCx = sbuf(att, "VCx", [128, 2, 2, 129], BF)
        bias_t = sbuf(att, "bias_t", [128, 8, 512], BF)
        bandx = sbuf(att, "bandx", [128, 2, 512], BF)
        farlow_t = sbuf(att, "farlow_t", [128, 512], BF)
        keymask_t = sbuf(att, "keymask_t", [128, 32], F32)
        cmask_t = sbuf(att, "cmask_t", [128, 2], F32)
        sinkexp = sbuf(att, "sinkexp", [128, 8], F32)
        BKA, BKW, BKS, BVx, BKCT, BVCx, Bbias = [Buf(n) for n in "KA KW KS Vx KCT VCx bias".split()]

        def bias_chain(stk):
            tabx = sbuf(stk, "tabx", [33, 16], F32)
            tab31 = sbuf(stk, "tab31", [32, 16], F32)
            oh_t = sbuf(stk, "oh_t", [33, 1024], F32)
            vec_t = sbuf(stk, "vec_t", [16, 1024], BF)
            Jf = sbuf(stk, "Jf", [128, 128], F32)
            Jb = sbuf(stk, "Jb", [128, 128], BF)
            Hxs = Rot([(sbuf(stk, "Hx%d" % i, [128, 512], BF), Buf("Hx%d" % i)) for i in range(4)])
            Bp0, Bvec, Boh, Btab, Bt31, BJ = [Buf(n) for n in "bc_p0 bc_vec bc_oh bc_tab bc_t31 bc_J".split()]
            S.add('sp', lambda e: e.dma_start(out=tabx[0:32, :], in_=table), writes=[Btab], dma=Btab)
            S.add('sp', lambda e: e.dma_start(out=tab31[:], in_=table[31:32, :].rearrange("a n -> (a n)").partition_broadcast(32)),
                  writes=[Bt31], dma=Bt31)
            S.add('pool', lambda e: e.memset(tabx[32:33, :], NEGM), writes=[Btab])
            S.add('dve', lambda e: e.tensor_sub(out=tabx[0:32, :], in0=tabx[0:32, :], in1=tab31[:]), reads=[Btab, Bt31], writes=[Btab])
            yield
            for m in range(2):
                S.add('sp', lambda e, m=m: e.dma_start(out=oh_t[:, :], in_=(oha if m == 0 else ohb)), writes=[Boh], dma=Boh)
                for hf in range(2):
                    S.add('pe', lambda e, m=m, hf=hf: e.matmul(ps[0][0:8, :], lhsT=tabx[:, m * 8:(m + 1) * 8],
                                                               rhs=oh_t[:, hf * 512:(hf + 1) * 512], start=True, stop=True),
                          reads=[Btab, Boh], writes=[Bps[0]])
                    S.add('act', lambda e, m=m, hf=hf: e.copy(out=vec_t[0:8, hf * 512:(hf + 1) * 512], in_=ps[0][0:8, :]),
                          reads=[Bps[0]], writes=[Bvec])
                    S.add('sp', lambda e, m=m, hf=hf: e.dma_start(out=biasd[m * 8:(m + 1) * 8, hf * 512:(hf + 1) * 512],
                                                                   in_=vec_t[0:8, hf * 512:(hf + 1) * 512]),
                          reads=[Bvec], writes=[Bp0], dma=Bvec)
                    yield
            bt = biasd.tensor
            S.add('pool', lambda e: e.memset(Jf[:], 0.0), writes=[BJ])
            S.add('pool', lambda e: e.affine_select(out=Jf[:], in_=Jf[:], pattern=[[1, 128]], compare_op=ALU.not_equal,
                                                    fill=1.0, base=-127, channel_multiplier=1), reads=[BJ], writes=[BJ])
            S.add('dve', lambda e: e.tensor_copy(out=Jb[:], in_=Jf[:]), reads=[BJ], writes=[BJ])
            yield
            for m in range(2):
                for kind in range(2):
                    for g in range(2):
                        idx = m * 4 + kind * 2 + g
                        Hx, BHx = Hxs.next()
                        src = bass.AP(bt, (m * 8 + 4 * g) * 1024 + 512 + 128 * kind - 127, [[1, 128], [1024, 4], [1, 128]])
                        S.add('sp', lambda e, Hx=Hx, src=src: e.dma_start(
                            out=Hx[:, :].rearrange("p (h q) -> p h q", h=4), in_=src), reads=[Bp0], writes=[BHx], dma=BHx)
                        S.add('pe', lambda e, Hx=Hx: e.matmul(ps[0][:, :], lhsT=Jb[:], rhs=Hx[:, :], start=True, stop=True),
                              reads=[BJ, BHx], writes=[Bps[0]])
                        S.add('act', lambda e, idx=idx: e.activation(out=bias_t[:, idx, :], in_=ps[0][:, :], func=AF.Exp), reads=[Bps[0]], writes=[Bbias])
                        yield
            for g in range(2):
                src = bass.AP(bt, (8 + 4 * g) * 1024 + 512 - 383, [[16, 32], [1024, 4], [1, 128]])
                S.add('sp', lambda e, g=g, src=src: e.dma_start(out=bandx[0:32, g, :].rearrange("p (h q) -> p h q", h=4), in_=src),
                      reads=[Bp0], writes=[Bbias], dma=Bbias)
            yield

        Bpre = {n: Buf('pre_' + n) for n in ('w1', 'wq', 'wup', 'wo', 'wfi', 'wfo')}
        kvs = ExitStack()
        KCr = sbuf(kvs, "KCr", [64, 2, 2, 16, 260], BF)
        BKCr = Buf("KCr")
        p1w = ExitStack()
        wkf_t = sbuf(p1w, "wkf_t", [128, 8, 640], BF)
        wv_t = sbuf(p1w, "wv_t", [128, 8, 384], BF)
        Bw1p = [Buf("w1p%d" % k) for k in range(2)]
        xts1 = Rot([(sbuf(p1w, "xt%d" % i, [128, 1024], F32), Buf("xt%d" % i)) for i in range(2)])
        hTs = Rot([(sbuf(p1w, "hTG%d" % i, [128, 8, 512], BF), Buf("hTG%d" % i)) for i in range(2)])
        stg0 = Rot([(xts1.items[0][0][:, :], xts1.items[0][1]), (xts1.items[1][0][:, :], xts1.items[1][1])] +
                   [(hTs.items[i][0][:, 4 * j:4 * j + 4, :].rearrange("p a b -> p (a b)").bitcast(F32), hTs.items[i][1]) for i in range(2) for j in range(2)])
        with ExitStack() as p0:
            identf = sbuf(p1w, "identf", [128, 128], F32)
            sk = sbuf(p1w, "sk", [128, 8], F32)
            t31 = sbuf(p1w, "t31", [128, 8], F32)
            Bp0 = Buf("p0")
            Bd = [Buf("c%d" % i) for i in range(12)]
            for k in range(8):
                load_cast(wkf_t[:, k, :], w_kf[k * 128:(k + 1) * 128, :], stg0, Bw1p[k // 4], engs=(('dve',) if k < 4 else ('act',)))
                load_cast(wv_t[:, k, :], w_v[k * 128:(k + 1) * 128, :], stg0, Bw1p[k // 4], engs=(('dve',) if k < 4 else ('act',)))
            S.add('pool', lambda e: e.memset(identf[:], 0.0), writes=[Bconst])
            S.add('pool', lambda e: e.affine_select(out=identf[:], in_=identf[:], pattern=[[-1, 128]], compare_op=ALU.not_equal,
                                                    fill=1.0, base=0, channel_multiplier=1), reads=[Bconst], writes=[Bconst])
            S.add('dve', lambda e: e.tensor_copy(out=ident[:], in_=identf[:]), reads=[Bconst], writes=[Bconst])
            S.add('sp', lambda e: e.dma_start(out=gam_mix[:], in_=gam[0:1, :].rearrange("a n -> (a n)").partition_broadcast(128)),
                  writes=[Bgam], dma=Bgam)
            S.add('sp', lambda e: e.dma_start(out=keymask_t[:], in_=keymask), writes=[Bd[0]], dma=Bd[0])
            S.add('sp', lambda e: e.dma_start(out=cmask_t[:], in_=cmask), writes=[Bd[1]], dma=Bd[1])
            S.add('sp', lambda e: e.dma_start(out=farlow_t[:], in_=farlow), writes=[Bd[4]], dma=Bd[4])
            S.add('act', lambda e: e.activation(out=farlow_t[:], in_=farlow_t[:], func=AF.Exp), reads=[Bd[4]], writes=[Bconst])
            for g in range(2):
                S.add('sp', lambda e, g=g: e.dma_start(out=KS[64:128, g, :], in_=emat), writes=[Bd[6 + g]], dma=Bd[6 + g])
            S.add('sp', lambda e: e.dma_start(out=sk[:], in_=sinks.rearrange("a n -> (a n)").partition_broadcast(128)),
                  writes=[Bd[11]], dma=Bd[11])
            S.add('sp', lambda e: e.dma_start(out=t31[:], in_=table[31:32, 0:8].rearrange("a n -> (a n)").partition_broadcast(128)),
                  writes=[Bd[11]], dma=Bd[11])
            S.add('dve', lambda e: e.tensor_sub(out=sk[:], in0=sk[:], in1=t31[:]), reads=[Bd[11]], writes=[Bp0])
            S.add('act', lambda e: e.activation(out=sinkexp[:], in_=sk[:], func=AF.Exp), reads=[Bp0], writes=[Bconst])
            S.add('pool', lambda e: e.memset(bandx[32:64, :, :], 0.0), writes=[Bconst])
            S.add('pool', lambda e: e.memset(bandx[64:128, :, :], 0.0), writes=[Bconst])
            S.add('pool', lambda e: e.memset(bandx[32:33, :, :], NEGM), writes=[Bconst])
            S.add('pool', lambda e: e.memset(Vx[:, :, :, :, 64:65], 1.0), writes=[BVx])
            S.add('pool', lambda e: e.memset(VCx[:, :, :, 64:65], 1.0), writes=[BVCx])
            for g in range(2):
                S.add('pool', lambda e, g=g: e.dma_start(out=VCx[:, :, g, 65:129], in_=overlap), writes=[BVCx], dma=BVCx)

        with ExitStack() as p1:
            Bw = Bw1p
            for k in range(8):
                S.add('pool', lambda e, k=k: e.dma_start(out=wg_t[:, k, :], in_=w_g[k * 128:(k + 1) * 128, 0:2048]), writes=[Bwg], dma=Bwg)
            S.add('pool', lambda e: e.memset(KCr[:, :, :, :, 256:260], 0.0), writes=[BKCr])
            Bpad = Buf("kpad")
            S.add('pool', lambda e: e.memset(KA[64:128, :, :], 0.0), writes=[Bpad])
            S.add('pool', lambda e: e.memset(KW[64:128, :, :], 0.0), writes=[Bpad])
            S.add('pool', lambda e: e.memset(KCT[64:128, :, :], 0.0), writes=[Bpad])
            bgq = []
            for kd in range(2):
                for hf in range(2):
                    bgq.append((lambda e, kd=kd, hf=hf: e.dma_start(out=w1b[kd, :, hf * 16:(hf + 1) * 16, :], in_=w1[kd, hf * 1024:(hf + 1) * 1024, :].rearrange("(l d) n -> d l n", d=64)), 'w1'))
            for hf in range(2):
                bgq.append((lambda e, hf=hf: e.dma_start(out=wqb[hf * 512:(hf + 1) * 512, :], in_=w_q[hf * 512:(hf + 1) * 512, :]), 'wq'))

            def issue_bg(n):
                for _ in range(n):
                    if bgq:
                        fn, nm = bgq.pop(0)
                        S.add('pool', fn, writes=[Bpre[nm]], dma=Bpre[nm], bg=True, nowaw=True)
            xts = xts1
            tmps = Rot([mk_tmp(p1, "a%d" % i) for i in range(4)])
            fb = Rot([1, 2])
            tb = Rot([0, 4])
            vb = Rot([3, 5])
            normed = {}

            def stageN(G):
                lst = []
                for t in range(4):
                    p = G * 4 + t
                    xt, Bx = xts.next()
                    S.add('sp', lambda e, xt=xt, p=p: e.dma_start(out=xt[:], in_=xk[p * 128:(p + 1) * 128, :]), writes=[Bx], dma=Bx)
                    tmp = tmps.next()
                    yield from norm_h_g(xt[:], Bx, gam_mix[:], tmp)
                    lst.append(tmp)
                normed[G] = lst

            def stageP(G):
                issue_bg(1)
                hTG, BhTG = hTs.next()
                lst = normed.pop(G)
                for t in range(4):
                    p = G * 4 + t
                    tmp = lst[t]
                    yield from trans_g(tmp[2], tmp[3], hTG[:, :, t * 128:(t + 1) * 128], BhTG, tb.next(), eng=('act' if t % 2 == 0 else 'dve'))
                for t in range(4):
                    p = G * 4 + t
                    b3 = vb.next()
                    for k in range(8):
                        S.add('pe', lambda e, k=k, t=t, b3=b3: e.matmul(ps[b3][:, 0:384], lhsT=hTG[:, k, t * 128:(t + 1) * 128],
                                                                        rhs=wv_t[:, k, :], start=(k == 0), stop=(k == 7)),
                              reads=[BhTG, Bw[k // 4]], writes=[Bps[b3]])
                        if k % 4 == 3:
                            yield
                    S.add('act', lambda e, p=p, b3=b3: e.copy(out=Vx[:, p, :, :, 0:64],
                                                              in_=ps[b3][:, 0:384].rearrange("p (a g d) -> p a g d", a=3, g=2)),
                          reads=[Bps[b3]], writes=[BVx])
                    yield
                for kind in range(5):
                    for g in range(2):
                        b = fb.next()
                        c0 = kind * 128 + g * 64
                        for k in range(8):
                            S.add('pe', lambda e, k=k, b=b, c0=c0: e.matmul(ps[b][0:64, :], lhsT=wkf_t[:, k, c0:c0 + 64],
                                                                            rhs=hTG[:, k, :], start=(k == 0), stop=(k == 7)),
                                  reads=[BhTG, Bw[k // 4]], writes=[Bps[b]])
                            if k % 4 == 3:
                                yield
                        src = ps[b][0:64, :]
                        if kind == 0:
                            dst, Bdst = KA[0:64, g, G * 512:(G + 1) * 512], BKA
                        elif kind == 1:
                            dst, Bdst = KS[0:64, g, G * 512:(G + 1) * 512], BKS
                        elif kind == 2:
                            dst, Bdst = KW[0:64, g, G * 512:(G + 1) * 512], BKW
                        else:
                            dst, Bdst = KCr[:, kind - 3, g, :, G * 32:(G + 1) * 32].rearrange("p r n -> p n r"), BKCr
                            src = ps[b][0:64, :].rearrange("p (n r) -> p n r", r=16)
                        if (kind + g) % 2 == 0:
                            S.add('act', lambda e, src=src, dst=dst: e.copy(out=dst, in_=src), reads=[Bps[b]], writes=[Bdst])
                        else:
                            S.add('dve', lambda e, src=src, dst=dst: e.tensor_copy(out=dst, in_=src), reads=[Bps[b]], writes=[Bdst])
                        yield

            zipgens([stageN(0)])
            for G in range(8):
                zipgens([stageP(G), stageN(G + 1) if G + 1 < 8 else None])
            S.emit()
        p1w.close()

        with ExitStack() as pc:
            w1_t = sbuf(pc, "w1_t", [64, 2, 32, 256], BF)
            w2_t = sbuf(pc, "w2_t", [128, 2, 2, 64], BF)
            pos_t = sbuf(pc, "pos_t", [64, 2, 32], BF)
            hb = sbuf(pc, "hb", [128, 4], F32)
            Bw = Buf("wc")
            Bhb = Buf("hb")
            Bw1c = [[Buf("w1c%d_%d" % (kd, l4)) for l4 in range(2)] for kd in range(2)]
            for kd in range(2):
                for l4 in range(8):
                    S.add('sp', lambda e, kd=kd, l4=l4: e.dma_start(
                        out=w1_t[:, kd, l4 * 4:(l4 + 1) * 4, :], in_=w1b[kd, :, l4 * 4:(l4 + 1) * 4, :]),
                        reads=[Bpre['w1']], writes=[Bw1c[kd][l4 // 4]], dma=Bw1c[kd][l4 // 4], nowaw=True)
                S.add('pool', lambda e, kd=kd: e.dma_start(out=w2_t[:, kd, :, :], in_=w2[kd].rearrange("(c p) n -> p c n", p=128)),
                      writes=[Bw], dma=Bw)
                S.add('pool', lambda e, kd=kd: e.dma_start(out=pos_t[:, kd, :], in_=posT[kd]), writes=[Bw], dma=Bw)
            def compress_g():
                for kd in range(2):
                    for hc in range(2):
                        col = kd * 2 + hc
                        for l in range(32):
                            S.add('pe', lambda e, kd=kd, hc=hc, l=l, col=col: e.matmul(
                                ps[4][:, col:col + 1], lhsT=w1_t[:, kd, l, hc * 128:(hc + 1) * 128], rhs=pos_t[:, kd, l:l + 1],
                                start=(l == 0), stop=(l == 31), skip_group_check=True), reads=[Bw, Bw1c[kd][l // 16]], writes=[Bps[4]])
                S.add('dve', lambda e: e.tensor_copy(out=hb[:], in_=ps[4][:, 0:4]), reads=[Bps[4]], writes=[Bhb])
                yield
                gt = Rot([(sbuf(pc, "gx%d" % i, [128, 256], F32), sbuf(pc, "gu%d" % i, [128, 256], F32), Buf("gt%d" % i)) for i in range(2)])
                gel = Rot([(sbuf(pc, "gel%d" % i, [128, 2, 256], BF), Buf("gel%d" % i)) for i in range(2)])
                hbk = Rot([1, 2])
                for kd in range(2):
                    for g in range(2):
                        ge, Bge = gel.next()
                        for hc in range(2):
                            b = hbk.next()
                            col = kd * 2 + hc
                            for l in range(32):
                                rhs = KCr[:, kd, g, l % 16, (l // 16):(l // 16) + 256]
                                S.add('pe', lambda e, kd=kd, hc=hc, l=l, b=b, rhs=rhs: e.matmul(
                                    ps[b][:, 0:256], lhsT=w1_t[:, kd, l, hc * 128:(hc + 1) * 128], rhs=rhs,
                                    start=(l == 0), stop=(l == 31)), reads=[BKCr, Bw1c[kd][l // 16]], writes=[Bps[b]])
                                if l % 8 == 7:
                                    yield
                            gx, gu, Bg = gt.next()
                            S.add('dve', lambda e, b=b, gx=gx, col=col: e.tensor_scalar(out=gx[:], in0=ps[b][:, 0:256], scalar1=hb[:, col:col + 1],
                                                                                        scalar2=None, op0=ALU.add), reads=[Bps[b], Bhb], writes=[Bg])
                            S.add('act', lambda e, gx=gx, gu=gu: e.activation(out=gu[:], in_=gx[:], func=AF.Square), reads=[Bg], writes=[Bg])
                            S.add('dve', lambda e, gu=gu: e.tensor_scalar(out=gu[:], in0=gu[:], scalar1=0.044715, scalar2=1.0,
                                                                          op0=ALU.mult, op1=ALU.add), reads=[Bg], writes=[Bg])
                            S.add('dve', lambda e, gx=gx, gu=gu: e.tensor_mul(out=gu[:], in0=gu[:], in1=gx[:]), reads=[Bg], writes=[Bg])
                            S.add('act', lambda e, gu=gu: e.activation(out=gu[:], in_=gu[:], func=AF.Sigmoid, scale=1.5957691216057308),
                                  reads=[Bg], writes=[Bg])
                            S.add('dve', lambda e, gx=gx, gu=gu, ge=ge, hc=hc: e.tensor_mul(out=ge[:, hc, :], in0=gu[:], in1=gx[:]),
                                  reads=[Bg], writes=[Bge])
                            yield
                        if kd == 0:
                            for hc in range(2):
                                S.add('pe', lambda e, hc=hc, ge=ge: e.matmul(ps[5][0:64, 0:256], lhsT=w2_t[:, 0, hc, :], rhs=ge[:, hc, :],
                                                                             start=(hc == 0), stop=(hc == 1)), reads=[Bge, Bw], writes=[Bps[5]])
                            S.add('act', lambda e, g=g: e.copy(out=KCT[0:64, g, :], in_=ps[5][0:64, 0:256]), reads=[Bps[5]], writes=[BKCT])
                        else:
                            for ct in range(2):
                                for hc in range(2):
                                    S.add('pe', lambda e, hc=hc, ct=ct, ge=ge: e.matmul(
                                        ps[6][:, ct * 64:(ct + 1) * 64], lhsT=ge[:, hc, ct * 128:(ct + 1) * 128], rhs=w2_t[:, 1, hc, :],
                                        start=(hc == 0), stop=(hc == 1), skip_group_check=True), reads=[Bge, Bw], writes=[Bps[6]])
                            S.add('act', lambda e, g=g: e.copy(out=VCx[:, :, g, 0:64], in_=ps[6][:, 0:128].rearrange("p (c d) -> p c d", c=2)),
                                  reads=[Bps[6]], writes=[BVCx])

            zipgens([compress_g(), bias_chain(pc)])
            S.emit()
        kvs.close()

        with ExitStack() as pa:
            wq_t = sbuf(pa, "wq_t", [128, 8, 1024], BF)
            xts = Rot([(sbuf(pa, "xq%d" % i, [128, 1024], F32), Buf("xq%d" % i)) for i in range(2)])
            stga = xts
            scoreadd_t = sbuf(pa, "scoreadd_t", [128, NQ, 64], F32)
            allowed_t = sbuf(pa, "allowed_t", [128, NQ, 64], F32)
            Bsa = Buf("scoreadd")
            shift_t = sbuf(pa, "shift_t", [128, NQ, 256], BF)
            S.add('dve', lambda e: e.memset(shift_t[32:64, :, :], 0.0), writes=[Bsa])
            S.add('dve', lambda e: e.memset(shift_t[64:128, :, :], 0.0), writes=[Bsa])
            S.add('sp', lambda e: e.dma_start(out=shift_t[0:33, :, :], in_=shiftext), writes=[Bsa], dma=Bsa)
            S.add('sp', lambda e: e.dma_start(out=scoreadd_t[:], in_=scoreadd), writes=[Bsa], dma=Bsa)
            S.add('sp', lambda e: e.dma_start(out=allowed_t[:], in_=allowed), writes=[Bsa], dma=Bsa)
            wgn_t = sbuf(pa, "wgn_t", [128, 8, 24], BF)
            Bw = Buf("wa")
            Bwq = [Buf("wq%d" % k) for k in range(2)]
            bgq2 = []
            bgq2.append((lambda e: e.dma_start(out=wupab, in_=w_upa), 'wup'))
            bgq2.append((lambda e: e.dma_start(out=wupbb, in_=w_upb), 'wup'))
            for hf in range(2):
                bgq2.append((lambda e, hf=hf: e.dma_start(out=wob[hf * 512:(hf + 1) * 512, :], in_=w_out[hf * 512:(hf + 1) * 512, :]), 'wo'))
            for rb in range(8):
                bgq2.append((lambda e, rb=rb: e.dma_start(out=wfib[rb * 128:(rb + 1) * 128, :].rearrange("r (a b) -> r a b", b=1408),
                                                          in_=w_fi[rb * 128:(rb + 1) * 128, :].rearrange("r (a b) -> r a b", b=1408)), 'wfi'))
            for rb in range(4):
                bgq2.append((lambda e, rb=rb: e.dma_start(out=wfob[rb * 704:(rb + 1) * 704, :], in_=w_fo[rb * 704:(rb + 1) * 704, :]), 'wfo'))

            def issue_bg2(n):
                for _ in range(n):
                    if bgq2:
                        fn, nm = bgq2.pop(0)
                        S.add('pool', fn, writes=[Bpre[nm]], dma=Bpre[nm], bg=True, nowaw=True)
            for k in range(8):
                S.add('sp', lambda e, k=k: e.dma_start(out=wq_t[:, k, :], in_=wqb[k * 128:(k + 1) * 128, :]), reads=[Bpre['wq']], writes=[Bwq[k // 4]], dma=Bwq[k // 4], nowaw=True)
            S.add('pool', lambda e: e.dma_start(out=wgn_t[:], in_=w_g[:, 2048:2072].rearrange("(k p) n -> p k n", p=128)), writes=[Bw], dma=Bw)
            tmps = Rot([mk_tmp(pa, "b%d" % i) for i in range(1)])
            hTq = Rot([(sbuf(pa, "hTq%d" % i, [128, 8, 128], BF), Buf("hTq%d" % i)) for i in range(2)])
            QAs = Rot([(sbuf(pa, "QA%d" % i, [128, 2, 512], BF), Buf("QA%d" % i)) for i in range(2)])
            for i_ in range(2):
                S.add('pool', lambda e, i_=i_: e.memset(QAs.items[i_][0][64:128, :, :], 0.0), writes=[QAs.items[i_][1]])
            QSs = Rot([(sbuf(pa, "QS%d" % i, [128, 2, 512], BF), Buf("QSlo%d" % i), [Buf("QShi%d_%d" % (i, g)) for g in range(2)])
                       for i in range(2)])
            for i_ in range(2):
                S.add('pool', lambda e, i_=i_: e.memset(QSs.items[i_][0][64:128, :, :], 0.0), writes=QSs.items[i_][2])
            gns = Rot([(sbuf(pa, "gn%d" % i, [128, 24], F32), Buf("gn%d" % i)) for i in range(2)])
            negs = Rot([(sbuf(pa, "negs%d" % i, [128, 128], BF), Buf("negs%d" % i)) for i in range(2)])
            for i in range(2):
                S.add('pool', lambda e, i=i: e.memset(negs.items[i][0][:], 0.0), writes=[negs.items[i][1]])
            Pts = Rot([(sbuf(pa, "Pt%d" % i, [128, 512], BF), Buf("Pt%d" % i)) for i in range(4)])
            sbank = Rot([0, 1, 6])
            abank = Rot([2, 3, 4, 5])
            ybf = Rot([(sbuf(pa, "ybf%d" % i, [128, 512], F32), Buf("ybf%d" % i)) for i in range(1)])
            caccs = Rot([(sbuf(pa, "cacc%d" % i, [128, 4, 129], F32), Buf("cacc%d" % i)) for i in range(2)])
            yab = Rot([(sbuf(pa, "yab%d" % i, [128, 1024], BF), Buf("yab%d" % i)) for i in range(2)])
            sm = Rot([(sbuf(pa, "smA%d" % i, [128, 8], F32), sbuf(pa, "smB%d" % i, [128, 8], F32),
                       sbuf(pa, "smT%d" % i, [128, 4, 64], F32), Buf("sm%d" % i)) for i in range(4)])
            tk = Rot([(sbuf(pa, "imp%d" % i, [128, 64], F32), sbuf(pa, "sc%d" % i, [128, 64], F32), sbuf(pa, "wk%d" % i, [128, 64], F32),
                       sbuf(pa, "m8a%d" % i, [128, 8], F32), sbuf(pa, "m8b%d" % i, [128, 8], F32), Buf("tk%d" % i)) for i in range(2)])
            dbgt = Rot([(sbuf(pa, "dbgt%d" % i, [128, 1024], F32), Buf("dbgt%d" % i)) for i in range(2)]) if debug else None

            prepped = {}

            Qtok = Rot([(sbuf(pa, "Qtok%d" % i, [128, 1024], BF), Buf("Qtok%d" % i)) for i in range(2)])
            gtmp = Rot([(sbuf(pa, "gtmp%d" % i, [128, 24], F32), Buf("gtmp%d" % i)) for i in range(2)])

            def prep_g(i):
                I = 2 * i + 1
                xt, Bx = xts.next()
                S.add('sp', lambda e: e.dma_start(out=xt[:], in_=xk[I * 128:(I + 1) * 128, :]), writes=[Bx], dma=Bx)
                hT, BhT = hTq.next()
                tmp = tmps.next()
                for _ in range(5):
                    yield
                yield from norm_h_g(xt[:], Bx, gam_mix[:], tmp, lnexp=True, slack=2)
                yield
                yield
                yield from trans_g(tmp[2], tmp[3], hT[:], BhT, 7, eng='dve')
                QA, BQA = QAs.next()
                QS, BQSlo, BQShi = QSs.next()
                gn, Bgn = gns.next()
                Qt, BQt = Qtok.next()
                for m in range(2):
                    for k in range(8):
                        S.add('pe', lambda e, k=k, m=m: e.matmul(ps[7][:, :], lhsT=hT[:, k, :], rhs=wq_t[:, k, m * 512:(m + 1) * 512],
                                                                 start=(k == 0), stop=(k == 7)), reads=[BhT, Bwq[k // 4]], writes=[Bps[7]])
                        if k % 2 == 1:
                            yield
                    S.add('dve', lambda e, m=m: e.tensor_scalar(out=Qt[:, m * 512:(m + 1) * 512], in0=ps[7][:, :], scalar1=0.125, scalar2=None,
                                                                op0=ALU.mult), reads=[Bps[7]], writes=[BQt])
                    yield
                for k in range(8):
                    S.add('pe', lambda e, k=k: e.matmul(ps[7][:, 0:24], lhsT=hT[:, k, :], rhs=wgn_t[:, k, :], start=(k == 0), stop=(k == 7)),
                          reads=[BhT, Bw], writes=[Bps[7]])
                yield
                gt_, Bgt = gtmp.next()
                S.add('act', lambda e: e.activation(out=gt_[:], in_=ps[7][:, 0:24], func=AF.Exp, scale=-1.0), reads=[Bps[7]], writes=[Bgt])
                yield
                S.add('dve', lambda e: e.tensor_scalar(out=gt_[:], in0=gt_[:], scalar1=1.0, scalar2=None, op0=ALU.add), reads=[Bgt], writes=[Bgt])
                S.add('dve', lambda e: e.reciprocal(out=gn[:], in_=gt_[:]), reads=[Bgt], writes=[Bgn])
                yield
                pT = ps[7][:].bitcast(BF)
                for m in range(2):
                    for hh in range(8):
                        S.add('pe', lambda e, m=m, hh=hh: e.transpose(out=pT[0:64, hh * 128:(hh + 1) * 128],
                                                                      in_=Qt[:, m * 512 + hh * 64:m * 512 + (hh + 1) * 64], identity=ident[:]),
                              reads=[BQt, Bconst], writes=[Bps[7]])
                        if hh % 4 == 3:
                            yield
                    for g in range(2):
                        dst, Bdst = (QA[0:64, g, :], BQA) if m == 0 else (QS[0:64, g, :], BQSlo)
                        S.add('dve', lambda e, dst=dst, g=g: e.tensor_copy(out=dst, in_=pT[0:64, g * 512:(g + 1) * 512]), reads=[Bps[7]], writes=[Bdst])
                        yield
                prepped[i] = (QA, BQA, QS, BQSlo, BQShi, gn, Bgn)

            def run_steps_g(steps, dyn=None):
                n = len(steps)
                banks = [sbank.next() for _ in range(n)]

                def qk(j):
                    stp = steps[j]
                    b = banks[j]
                    l, r, rd = stp['qk']
                    has_m = stp['mask'] is not None and stp['mask'][0] == 'pe'
                    S.add('pe', lambda e: e.matmul(ps[b][:, :], lhsT=l, rhs=r, start=True, stop=not has_m), reads=rd, writes=[Bps[b]])
                    if has_m:
                        _, l2, r2, rd2 = stp['mask']
                        S.add('pe', lambda e: e.matmul(ps[b][:, :], lhsT=l2, rhs=r2, start=False, stop=True), reads=rd2, writes=[Bps[b]])
                qk(0)
                if n > 1:
                    qk(1)
                for j in range(n):
                    if j + 2 < n:
                        qk(j + 2)
                    stp = steps[j]
                    b = banks[j]
                    Pt, BPt = Pts.next()
                    if stp['abias'] is None:
                        S.add('act', lambda e, b=b, Pt=Pt: e.activation(out=Pt[:], in_=ps[b][:, :], func=AF.Exp),
                              reads=[Bps[b]], writes=[BPt])
                    else:
                        S.add('act', lambda e, b=b, Pt=Pt, stp=stp: e.activation(out=Pt[:], in_=ps[b][:, :], func=AF.Exp, bias=stp['abias']),
                              reads=[Bps[b], Bconst], writes=[BPt])
                    if stp['mask'] is not None and stp['mask'][0] == 'mul':
                        _, map_, mrd = stp['mask']
                        S.add('dve', lambda e, Pt=Pt, map_=map_: e.tensor_mul(out=Pt[:], in0=Pt[:], in1=map_), reads=[BPt] + mrd, writes=[BPt])
                    for h in range(4):
                        acc_ap, vr = stp['v'][h]
                        S.add('pe', lambda e, h=h, acc_ap=acc_ap, vr=vr, Pt=Pt, stp=stp: e.matmul(
                            acc_ap, lhsT=Pt[:, h * 128:(h + 1) * 128], rhs=vr, start=stp['first'][h], stop=stp['last'],
                            skip_group_check=True), reads=[BPt] + stp['vreads'], writes=stp['accB'])
                    if stp['post'] is not None:
                        r_ = stp['post']()
                        if r_ is not None:
                            if dyn is not None:
                                dyn.append(r_)
                            else:
                                for _ in r_:
                                    pass
                    yield

            def run_steps(steps):
                for _ in run_steps_g(steps):
                    pass

            def do_tile(i):
                I = 2 * i + 1
                if i == 0:
                    zipgens([prep_g(0)])
                QA, BQA, QS, BQSlo, BQShi, gn, Bgn = prepped.pop(i)
                ya_bf, Bya = yab.next()
                yb, Byb = ybf.next()
                steps = []
                dyn = []
                for g in range(2):
                    bX, bY = abank.next(), abank.next()
                    accs = [ps[bX][:, 0:129], ps[bX][:, 129:258], ps[bY][:, 0:129], ps[bY][:, 129:258]]

                    def post_cmp(g=g, bX=bX, bY=bY):
                        smA, smB, smT, Bsm = sm.next()
                        imp, sc, wk, m8a, m8b, Btk = tk.next()
                        ca, Bca = caccs.next()
                        for pr, bb in enumerate((bX, bY)):
                            S.add('dve', lambda e, pr=pr, bb=bb: e.tensor_copy(out=ca[:, pr * 2:pr * 2 + 2, :],
                                                                               in_=ps[bb][:, 0:258].rearrange("p (h c) -> p h c", c=129)),
                                  reads=[Bps[bb]], writes=[Bca])

                        def cmp_rest_g():
                            S.add('dve', lambda e: e.tensor_scalar(out=smA[:, 0:4], in0=ca[:, :, 64], scalar1=1e-30, scalar2=None, op0=ALU.max),
                                  reads=[Bca], writes=[Bsm])
                            yield
                            S.add('dve', lambda e: e.reciprocal(out=smA[:, 0:4], in_=smA[:, 0:4]), reads=[Bsm], writes=[Bsm])
                            yield
                            gsl = gn[:, g * 12:(g + 1) * 12].rearrange("p (h b) -> p h b", b=3)[:, :, 0]
                            S.add('dve', lambda e: e.tensor_mul(out=smB[:, 0:4], in0=smA[:, 0:4], in1=gsl), reads=[Bsm, Bgn], writes=[Bsm])
                            yield
                            S.add('dve', lambda e: e.tensor_tensor(
                                out=yb[:, g * 256:(g + 1) * 256].rearrange("p (h d) -> p h d", d=64),
                                in0=ca[:, :, 0:64], in1=bc_last(smB[:, 0:4], 64), op=ALU.mult), reads=[Bca, Bsm], writes=[Byb])
                            yield
                            for h in range(4):
                                if h == 0:
                                    S.add('dve', lambda e, h=h: e.tensor_scalar(out=imp[:], in0=ca[:, h, 65:129], scalar1=smA[:, h:h + 1],
                                                                                scalar2=None, op0=ALU.mult), reads=[Bca, Bsm], writes=[Btk])
                                else:
                                    S.add('dve', lambda e, h=h: e.scalar_tensor_tensor(out=imp[:], in0=ca[:, h, 65:129], scalar=smA[:, h:h + 1],
                                                                                       in1=imp[:], op0=ALU.mult, op1=ALU.add),
                                          reads=[Bca, Bsm, Btk], writes=[Btk])
                                yield
                            yield from topk_g()

                        def topk_g():
                            S.add('dve', lambda e: e.tensor_add(out=sc[:], in0=imp[:], in1=scoreadd_t[:, i, :]), reads=[Btk, Bsa], writes=[Btk])
                            yield
                            S.add('dve', lambda e: e.max(out=m8a[:], in_=sc[:]), reads=[Btk], writes=[Btk])
                            yield
                            S.add('dve', lambda e: e.match_replace(out=wk[:], in_to_replace=m8a[:], in_values=sc[:], imm_value=-3.0e38),
                                  reads=[Btk], writes=[Btk])
                            yield
                            S.add('dve', lambda e: e.max(out=m8b[:], in_=wk[:]), reads=[Btk], writes=[Btk])
                            yield
                            S.add('dve', lambda e: e.tensor_scalar(out=wk[:], in0=sc[:], scalar1=m8b[:, 7:8], scalar2=None, op0=ALU.is_ge),
                                  reads=[Btk], writes=[Btk])
                            yield
                            S.add('dve', lambda e: e.tensor_mul(out=wk[:], in0=wk[:], in1=allowed_t[:, i, :]), reads=[Btk, Bsa], writes=[Btk])
                            yield
                            ng, Bng = negs.next()
                            S.add('dve', lambda e: e.tensor_scalar(out=ng[:, 64:128], in0=wk[:], scalar1=-1.0, scalar2=-NEGM, op0=ALU.add, op1=ALU.mult),
                                  reads=[Btk], writes=[Bng])
                            yield
                            pT = ps[7][:].bitcast(BF)
                            S.add('pe', lambda e: e.transpose(out=pT[:, 0:128], in_=ng[:], identity=ident[:]), reads=[Bng, Bconst], writes=[Bps[7]])
                            yield
                            src = pT[64:128, 0:128]
                            srcb = bass.AP(src.tensor, src.offset, [list(src.ap[0]), [0, 4], list(src.ap[1])])
                            S.add('dve', lambda e: e.tensor_copy(out=QS[64:128, g, :].rearrange("p (h q) -> p h q", h=4), in_=srcb),
                                  reads=[Bps[7]], writes=[BQShi[g]])
                            yield

                        dyn.append(cmp_rest_g())
                    for ct in range(2):
                        steps.append(dict(
                            qk=(KCT[:, g, ct * 128:(ct + 1) * 128], QS[:, g, :], [BKCT, BQSlo, BQShi[g]]),
                            mask=('pe', shift_t[:, i, ct * 128:(ct + 1) * 128], bandx[:, g, :], [Bsa, Bbias]),
                            abias=None,
                            v=[(accs[h], VCx[:, ct, g, :]) for h in range(4)], vreads=[BVCx],
                            accB=[Bps[bX], Bps[bY]], first=[ct == 0 and h in (0, 2) for h in range(4)], last=(ct == 1),
                            post=post_cmp if ct == 1 else None))

                def std_branch(g, Js, klhs, Bk, qrhs, Bq, K, mixer, vkind, gate_br, is_swa, is_slc, first_yb, last_yb):
                    bA = abank.next()
                    a3 = ps[bA][:, 0:260].rearrange("p (h c) -> p h c", c=65)

                    def post():
                        smA, smB, smT, Bsm = sm.next()
                        if is_swa:
                            S.add('dve', lambda e: e.tensor_add(out=smA[:, 0:4], in0=a3[:, :, 64], in1=sinkexp[:, g * 4:(g + 1) * 4]),
                                  reads=[Bps[bA], Bconst], writes=[Bsm])
                            yield
                            S.add('dve', lambda e: e.reciprocal(out=smB[:, 0:4], in_=smA[:, 0:4]), reads=[Bsm], writes=[Bsm])
                            yield
                            S.add('dve', lambda e: e.tensor_tensor(
                                out=ya_bf[:, g * 256:(g + 1) * 256].rearrange("p (h d) -> p h d", d=64),
                                in0=a3[:, :, 0:64], in1=bc_last(smB[:, 0:4], 64), op=ALU.mult), reads=[Bps[bA], Bsm], writes=[Bya])
                            yield
                            return
                        S.add('dve', lambda e: e.reciprocal(out=smA[:, 0:4], in_=a3[:, :, 64]), reads=[Bps[bA]], writes=[Bsm])
                        yield
                        gsl = gn[:, g * 12:(g + 1) * 12].rearrange("p (h b) -> p h b", b=3)[:, :, gate_br]
                        S.add('dve', lambda e: e.tensor_mul(out=smB[:, 0:4], in0=smA[:, 0:4], in1=gsl), reads=[Bsm, Bgn], writes=[Bsm])
                        yield
                        S.add('dve', lambda e: e.tensor_tensor(out=smT[:], in0=a3[:, :, 0:64], in1=bc_last(smB[:, 0:4], 64), op=ALU.mult),
                              reads=[Bps[bA], Bsm], writes=[Bsm])
                        yield
                        ybg = yb[:, g * 256:(g + 1) * 256].rearrange("p (h d) -> p h d", d=64)
                        if last_yb:
                            S.add('pool', lambda e: e.tensor_add(
                                out=ya_bf[:, 512 + g * 256:512 + (g + 1) * 256].rearrange("p (h d) -> p h d", d=64), in0=ybg, in1=smT[:]),
                                reads=[Byb, Bsm], writes=[Bya])
                        else:
                            S.add('pool', lambda e: e.tensor_add(out=ybg, in0=ybg, in1=smT[:]), reads=[Byb, Bsm], writes=[Byb])
                        yield
                    for n_, J in enumerate(Js):
                        mask = None
                        if J == I:
                            mask = ('mul', bias_t[:, mixer * 4 + 0 + g, :], [Bbias])
                        elif J == I - 1:
                            mask = ('mul', bias_t[:, mixer * 4 + 2 + g, :], [Bbias])
                        elif (not is_swa) and (not is_slc) and J == I - 4:
                            mask = ('mul', farlow_t[:], [Bconst])
                        rd = [Bk, Bq] + ([BQShi[g]] if is_slc else [])
                        steps.append(dict(
                            qk=(klhs[0:K, g, J * 128:(J + 1) * 128], qrhs[0:K, g, :], rd),
                            mask=mask, abias=(keymask_t[:, J:J + 1] if (J == 0 and not is_slc) else None),
                            v=[(a3[:, h, :], Vx[:, J, vkind, g, :]) for h in range(4)], vreads=[BVx],
                            accB=[Bps[bA]], first=[n_ == 0 and h == 0 for h in range(4)], last=(n_ == len(Js) - 1),
                            post=post if n_ == len(Js) - 1 else None))

                for g in range(2):
                    std_branch(g, list(range(max(0, I - 4), I + 1)), KW, BKW, QS, BQSlo, 128, 1, 2, 2, False, False, False, False)
                    std_branch(g, [I - 1, I], KA, BKA, QA, BQA, 128, 0, 0, 0, True, False, False, False)
                dyn.append(run_steps_g(steps, dyn))
                zipgens_dyn(dyn)
                issue_bg2(1)
                steps = []
                for g in range(2):
                    std_branch(g, list(range(0, I + 1)), KS, BKS, QS, BQSlo, 128, 1, 1, 1, False, True, False, True)
                dyn2 = []
                dyn2.append(run_steps_g(steps, dyn2))
                if i + 1 < NQ:
                    dyn2.append(prep_g(i + 1))
                zipgens_dyn(dyn2)
                S.add('sp', lambda e, ya_bf=ya_bf, i=i: e.dma_start(out=yabd[i * 128:(i + 1) * 128, :], in_=ya_bf[:]), reads=[Bya], dma=Bya)
                if debug:
                    dt_, Bdt = dbgt.next()
                    S.add('dve', lambda e, dt_=dt_, ya_bf=ya_bf: e.tensor_copy(out=dt_[:], in_=ya_bf[:]), reads=[Bya], writes=[Bdt])
                    S.add('sp', lambda e, dt_=dt_, i=i: e.dma_start(out=dbg['ya'][i * 128:(i + 1) * 128, :], in_=dt_[:, 0:512]), reads=[Bdt], dma=Bdt)
                    S.add('sp', lambda e, dt_=dt_, i=i: e.dma_start(out=dbg['yb'][i * 128:(i + 1) * 128, :], in_=dt_[:, 512:1024]), reads=[Bdt], dma=Bdt)
            for i_ in range(NQ):
                do_tile(i_)
            S.emit()
        att.close()

        hsend = sbuf(st, "hsend", [128, 8, NQ, 2], BF)
        Bhs = Buf("hsend")
        wfo_s = ExitStack()
        wfo_t = sbuf(wfo_s, "wfo_t", [128, 22, 1024], BF)
        Bwo_ffn = Buf("wfo")
        with ExitStack() as pb:
            wup_t = sbuf(pb, "wup_t", [128, 2, 4, 1024], BF)
            wo_t = sbuf(pb, "wo_t", [128, 8, 1024], BF)
            gam_ffn = sbuf(pb, "gam_ffn", [128, 1024], F32)
            Bwup = [Buf("wup%d" % m) for m in range(2)]
            Bwo = [Buf("wo%d" % k) for k in range(2)]
            S.add('sp', lambda e: e.dma_start(out=gam_ffn[:], in_=gam[1:2, :].rearrange("a n -> (a n)").partition_broadcast(128)),
                  writes=[Bgam], dma=Bgam)
            xts = Rot([(sbuf(pb, "xb%d" % i, [128, 1024], F32), Buf("xb%d" % i)) for i in range(4)])
            tmps = Rot([mk_tmp(pb, "c%d" % i) for i in range(3)])
            hTq = Rot([(sbuf(pb, "hTb%d" % i, [128, 8, 128], BF), Buf("hTb%d" % i)) for i in range(2)])
            sgs = Rot([(sbuf(pb, "sg%d" % i, [128, 2048], F32), Buf("sg%d" % i)) for i in range(2)])
            yabs = Rot([(sbuf(pb, "yabl%d" % i, [128, 1024], BF), Buf("yabl%d" % i)) for i in range(3)])
            yTs = Rot([(sbuf(pb, "yT%d" % i, [128, 8, 128], BF), Buf("yT%d" % i)) for i in range(2)])
            mgs = Rot([(sbuf(pb, "mg%d" % i, [128, 1024], F32), sbuf(pb, "mgt%d" % i, [128, 1024], F32),
                        sbuf(pb, "mgb%d" % i, [128, 1024], BF), Buf("mg%d" % i)) for i in range(2)])
            mTs = Rot([(sbuf(pb, "mT%d" % i, [128, 8, 128], BF), Buf("mT%d" % i)) for i in range(2)])
            x1s = Rot([(sbuf(pb, "x1_%d" % i, [128, 1024], F32), Buf("x1_%d" % i)) for i in range(2)])
            h2Ts = Rot([(sbuf(pb, "h2T%d" % i, [128, 8, 128], BF), Buf("h2T%d" % i)) for i in range(2)])
            gb = Rot([0, 1])
            ub = Rot([2, 3])
            ob = Rot([4, 5])
            def pb_weights():
                for c in range(4):
                    S.add('sp', lambda e, c=c: e.dma_start(out=wup_t[:, 0, c, :], in_=wupab[c * 128:(c + 1) * 128, :]), reads=[Bpre['wup']], writes=[Bwup[0]], dma=Bwup[0], nowaw=True)
                for c in range(4):
                    S.add('sp', lambda e, c=c: e.dma_start(out=wup_t[:, 1, c, :], in_=wupbb[c * 128:(c + 1) * 128, :]), reads=[Bpre['wup']], writes=[Bwup[1]], dma=Bwup[1], nowaw=True)
                for k in range(8):
                    S.add('sp', lambda e, k=k: e.dma_start(out=wo_t[:, k, :], in_=wob[k * 128:(k + 1) * 128, :]), reads=[Bpre['wo']], writes=[Bwo[k // 4]], dma=Bwo[k // 4], nowaw=True)

            wfoq = list(range(22))

            def issue_wfo(n):
                for _ in range(n):
                    if wfoq:
                        c = wfoq.pop(0)
                        S.add('sp', lambda e, c=c: e.dma_start(out=wfo_t[:, c, :], in_=wfob[c * 128:(c + 1) * 128, :]), reads=[Bpre['wfo']], writes=[Bwo_ffn], dma=Bwo_ffn, nowaw=True)
            stA, stB, stA1 = {}, {}, {}

            def stageA(i):
                I = 2 * i + 1
                xt, Bx = xts.next()
                S.add('sp', lambda e: e.dma_start(out=xt[:], in_=xk[I * 128:(I + 1) * 128, :]), writes=[Bx], dma=Bx)
                yl, Byl = yabs.next()
                S.add('sp', lambda e: e.dma_start(out=yl[:], in_=yabd[i * 128:(i + 1) * 128, :]), writes=[Byl], dma=Byl)
                hT, BhT = hTq.next()
                yield from norm_T_g(xt[:], Bx, gam_mix[:], hT[:], BhT, tmps.next(), 7)
                stA1[i] = (xt, Bx, yl, Byl, hT, BhT)

            def stageA2(i):
                xt, Bx, yl, Byl, hT, BhT = stA1.pop(i)
                sg, Bsg = sgs.next()
                for cc in range(4):
                    b = gb.next()
                    for k in range(8):
                        S.add('pe', lambda e, k=k, b=b, cc=cc: e.matmul(ps[b][:, :], lhsT=hT[:, k, :], rhs=wg_t[:, k, cc * 512:(cc + 1) * 512],
                                                                        start=(k == 0), stop=(k == 7)), reads=[BhT, Bwg], writes=[Bps[b]])
                    S.add('act', lambda e, b=b, cc=cc: e.activation(out=sg[:, cc * 512:(cc + 1) * 512], in_=ps[b][:, :], func=AF.Sigmoid),
                          reads=[Bps[b]], writes=[Bsg])
                    yield
                yT, ByT = yTs.next()
                pT = ps[6][:].bitcast(BF)
                for c in range(8):
                    S.add('pe', lambda e, c=c: e.transpose(out=pT[:, c * 128:(c + 1) * 128], in_=yl[:, c * 128:(c + 1) * 128], identity=ident[:]),
                          reads=[Byl, Bconst], writes=[Bps[6]])
                S.add('dve', lambda e: e.tensor_copy(out=yT[:], in_=pT[:, 0:1024].rearrange("p (k t) -> p k t", k=8)), reads=[Bps[6]], writes=[ByT])
                yield
                stA[i] = (xt, Bx, sg, Bsg, yT, ByT)

            def stageB(i):
                xt, Bx, sg, Bsg, yT, ByT = stA.pop(i)
                mg, mgt, mgb, Bmg = mgs.next()
                for m in range(2):
                    for hf in range(2):
                        b = ub.next()
                        for c in range(4):
                            S.add('pe', lambda e, c=c, b=b, m=m, hf=hf: e.matmul(ps[b][:, :], lhsT=yT[:, m * 4 + c, :],
                                                                                rhs=wup_t[:, m, c, hf * 512:(hf + 1) * 512],
                                                                                start=(c == 0), stop=(c == 3)), reads=[ByT, Bwup[m]], writes=[Bps[b]])
                        dst = mg if m == 0 else mgt
                        S.add('dve', lambda e, b=b, m=m, hf=hf, dst=dst: e.tensor_mul(out=dst[:, hf * 512:(hf + 1) * 512], in0=ps[b][:, :],
                                                                                     in1=sg[:, m * 1024 + hf * 512:m * 1024 + (hf + 1) * 512]),
                              reads=[Bps[b], Bsg], writes=[Bmg])
                        yield
                S.add('dve', lambda e: e.tensor_add(out=mgb[:], in0=mg[:], in1=mgt[:]), reads=[Bmg], writes=[Bmg])
                yield
                mT, BmT = mTs.next()
                pT7 = ps[7][:].bitcast(BF)
                for c in range(8):
                    S.add('pe', lambda e, c=c: e.transpose(out=pT7[:, c * 128:(c + 1) * 128], in_=mgb[:, c * 128:(c + 1) * 128], identity=ident[:]),
                          reads=[Bmg, Bconst], writes=[Bps[7]])
                S.add('act', lambda e: e.copy(out=mT[:], in_=pT7[:, 0:1024].rearrange("p (k t) -> p k t", k=8)), reads=[Bps[7]], writes=[BmT])
                yield
                stB[i] = (xt, Bx, mT, BmT)

            def stageC(i):
                xt, Bx, mT, BmT = stB.pop(i)
                x1, Bx1 = x1s.next()
                for hf in range(2):
                    b = ob.next()
                    for c in range(8):
                        S.add('pe', lambda e, c=c, b=b, hf=hf: e.matmul(ps[b][:, :], lhsT=mT[:, c, :], rhs=wo_t[:, c, hf * 512:(hf + 1) * 512],
                                                                        start=(c == 0), stop=(c == 7)), reads=[BmT, Bwo[c // 4]], writes=[Bps[b]])
                    S.add('dve', lambda e, b=b, hf=hf: e.tensor_add(out=x1[:, hf * 512:(hf + 1) * 512], in0=ps[b][:, :],
                                                                    in1=xt[:, hf * 512:(hf + 1) * 512]), reads=[Bps[b], Bx], writes=[Bx1])
                    yield
                S.add('pool', lambda e: e.dma_start(out=x1d[i * 128:(i + 1) * 128, :], in_=x1[:]), reads=[Bx1], dma=Bx1)
                if debug:
                    S.add('sp', lambda e: e.dma_start(out=dbg['x1'][i * 128:(i + 1) * 128, :], in_=x1[:]), reads=[Bx1], dma=Bx1)
                h2T, Bh2T = h2Ts.next()
                yield from norm_T_g(x1[:], Bx1, gam_ffn[:], h2T[:], Bh2T, tmps.next(), 6)
                S.add('pool', lambda e: e.dma_start(
                    out=h2Td.rearrange("p (k t) -> p k t", k=8)[:, :, i * 128:(i + 1) * 128], in_=h2T[:]), reads=[Bh2T], dma=Bh2T)
                S.add('act', lambda e: e.copy(out=hsend[:, :, i, :], in_=h2T[:, :, 126:128]), reads=[Bh2T], writes=[Bhs])

            for s_ in range(NQ + 3):
                zipgens([stageC(s_ - 3) if 0 <= s_ - 3 < NQ else None,
                         stageB(s_ - 2) if 0 <= s_ - 2 < NQ else None,
                         stageA2(s_ - 1) if 0 <= s_ - 1 < NQ else None,
                         stageA(s_) if s_ < NQ else None])
                if s_ == 0:
                    pb_weights()
                elif s_ >= 2:
                    issue_wfo(2)
            issue_wfo(22)
            S.emit()

        with ExitStack() as pf:
            wfi_t = sbuf(pf, "wfi_t", [128, 8, 5632], BF)
            cw_t = sbuf(pf, "cw_t", [128, 4, 44], F32)
            a_t = sbuf(pf, "a_t", [128, 1], F32)
            gam_fin = sbuf(pf, "gam_fin", [128, 1024], F32)
            hrecv = sbuf(pf, "hrecv", [128, 2, 8, NQ, 2], BF)
            hh = sbuf(pf, "hh", [128, 8, NQ, 2], BF)
            hcb = sbuf(pf, "hcb", [128, 44, NQ, 2], F32)
            sav = arena0[:, 15360:16064].bitcast(F32).rearrange("p (c t x) -> p c t x", c=44, t=4)
            hd = hcb[:, 0:8, :, :]
            Bsav = Buf("sav")
            Bwfi = [Buf("wfi%d" % c) for c in range(22)]
            Bcw, Bcin, Bcout, Bhr, Bhh = [Buf(n) for n in "cw cin cout hr hh".split()]
            Bhcb = [Buf("hcb%d" % c) for c in range(44)]
            S.add('sp', lambda e: e.dma_start(out=cin.ap(), in_=hsend[:].rearrange("p k t c -> p (k t c)")), reads=[Bhs], writes=[Bcin], dma=Bcin)
            S.add('pool', lambda e: e.collective_compute("AllGather", ALU.bypass, replica_groups=[[0, 1], [2, 3], [4, 5], [6, 7]],
                                                         ins=[cin.ap().opt()], outs=[cout.ap().opt()]), reads=[Bcin], writes=[Bcout], own_sem=True)
            S.add('sp', lambda e: e.dma_start(out=cw_t[:], in_=cwb), writes=[Bcw], dma=Bcw)
            S.add('sp', lambda e: e.dma_start(out=a_t[:], in_=asel), writes=[Bcw], dma=Bcw)
            S.add('sp', lambda e: e.dma_start(out=gam_fin[:], in_=gam[2:3, :].rearrange("a n -> (a n)").partition_broadcast(128)),
                  writes=[Bgam], dma=Bgam)
            aT = arena0[:, 0:11264].rearrange("p (c n) -> p c n", c=22)
            BaT = Buf("actT")
            hg = arena0[:, 11264:11264 + 4096].rearrange("p (k t) -> p k t", k=8)
            Bhg = Buf("h2g")
            tus = Rot([(sbuf(pf, "tu%d" % i, [128, 4, 128], F32), Buf("tu%d" % i)) for i in range(3)])
            tgs = Rot([(sbuf(pf, "tg%d" % i, [128, 4, 128], F32), Buf("tg%d" % i)) for i in range(3)])
            htm = Rot([(sbuf(pf, "htm%d" % i, [128, NQ], F32), Buf("htm%d" % i)) for i in range(2)])
            x1s = Rot([(sbuf(pf, "x1f%d" % i, [128, 1024], F32), Buf("x1f%d" % i)) for i in range(2)])
            fin = Rot([(sbuf(pf, "fs%d" % i, [128, 1], F32), sbuf(pf, "fr%d" % i, [128, 1], F32),
                        sbuf(pf, "fo%d" % i, [128, 1024], F32), Buf("fin%d" % i)) for i in range(1)])
            ubk = Rot([0, 1])
            gbk = Rot([2, 3])
            obk = Rot([4, 5])
            hbk = Rot([4, 5])
            S.add('sp', lambda e: e.dma_start(out=hg, in_=h2Td.rearrange("p (k t) -> p k t", k=8)[:, :, 0:512]), writes=[Bhg], dma=Bhg)
            Bwfi_h = [[Buf("wfi%d_%d" % (half, c2)) for c2 in range(6)] for half in range(2)]
            for c2 in range(11):
                for half in range(2):
                    c0 = (2 * c2 + 22 * half) * 128
                    S.add('sp', lambda e, c0=c0: e.dma_start(out=wfi_t[:, :, c0:c0 + 256],
                                                            in_=wfib[:, c0:c0 + 256].rearrange("(k p) n -> p k n", p=128)),
                          reads=[Bpre['wfi']], writes=[Bwfi_h[half][c2 // 2]], dma=Bwfi_h[half][c2 // 2], nowaw=True)
            S.add('sp', lambda e: e.dma_start(out=hrecv[:].rearrange("p r k t c -> p r (k t c)"),
                                              in_=cout.ap().rearrange("(r p) n -> p r n", p=128)), reads=[Bcout], writes=[Bhr], dma=Bhr)
            pend = []
            ew_eng = ['pool']
            ubk3 = Rot([0, 1, 6])
            gbk3 = Rot([2, 3, 7])

            def fin_pair(c, res):
                (tu, Btu), (tg, Btg) = res
                S.add('act', lambda e: e.activation(out=tg[:], in_=tg[:], func=AF.Silu), reads=[Btg], writes=[Btg])
                S.add(ew_eng[0], lambda e: e.tensor_mul(out=aT[:, c, :].rearrange("p (t n) -> p t n", t=4), in0=tu[:], in1=tg[:]),
                      reads=[Btu, Btg], writes=[BaT])
            def hh_compute():
                G0 = hrecv[:, 0]
                G1 = hrecv[:, 1]
                S.add('dve', lambda e: e.tensor_copy(out=hd[:, :, 0, :], in_=G0[:, :, 0, :]), reads=[Bhr], writes=[Bhh])
                S.add('dve', lambda e: e.tensor_sub(out=hd[:, :, 1:NQ, :], in0=G0[:, :, 1:NQ, :], in1=G1[:, :, 0:NQ - 1, :]), reads=[Bhr], writes=[Bhh])
                S.add('dve', lambda e: e.tensor_scalar(out=hh[:, :, 0, :], in0=hd[:, :, 0, :], scalar1=a_t[:, 0:1], scalar2=None, op0=ALU.mult),
                      reads=[Bhh, Bcw], writes=[Bhh])
                S.add('dve', lambda e: e.scalar_tensor_tensor(out=hh[:, :, 1:NQ, :], in0=hd[:, :, 1:NQ, :], scalar=a_t[:, 0:1], in1=G1[:, :, 0:NQ - 1, :],
                                                              op0=ALU.mult, op1=ALU.add), reads=[Bhh, Bcw, Bhr], writes=[Bhh])

            def halo_chain(cc):
                half, c = cc // 22, cc % 22
                hb_ = hbk.next()
                for k in range(8):
                    S.add('pe', lambda e, k=k: e.matmul(ps[hb_][:, 0:32], lhsT=wfi_t[:, k, cc * 128:(cc + 1) * 128],
                                                        rhs=hh[:, k, :, :].rearrange("p t c -> p (t c)"),
                                                        start=(k == 0), stop=(k == 7)), reads=[Bwfi_h[half][c // 4], Bhh], writes=[Bps[hb_]])
                p2 = ps[hb_][:, 0:32].rearrange("p (t c) -> p t c", c=2)
                ht, Bht = htm.next()
                S.add('dve', lambda e: e.tensor_scalar(out=hcb[:, cc, :, 1], in0=p2[:, :, 1], scalar1=cw_t[:, 0, cc:cc + 1],
                                                       scalar2=None, op0=ALU.mult), reads=[Bps[hb_], Bcw], writes=[Bhcb[cc]])
                S.add('dve', lambda e: e.tensor_scalar(out=ht[:], in0=p2[:, :, 0], scalar1=cw_t[:, 0, cc:cc + 1],
                                                       scalar2=None, op0=ALU.mult), reads=[Bps[hb_], Bcw], writes=[Bht])
                S.add('dve', lambda e: e.scalar_tensor_tensor(out=hcb[:, cc, :, 0], in0=p2[:, :, 1], scalar=cw_t[:, 1, cc:cc + 1],
                                                              in1=ht[:], op0=ALU.mult, op1=ALU.add),
                      reads=[Bps[hb_], Bcw, Bht], writes=[Bhcb[cc]])
            hq = [p + 22 * h_ for p in range(22) for h_ in range(2)]
            for Gq in range(4):
                if Gq > 0:
                    ew_eng[0] = 'pool'
                for c in range(22):
                    res = []
                    for half, bk, ts_ in ((0, ubk3, tus), (1, gbk3, tgs)):
                        cc = c + 22 * half
                        b = bk.next()
                        for k in range(8):
                            S.add('pe', lambda e, k=k, b=b, cc=cc: e.matmul(ps[b][:, :], lhsT=wfi_t[:, k, cc * 128:(cc + 1) * 128], rhs=hg[:, k, :],
                                                                            start=(k == 0), stop=(k == 7)), reads=[Bwfi_h[half][c // 4], Bhg], writes=[Bps[b]])
                        tt, Btt = ts_.next()
                        p3 = ps[b][:, :].rearrange("p (t n) -> p t n", t=4)
                        S.add('act', lambda e, b=b, cc=cc, tt=tt: e.activation(out=tt[:].rearrange("p t n -> p (t n)"), in_=ps[b][:, :], func=AF.Identity,
                                                                               scale=cw_t[:, 2, cc:cc + 1], bias=cw_t[:, 3, cc:cc + 1]),
                              reads=[Bps[b], Bcw], writes=[Btt])
                        S.add('dve', lambda e, p3=p3, cc=cc, tt=tt: e.scalar_tensor_tensor(out=tt[:, :, 1:128], in0=p3[:, :, 0:127], scalar=cw_t[:, 1, cc:cc + 1],
                                                                                          in1=tt[:, :, 1:128], op0=ALU.mult, op1=ALU.add),
                              reads=[Bps[b], Bcw, Btt], writes=[Btt])
                        S.add('dve', lambda e, p3=p3, cc=cc, tt=tt: e.scalar_tensor_tensor(out=tt[:, :, 2:128], in0=p3[:, :, 0:126], scalar=cw_t[:, 0, cc:cc + 1],
                                                                                          in1=tt[:, :, 2:128], op0=ALU.mult, op1=ALU.add),
                              reads=[Bps[b], Bcw, Btt], writes=[Btt])
                        if Gq == 0:
                            S.add(ew_eng[0], lambda e, cc=cc, tt=tt: e.tensor_copy(out=sav[:, cc, :, :], in_=tt[:, :, 0:2]), reads=[Btt], writes=[Bsav])
                        else:
                            S.add(ew_eng[0], lambda e, cc=cc, tt=tt, Gq=Gq: e.tensor_add(out=tt[:, :, 0:2], in0=tt[:, :, 0:2], in1=hcb[:, cc, Gq * 4:(Gq + 1) * 4, :]),
                                  reads=[Btt, Bhcb[cc]], writes=[Btt])
                        res.append((tt, Btt))
                    pend.append((c, res))
                    if len(pend) > 1:
                        fin_pair(*pend.pop(0))
                    if Gq == 0 and c >= 8:
                        if c == 8:
                            hh_compute()
                        for _ in range(3):
                            if hq:
                                halo_chain(hq.pop(0))
                while pend:
                    fin_pair(*pend.pop(0))
                if Gq == 0:
                    while hq:
                        halo_chain(hq.pop(0))
                    S.add('dve', lambda e: e.tensor_add(out=sav[:, :, :, :], in0=sav[:, :, :, :], in1=hcb[:, :, 0:4, :]), reads=[Bsav] + Bhcb, writes=[Bsav])
                    S.add('act', lambda e: e.activation(out=sav[:, 22:44, :, :], in_=sav[:, 22:44, :, :], func=AF.Silu), reads=[Bsav], writes=[Bsav])
                    S.add('dve', lambda e: e.tensor_mul(out=aT[:, :, :].rearrange("p c (t n) -> p c t n", t=4)[:, :, :, 0:2], in0=sav[:, 0:22, :, :],
                                                        in1=sav[:, 22:44, :, :]), reads=[Bsav, BaT], writes=[BaT])
                if Gq + 1 < 4:
                    S.add('sp', lambda e, Gq=Gq: e.dma_start(out=hg, in_=h2Td.rearrange("p (k t) -> p k t", k=8)[:, :, (Gq + 1) * 512:(Gq + 2) * 512]),
                          writes=[Bhg], dma=Bhg)
                for t in range(4):
                    i = Gq * 4 + t
                    x1, Bx1 = x1s.next()
                    S.add('sp', lambda e, x1=x1, i=i: e.dma_start(out=x1[:], in_=x1d[i * 128:(i + 1) * 128, :]), writes=[Bx1], dma=Bx1)
                    x2, Bx2 = x1, Bx1
                    for hf in range(2):
                        b = obk.next()
                        for c in range(22):
                            S.add('pe', lambda e, c=c, b=b, hf=hf, t=t: e.matmul(ps[b][:, :], lhsT=aT[:, c, t * 128:(t + 1) * 128],
                                                                                rhs=wfo_t[:, c, hf * 512:(hf + 1) * 512],
                                                                                start=(c == 0), stop=(c == 21)), reads=[BaT, Bwo_ffn], writes=[Bps[b]])
                        S.add('dve', lambda e, b=b, hf=hf, x1=x1, x2=x2: e.tensor_add(out=x2[:, hf * 512:(hf + 1) * 512], in0=ps[b][:, :],
                                                                                      in1=x1[:, hf * 512:(hf + 1) * 512]), reads=[Bps[b], Bx1], writes=[Bx2])
                    fs, fr, fo, Bf = fin.next()
                    S.add('act', lambda e, fo=fo, fs=fs, x2=x2: e.activation(out=fo[:], in_=x2[:], func=AF.Square, accum_out=fs[:]), reads=[Bx2], writes=[Bf])
                    S.add('dve', lambda e, fs=fs, fr=fr: e.tensor_scalar(out=fr[:], in0=fs[:], scalar1=1.0 / 1024, scalar2=1e-6, op0=ALU.mult, op1=ALU.add),
                          reads=[Bf], writes=[Bf])
                    S.add('act', lambda e, fr=fr: e.activation(out=fr[:], in_=fr[:], func=AF.Sqrt), reads=[Bf], writes=[Bf])
                    S.add('dve', lambda e, fr=fr: e.reciprocal(out=fr[:], in_=fr[:]), reads=[Bf], writes=[Bf])
                    S.add('dve', lambda e, fr=fr, fo=fo, x2=x2: e.scalar_tensor_tensor(out=fo[:], in0=x2[:], scalar=fr[:, 0:1], in1=gam_fin[:],
                                                                                      op0=ALU.mult, op1=ALU.mult), reads=[Bf, Bx2, Bgam], writes=[Bf])
                    S.add('pool', lambda e, fo=fo, i=i: e.dma_start(out=out[i * 128:(i + 1) * 128, :], in_=fo[:]), reads=[Bf], dma=Bf)
            S.emit()
        wfo_s.close()
    return nc


def _t5_bucket(d):
    d = np.maximum(d, 0)
    dd = np.maximum(d, 1).astype(np.float32)
    large = 16 + (np.log(dd / np.float32(16)) / np.float32(math.log(128 / 16)) * np.float32(16)).astype(np.int32)
    large = np.minimum(large, 31)
    return np.where(d < 16, d, large)


def _host_consts(r):
    c = {}
    km = np.zeros((128, 32), np.float32)
    cm = np.zeros((128, 2), np.float32)
    if r == 0:
        km[:, 0] = NEGM
        cm[0:8, 0] = NEGM
    cm[127, 1] = NEGM
    c['keymask'] = km
    c['cmask'] = cm
    k = np.arange(4096)
    c['emat'] = (k[None, :] // 64 == np.arange(64)[:, None]).astype(np.float32).astype(BF_NP)
    sa = np.zeros((128, NQ, 64), np.float32)
    al = np.zeros((128, NQ, 64), np.float32)
    shift = 1 - r
    for i in range(NQ):
        I = 2 * i + 1
        qpos = I * 128 + np.arange(128)
        qblk = qpos // 64
        j = np.arange(64)[None, :]
        first = 2 * shift
        forced = (j == first) | (j == qblk[:, None]) | (j == qblk[:, None] - 1)
        future = j > qblk[:, None]
        dummy = j < first
        a = np.where(forced, 1e30, 0.0)
        a = np.where(future | dummy, -1e30, a)
        sa[:, i, :] = a
        al[:, i, :] = (~(future | dummy)).astype(np.float32)
    c['scoreadd'] = sa
    c['allowed'] = al
    kk = np.arange(128)[:, None]
    qq = np.arange(128)[None, :]
    c['farlow'] = np.tile(np.where(kk > qq, 0.0, NEGM).astype(np.float32), (1, 4)).astype(BF_NP)
    se = np.zeros((33, NQ, 256), np.float32)
    for i in range(NQ):
        I = 2 * i + 1
        for m in range(32):
            cc = 8 * I - 9 + (31 - m)
            if 0 <= cc < 256:
                se[m, i, cc] = 1.0
        lo = 8 * I - 9 + 32
        se[32, i, max(lo, 0):] = 1.0
        se[32, i, 255] = 1.0
        if r == 0:
            se[32, i, 0:8] = 1.0
    c['shiftext'] = se.astype(BF_NP)
    oha = np.zeros((33, 1024), np.float32)
    ohb = np.zeros((33, 1024), np.float32)
    d = np.arange(1024) - 512
    bk = _t5_bucket(d)
    for idx in range(1024):
        if d[idx] < 0:
            oha[32, idx] = 1
            ohb[32, idx] = 1
        else:
            ohb[bk[idx], idx] = 1
            if d[idx] < 128:
                oha[bk[idx], idx] = 1
            else:
                oha[32, idx] = 1
    c['oha'] = oha
    c['ohb'] = ohb
    c['asel'] = np.full((128, 1), float(r), np.float32)
    ov = np.zeros((256, 64), np.float32)
    for j in range(64):
        for m in range(4):
            for n in range(2):
                ci = 4 * j + m - n
                if 0 <= ci < 256:
                    ov[ci, j] += 1
    c['overlap'] = np.ascontiguousarray(ov.reshape(2, 128, 64).transpose(1, 0, 2))
    return c


_NC_CACHE = {}


def run(inputs, debug=False):
    f = lambda a: np.ascontiguousarray(np.asarray(a, dtype=np.float32))
    x = f(inputs['x'])
    w_in = f(inputs['w_in'])[0]
    cs = lambda a, b: w_in[:, a:b]
    shared = {
        'w_kf': np.ascontiguousarray(np.concatenate([cs(O_KA, O_KA + 128), cs(O_KSL, O_KSL + 128), cs(O_KW, O_KW + 128),
                                                     cs(O_KC, O_KC + 128), cs(O_VC, O_VC + 128)], axis=1)),
        'w_v': np.ascontiguousarray(np.concatenate([cs(O_VA, O_VA + 128), cs(O_VSL, O_VSL + 128), cs(O_VW, O_VW + 128)], axis=1)),
        'w_q': np.ascontiguousarray(np.concatenate([cs(O_QA, O_QA + 512), cs(O_QB, O_QB + 512)], axis=1)),
        'w_g': np.ascontiguousarray(np.concatenate([cs(O_GA, O_GA + 1024), cs(O_GB, O_GB + 1024), cs(O_GN, O_GN + 24)], axis=1)),
        'gam': np.ascontiguousarray(np.stack([f(inputs['norm_mix'])[0], f(inputs['norm_ffn'])[0], f(inputs['norm_final'])])),
        'sinks': f(inputs['attn_sinks']),
        'table': f(inputs['rel_bias_table']),
        'posT': np.ascontiguousarray(np.stack([f(inputs['cmp_pos_k'])[0].T, f(inputs['cmp_pos_v'])[0].T])),
        'w1': np.ascontiguousarray(np.stack([f(inputs['cmp_w1_k'])[0], f(inputs['cmp_w1_v'])[0]])),
        'w2': np.ascontiguousarray(np.stack([f(inputs['cmp_w2_k'])[0], f(inputs['cmp_w2_v'])[0]])),
        'w_upa': f(inputs['w_up_a'])[0], 'w_upb': f(inputs['w_up_b'])[0], 'w_out': f(inputs['w_out'])[0],
        'w_fi': f(inputs['w_ffn_in'])[0], 'w_fo': f(inputs['w_ffn_out'])[0],
    }
    cw = f(inputs['conv_w'])[0]
    cb = f(inputs['conv_b'])
    cwb = np.concatenate([cw, cb], axis=0).reshape(4, 44, 128).transpose(2, 0, 1)
    shared['cwb'] = np.ascontiguousarray(cwb)
    consts = [_host_consts(0), _host_consts(1)]
    in_maps = []
    for c in range(8):
        b, r = c // 2, c % 2
        if r == 1:
            xkk = x[b]
        else:
            xkk = np.concatenate([np.zeros((128, 1024), np.float32), x[b][:3968]], axis=0)
        m = dict(shared)
        m.update(consts[r])
        m['xk'] = np.ascontiguousarray(xkk)
        in_maps.append(m)
    key = bool(debug)
    if key not in _NC_CACHE:
        _NC_CACHE[key] = build(debug)
    nc = _NC_CACHE[key]
    res = run_bass_kernel_spmd(nc, in_maps, core_ids=list(range(8)))
    outp = np.zeros((4, 4096, 1024), np.float32)
    for c in range(8):
        b, r = c // 2, c % 2
        o = np.asarray(res.results[c]['out']).reshape(NQ, 128, 1024)
        outp[b].reshape(16, 2, 128, 1024)[:, r] = o
    if debug:
        return outp, res
    return outp


def kernel(**inputs):
    return run(inputs)
```

```python
import math
from contextlib import ExitStack
import numpy as np
import ml_dtypes
BF_NP = ml_dtypes.bfloat16
import concourse.bass as bass
import concourse.mybir as mybir
from concourse.bass_utils import run_bass_kernel_spmd

F32 = mybir.dt.float32
BF = mybir.dt.bfloat16
AF = mybir.ActivationFunctionType
ALU = mybir.AluOpType
AX = mybir.AxisListType
NEGM = -30000.0
NQ = 16


class Buf:
    def __init__(self, name):
        self.name = name
        self.w = None
        self.r = []
        self.sem = None
        self.cnt = 0


class Sched:
    ENG = ['pe', 'act', 'dve', 'pool', 'sp']

    def __init__(self, nc, stack):
        self.nc = nc
        self.ops = []
        self.start = 0
        self.stack = stack
        self.esem = {e: stack.enter_context(nc.semaphore('sem_' + e)) for e in self.ENG}
        self.ecnt = {e: 0 for e in self.ENG}
        self.bar = stack.enter_context(nc.semaphore('sem_bar'))
        self.nphase = 0

    def add(self, eng, fn, reads=(), writes=(), dma=None, bg=False, nowaw=False, own_sem=False):
        i = len(self.ops)
        deps = set()
        for b in reads:
            if b.w is not None:
                deps.add(b.w)
        for b in writes:
            if b.w is not None and not nowaw:
                deps.add(b.w)
            deps.update(b.r)
        for b in reads:
            b.r.append(i)
        for b in writes:
            b.w = i
            b.r = []
        self.ops.append(dict(eng=eng, fn=fn, deps=deps, dma=dma, bg=bg, own_sem=own_sem))
        return i

    def emit(self):
        nc = self.nc
        ops = self.ops
        s0 = self.start
        for o in ops[s0:]:
            o['deps'] = {d for d in o['deps'] if d >= s0 or ops[d]['bg']}
            if o['eng'] == 'pe' and o['dma'] is None:
                o['deps'] = {d for d in o['deps'] if not (ops[d]['eng'] == 'pe' and ops[d]['dma'] is None)}
        need = [False] * len(ops)
        for o in ops[s0:]:
            for d in o['deps']:
                need[d] = True
        self.nphase += 1
        mine_last = {}
        for e in self.ENG:
            idxs = [i for i in range(s0, len(ops)) if ops[i]['eng'] == e and ops[i]['fn'] is not None and ops[i]['dma'] is None]
            mine_last[e] = idxs[-1] if idxs else None
            if idxs:
                need[idxs[-1]] = True
        alld = []
        for i in range(s0, len(ops)):
            o = ops[i]
            if o['dma'] is not None:
                b = o['dma']
                if b.sem is None:
                    b.sem = self.stack.enter_context(nc.semaphore('dsem_' + b.name))
                b.cnt += 16
                o['sig'] = (b.sem, b.cnt)
                if not o['bg']:
                    alld.append(i)
            elif o['own_sem']:
                o['sig'] = (self.stack.enter_context(nc.semaphore('osem_%d' % i)), 1)
            elif need[i] and o['fn'] is not None:
                self.ecnt[o['eng']] += 1
                o['sig'] = (self.esem[o['eng']], self.ecnt[o['eng']])
            else:
                o['sig'] = None
        with nc.Block() as block:
            reg = dict(pe=block.tensor, act=block.scalar, dve=block.vector, pool=block.gpsimd, sp=block.sync)
            for e in self.ENG:
                mine = [o for o in ops[s0:] if o['eng'] == e]

                def body(eh, mine=mine, e=e):
                    seen = {}

                    def wait_for(d):
                        if ops[d]['sig'] is None:
                            return
                        sem, val = ops[d]['sig']
                        k = id(sem)
                        if seen.get(k, 0) >= val:
                            return
                        eh.wait_ge(sem, val)
                        seen[k] = val
                    for o in mine:
                        for d in sorted(o['deps']):
                            wait_for(d)
                        if o['fn'] is None:
                            continue
                        ins = o['fn'](eh)
                        if o['sig'] is not None:
                            sem, val = o['sig']
                            ins.then_inc(sem, 16 if o['dma'] is not None else 1)
                    if e == 'sp':
                        last = {}
                        for d in alld:
                            sem, val = ops[d]['sig']
                            if id(sem) not in last or ops[last[id(sem)]]['sig'][1] < val:
                                last[id(sem)] = d
                        for d in sorted(last.values()):
                            wait_for(d)
                    if mine_last[e] is not None:
                        wait_for(mine_last[e])
                    eh.sem_inc(self.bar, 1)
                    eh.wait_ge(self.bar, 5 * self.nphase)
                reg[e](body)
        self.start = len(ops)


def bc_last(ap, n):
    return bass.AP(ap.tensor, ap.offset, [list(a) for a in ap.ap] + [[0, n]])


class Rot:
    def __init__(self, items):
        self.items = items
        self.i = 0

    def next(self):
        it = self.items[self.i % len(self.items)]
        self.i += 1
        return it


O_QA, O_KA, O_VA, O_QB, O_KC, O_VC, O_KSL, O_VSL, O_KW, O_VW, O_GN, O_GA, O_GB = (
    0, 512, 640, 768, 1280, 1408, 1536, 1664, 1792, 1920, 2048, 2072, 3096)


def build(debug=False):
    nc = bass.Bass("TRN2", target_bir_lowering=False)

    def di(n, s, dt=F32):
        return nc.dram_tensor(n, list(s), dt, kind="ExternalInput").ap()
    xk = di("xk", [4096, 1024])
    w_kf = di("w_kf", [1024, 640])
    w_v = di("w_v", [1024, 384])
    w_q = di("w_q", [1024, 1024])
    w_g = di("w_g", [1024, 2072])
    gam = di("gam", [3, 1024])
    sinks = di("sinks", [1, 8])
    table = di("table", [32, 16])
    posT = di("posT", [2, 64, 32])
    w1 = di("w1", [2, 2048, 256])
    w2 = di("w2", [2, 256, 64])
    w_upa = di("w_upa", [512, 1024])
    w_upb = di("w_upb", [512, 1024])
    w_out = di("w_out", [1024, 1024])
    w_fi = di("w_fi", [1024, 5632])
    cwb = di("cwb", [128, 4, 44])
    w_fo = di("w_fo", [2816, 1024])
    keymask = di("keymask", [128, 32])
    cmask = di("cmask", [128, 2])
    emat = di("emat", [64, 4096], BF)
    scoreadd = di("scoreadd", [128, NQ, 64])
    allowed = di("allowed", [128, NQ, 64])
    farlow = di("farlow", [128, 512], BF)
    shiftext = di("shiftext", [33, NQ, 256], BF)
    oha = di("oha", [33, 1024])
    ohb = di("ohb", [33, 1024])
    asel = di("asel", [128, 1])
    overlap = di("overlap", [128, 2, 64])
    out = nc.dram_tensor("out", [2048, 1024], F32, kind="ExternalOutput").ap()
    dbg = {}
    if debug:
        dbg['ya'] = nc.dram_tensor("dbg_ya", [2048, 512], F32, kind="ExternalOutput").ap()
        dbg['yb'] = nc.dram_tensor("dbg_yb", [2048, 512], F32, kind="ExternalOutput").ap()
        dbg['x1'] = nc.dram_tensor("dbg_x1", [2048, 1024], F32, kind="ExternalOutput").ap()
    biasd = nc.dram_tensor("biasd", [16, 1024], BF, kind="Internal").ap()
    x1d = nc.dram_tensor("x1d", [2048, 1024], F32, kind="Internal").ap()
    h2Td = nc.dram_tensor("h2Td", [128, 8 * 2048], BF, kind="Internal").ap()
    yabd = nc.dram_tensor("yabd", [2048, 1024], BF, kind="Internal").ap()
    cin = nc.dram_tensor("cin", [128, 256], BF, kind="Internal")
    w1b = nc.dram_tensor("w1b", [2, 64, 32, 256], BF, kind="Internal").ap()
    wqb = nc.dram_tensor("wqb", [1024, 1024], BF, kind="Internal").ap()
    wupab = nc.dram_tensor("wupab", [512, 1024], BF, kind="Internal").ap()
    wupbb = nc.dram_tensor("wupbb", [512, 1024], BF, kind="Internal").ap()
    wob = nc.dram_tensor("wob", [1024, 1024], BF, kind="Internal").ap()
    wfib = nc.dram_tensor("wfib", [1024, 5632], BF, kind="Internal").ap()
    wfob = nc.dram_tensor("wfob", [2816, 1024], BF, kind="Internal").ap()
    cout = nc.dram_tensor("cout", [256, 256], BF, kind="Internal")

    with ExitStack() as st:
        S = Sched(nc, st)

        def sbuf(stk, n, s, d):
            return stk.enter_context(nc.sbuf_tensor(n, list(s), d))
        ps = [st.enter_context(nc.psum_tensor("ps%d" % i, [128, 512], F32)) for i in range(8)]
        Bps = [Buf("ps%d" % i) for i in range(8)]

        ident = sbuf(st, "ident", [128, 128], BF)
        gam_mix = sbuf(st, "gam_mix", [128, 1024], F32)
        junk_t = sbuf(st, "junk_t", [128, 1024], BF)
        Bjunk = Buf("junk")
        arena0 = sbuf(st, "arena0", [128, 16384], BF)
        wg_t = arena0[:, :].rearrange("p (k n) -> p k n", k=8)
        Bwg = Buf("wg")
        Bconst = Buf("const")
        Bgam = Buf("gam")

        def norm_h_g(xt, Bx, gam_ap, tmp, lnexp=False):
            ssq, rstd, h, Bt = tmp
            if lnexp:
                S.add('act', lambda e: e.activation(out=junk_t[:], in_=xt, func=AF.Square, accum_out=ssq[:]),
                      reads=[Bx], writes=[Bt, Bjunk])
                yield
                S.add('dve', lambda e: e.tensor_scalar(out=rstd[:], in0=ssq[:], scalar1=1.0 / 1024, scalar2=1e-6,
                                                       op0=ALU.mult, op1=ALU.add), reads=[Bt], writes=[Bt])
                yield
                S.add('act', lambda e: e.activation(out=rstd[:], in_=rstd[:], func=AF.Ln), reads=[Bt], writes=[Bt])
                S.add('act', lambda e: e.activation(out=rstd[:], in_=rstd[:], func=AF.Exp, scale=-0.5), reads=[Bt], writes=[Bt])
                yield
                S.add('dve', lambda e: e.scalar_tensor_tensor(out=h[:], in0=xt, scalar=rstd[:, 0:1], in1=gam_ap,
                                                              op0=ALU.mult, op1=ALU.mult),
                      reads=[Bx, Bt, Bgam], writes=[Bt])
                yield
                return
            S.add('act', lambda e: e.activation(out=junk_t[:], in_=xt, func=AF.Square, accum_out=ssq[:]),
                  reads=[Bx], writes=[Bt, Bjunk])
            yield
            S.add('dve', lambda e: e.tensor_scalar(out=rstd[:], in0=ssq[:], scalar1=1.0 / 1024, scalar2=1e-6,
                                                   op0=ALU.mult, op1=ALU.add), reads=[Bt], writes=[Bt])
            yield
            S.add('act', lambda e: e.activation(out=rstd[:], in_=rstd[:], func=AF.Sqrt), reads=[Bt], writes=[Bt])
            yield
            S.add('dve', lambda e: e.reciprocal(out=rstd[:], in_=rstd[:]), reads=[Bt], writes=[Bt])
            yield
            S.add('dve', lambda e: e.scalar_tensor_tensor(out=h[:], in0=xt, scalar=rstd[:, 0:1], in1=gam_ap,
                                                          op0=ALU.mult, op1=ALU.mult),
                  reads=[Bx, Bt, Bgam], writes=[Bt])
            yield

        def trans_g(h, Bt, hT_out, BhT, bank, eng='act'):
            pT = ps[bank][:].bitcast(BF)
            for k in range(8):
                S.add('pe', lambda e, k=k: e.transpose(out=pT[:, k * 128:(k + 1) * 128], in_=h[:, k * 128:(k + 1) * 128],
                                                       identity=ident[:]), reads=[Bt, Bconst], writes=[Bps[bank]])
                if k % 4 == 3:
                    yield
            if eng == 'act':
                S.add('act', lambda e: e.copy(out=hT_out, in_=pT[:, 0:1024].rearrange("p (k t) -> p k t", k=8)),
                      reads=[Bps[bank]], writes=[BhT])
            else:
                S.add('dve', lambda e: e.tensor_copy(out=hT_out, in_=pT[:, 0:1024].rearrange("p (k t) -> p k t", k=8)),
                      reads=[Bps[bank]], writes=[BhT])
            yield

        def norm_T_g(xt, Bx, gam_ap, hT_out, BhT, tmp, bank):
            yield from norm_h_g(xt, Bx, gam_ap, tmp)
            yield from trans_g(tmp[2], tmp[3], hT_out, BhT, bank)

        def norm_T(xt, Bx, gam_ap, hT_out, BhT, tmp, bank):
            for _ in norm_T_g(xt, Bx, gam_ap, hT_out, BhT, tmp, bank):
                pass

        def zipgens_dyn(lst):
            while lst:
                for g in list(lst):
                    try:
                        next(g)
                    except StopIteration:
                        lst.remove(g)

        def zipgens(gens):
            gens = [g for g in gens if g is not None]
            while gens:
                alive = []
                for g in gens:
                    try:
                        next(g)
                        alive.append(g)
                    except StopIteration:
                        pass
                gens = alive

        cast_rr = [0]

        def load_cast(dst, src, stage_rot, Bdst, engs=('dve', 'act')):
            stg_t, Bst = stage_rot.next()
            n = 1
            for d_ in dst.shape[1:]:
                n *= d_
            sv = stg_t[0:dst.shape[0], 0:n]
            if len(dst.shape) == 3:
                sv = sv.rearrange("p (a b) -> p a b", a=dst.shape[1])
            S.add('sp', lambda e: e.dma_start(out=sv, in_=src), writes=[Bst], dma=Bst)
            eng = engs[cast_rr[0] % len(engs)]
            cast_rr[0] += 1
            if eng == 'act':
                S.add('act', lambda e: e.copy(out=dst, in_=sv), reads=[Bst], writes=[Bdst], nowaw=True)
            else:
                S.add(eng, lambda e: e.tensor_copy(out=dst, in_=sv), reads=[Bst], writes=[Bdst], nowaw=True)

        def mk_tmp(stk, n):
            return (sbuf(stk, "ssq" + n, [128, 1], F32),
                    sbuf(stk, "rstd" + n, [128, 1], F32), sbuf(stk, "hbf" + n, [128, 1024], BF), Buf("nt" + n))

        att = ExitStack()
        KA = sbuf(att, "KA", [128, 2, 4096], BF)
        KW = sbuf(att, "KW", [128, 2, 4096], BF)
        KS = sbuf(att, "KS", [128, 2, 4096], BF)
        Vx = sbuf(att, "Vx", [128, 32, 3, 2, 65], BF)
        KCT = sbuf(att, "KCT", [128, 2, 256], BF)
        VCx = sbuf(att, "VCx", [128, 2, 2, 129], BF)
        bias_t = sbuf(att, "bias_t", [128, 8, 512], BF)
        bandx = sbuf(att, "bandx", [128, 2, 512], BF)
        farlow_t = sbuf(att, "farlow_t", [128, 512], BF)
        keymask_t = sbuf(att, "keymask_t", [128, 32], F32)
        cmask_t = sbuf(att, "cmask_t", [128, 2], F32)
        sinkexp = sbuf(att, "sinkexp", [128, 8], F32)
        BKA, BKW, BKS, BVx, BKCT, BVCx, Bbias = [Buf(n) for n in "KA KW KS Vx KCT VCx bias".split()]

        def bias_chain(stk):
            tabx = sbuf(stk, "tabx", [33, 16], F32)
            tab31 = sbuf(stk, "tab31", [32, 16], F32)
            oh_t = sbuf(stk, "oh_t", [33, 1024], F32)
            vec_t = sbuf(stk, "vec_t", [16, 1024], BF)
            Jf = sbuf(stk, "Jf", [128, 128], F32)
            Jb = sbuf(stk, "Jb", [128, 128], BF)
            Hxs = Rot([(sbuf(stk, "Hx%d" % i, [128, 512], BF), Buf("Hx%d" % i)) for i in range(4)])
            Bp0, Bvec, Boh, Btab, Bt31, BJ = [Buf(n) for n in "bc_p0 bc_vec bc_oh bc_tab bc_t31 bc_J".split()]
            S.add('sp', lambda e: e.dma_start(out=tabx[0:32, :], in_=table), writes=[Btab], dma=Btab)
            S.add('sp', lambda e: e.dma_start(out=tab31[:], in_=table[31:32, :].rearrange("a n -> (a n)").partition_broadcast(32)),
                  writes=[Bt31], dma=Bt31)
            S.add('pool', lambda e: e.memset(tabx[32:33, :], NEGM), writes=[Btab])
            S.add('dve', lambda e: e.tensor_sub(out=tabx[0:32, :], in0=tabx[0:32, :], in1=tab31[:]), reads=[Btab, Bt31], writes=[Btab])
            yield
            for m in range(2):
                S.add('sp', lambda e, m=m: e.dma_start(out=oh_t[:, :], in_=(oha if m == 0 else ohb)), writes=[Boh], dma=Boh)
                for hf in range(2):
                    S.add('pe', lambda e, m=m, hf=hf: e.matmul(ps[0][0:8, :], lhsT=tabx[:, m * 8:(m + 1) * 8],
                                                               rhs=oh_t[:, hf * 512:(hf + 1) * 512], start=True, stop=True),
                          reads=[Btab, Boh], writes=[Bps[0]])
                    S.add('act', lambda e, m=m, hf=hf: e.copy(out=vec_t[0:8, hf * 512:(hf + 1) * 512], in_=ps[0][0:8, :]),
                          reads=[Bps[0]], writes=[Bvec])
                    S.add('sp', lambda e, m=m, hf=hf: e.dma_start(out=biasd[m * 8:(m + 1) * 8, hf * 512:(hf + 1) * 512],
                                                                   in_=vec_t[0:8, hf * 512:(hf + 1) * 512]),
                          reads=[Bvec], writes=[Bp0], dma=Bvec)
                    yield
            bt = biasd.tensor
            S.add('pool', lambda e: e.memset(Jf[:], 0.0), writes=[BJ])
            S.add('pool', lambda e: e.affine_select(out=Jf[:], in_=Jf[:], pattern=[[1, 128]], compare_op=ALU.not_equal,
                                                    fill=1.0, base=-127, channel_multiplier=1), reads=[BJ], writes=[BJ])
            S.add('dve', lambda e: e.tensor_copy(out=Jb[:], in_=Jf[:]), reads=[BJ], writes=[BJ])
            yield
            for m in range(2):
                for kind in range(2):
                    for g in range(2):
                        idx = m * 4 + kind * 2 + g
                        Hx, BHx = Hxs.next()
                        src = bass.AP(bt, (m * 8 + 4 * g) * 1024 + 512 + 128 * kind - 127, [[1, 128], [1024, 4], [1, 128]])
                        S.add('sp', lambda e, Hx=Hx, src=src: e.dma_start(
                            out=Hx[:, :].rearrange("p (h q) -> p h q", h=4), in_=src), reads=[Bp0], writes=[BHx], dma=BHx)
                        S.add('pe', lambda e, Hx=Hx: e.matmul(ps[0][:, :], lhsT=Jb[:], rhs=Hx[:, :], start=True, stop=True),
                              reads=[BJ, BHx], writes=[Bps[0]])
                        S.add('act', lambda e, idx=idx: e.activation(out=bias_t[:, idx, :], in_=ps[0][:, :], func=AF.Exp), reads=[Bps[0]], writes=[Bbias])
                        yield
            for g in range(2):
                src = bass.AP(bt, (8 + 4 * g) * 1024 + 512 - 383, [[16, 32], [1024, 4], [1, 128]])
                S.add('sp', lambda e, g=g, src=src: e.dma_start(out=bandx[0:32, g, :].rearrange("p (h q) -> p h q", h=4), in_=src),
                      reads=[Bp0], writes=[Bbias], dma=Bbias)
            yield

        Bpre = {n: Buf('pre_' + n) for n in ('w1', 'wq', 'wup', 'wo', 'wfi', 'wfo')}
        kvs = ExitStack()
        KCr = sbuf(kvs, "KCr", [64, 2, 2, 16, 260], BF)
        BKCr = Buf("KCr")
        p1w = ExitStack()
        wkf_t = sbuf(p1w, "wkf_t", [128, 8, 640], BF)
        wv_t = sbuf(p1w, "wv_t", [128, 8, 384], BF)
        Bw1p = [Buf("w1p%d" % k) for k in range(2)]
        xts1 = Rot([(sbuf(p1w, "xt%d" % i, [128, 1024], F32), Buf("xt%d" % i)) for i in range(2)])
        hTs = Rot([(sbuf(p1w, "hTG%d" % i, [128, 8, 512], BF), Buf("hTG%d" % i)) for i in range(2)])
        stg0 = Rot([(xts1.items[0][0][:, :], xts1.items[0][1]), (xts1.items[1][0][:, :], xts1.items[1][1])] +
                   [(hTs.items[i][0][:, 4 * j:4 * j + 4, :].rearrange("p a b -> p (a b)").bitcast(F32), hTs.items[i][1]) for i in range(2) for j in range(2)])
        with ExitStack() as p0:
            identf = sbuf(p1w, "identf", [128, 128], F32)
            sk = sbuf(p1w, "sk", [128, 8], F32)
            t31 = sbuf(p1w, "t31", [128, 8], F32)
            Bp0 = Buf("p0")
            Bd = [Buf("c%d" % i) for i in range(12)]
            for k in range(8):
                load_cast(wkf_t[:, k, :], w_kf[k * 128:(k + 1) * 128, :], stg0, Bw1p[k // 4], engs=(('dve',) if k < 4 else ('act',)))
                load_cast(wv_t[:, k, :], w_v[k * 128:(k + 1) * 128, :], stg0, Bw1p[k // 4], engs=(('dve',) if k < 4 else ('act',)))
            S.add('pool', lambda e: e.memset(identf[:], 0.0), writes=[Bconst])
            S.add('pool', lambda e: e.affine_select(out=identf[:], in_=identf[:], pattern=[[-1, 128]], compare_op=ALU.not_equal,
                                                    fill=1.0, base=0, channel_multiplier=1), reads=[Bconst], writes=[Bconst])
            S.add('dve', lambda e: e.tensor_copy(out=ident[:], in_=identf[:]), reads=[Bconst], writes=[Bconst])
            S.add('sp', lambda e: e.dma_start(out=gam_mix[:], in_=gam[0:1, :].rearrange("a n -> (a n)").partition_broadcast(128)),
                  writes=[Bgam], dma=Bgam)
            S.add('sp', lambda e: e.dma_start(out=keymask_t[:], in_=keymask), writes=[Bd[0]], dma=Bd[0])
            S.add('sp', lambda e: e.dma_start(out=cmask_t[:], in_=cmask), writes=[Bd[1]], dma=Bd[1])
            S.add('sp', lambda e: e.dma_start(out=farlow_t[:], in_=farlow), writes=[Bd[4]], dma=Bd[4])
            S.add('act', lambda e: e.activation(out=farlow_t[:], in_=farlow_t[:], func=AF.Exp), reads=[Bd[4]], writes=[Bconst])
            for g in range(2):
                S.add('sp', lambda e, g=g: e.dma_start(out=KS[64:128, g, :], in_=emat), writes=[Bd[6 + g]], dma=Bd[6 + g])
            S.add('sp', lambda e: e.dma_start(out=sk[:], in_=sinks.rearrange("a n -> (a n)").partition_broadcast(128)),
                  writes=[Bd[11]], dma=Bd[11])
            S.add('sp', lambda e: e.dma_start(out=t31[:], in_=table[31:32, 0:8].rearrange("a n -> (a n)").partition_broadcast(128)),
                  writes=[Bd[11]], dma=Bd[11])
            S.add('dve', lambda e: e.tensor_sub(out=sk[:], in0=sk[:], in1=t31[:]), reads=[Bd[11]], writes=[Bp0])
            S.add('act', lambda e: e.activation(out=sinkexp[:], in_=sk[:], func=AF.Exp), reads=[Bp0], writes=[Bconst])
            S.add('pool', lambda e: e.memset(bandx[32:64, :, :], 0.0), writes=[Bconst])
            S.add('pool', lambda e: e.memset(bandx[64:128, :, :], 0.0), writes=[Bconst])
            S.add('pool', lambda e: e.memset(bandx[32:33, :, :], NEGM), writes=[Bconst])
            S.add('pool', lambda e: e.memset(Vx[:, :, :, :, 64:65], 1.0), writes=[BVx])
            S.add('pool', lambda e: e.memset(VCx[:, :, :, 64:65], 1.0), writes=[BVCx])
            for g in range(2):
                S.add('pool', lambda e, g=g: e.dma_start(out=VCx[:, :, g, 65:129], in_=overlap), writes=[BVCx], dma=BVCx)

        with ExitStack() as p1:
            Bw = Bw1p
            for k in range(8):
                S.add('pool', lambda e, k=k: e.dma_start(out=wg_t[:, k, :], in_=w_g[k * 128:(k + 1) * 128, 0:2048]), writes=[Bwg], dma=Bwg)
            S.add('pool', lambda e: e.memset(KCr[:, :, :, :, 256:260], 0.0), writes=[BKCr])
            Bpad = Buf("kpad")
            S.add('pool', lambda e: e.memset(KA[64:128, :, :], 0.0), writes=[Bpad])
            S.add('pool', lambda e: e.memset(KW[64:128, :, :], 0.0), writes=[Bpad])
            S.add('pool', lambda e: e.memset(KCT[64:128, :, :], 0.0), writes=[Bpad])
            bgq = []
            for kd in range(2):
                for hf in range(2):
                    bgq.append((lambda e, kd=kd, hf=hf: e.dma_start(out=w1b[kd, :, hf * 16:(hf + 1) * 16, :], in_=w1[kd, hf * 1024:(hf + 1) * 1024, :].rearrange("(l d) n -> d l n", d=64)), 'w1'))
            for hf in range(2):
                bgq.append((lambda e, hf=hf: e.dma_start(out=wqb[hf * 512:(hf + 1) * 512, :], in_=w_q[hf * 512:(hf + 1) * 512, :]), 'wq'))

            def issue_bg(n):
                for _ in range(n):
                    if bgq:
                        fn, nm = bgq.pop(0)
                        S.add('pool', fn, writes=[Bpre[nm]], dma=Bpre[nm], bg=True, nowaw=True)
            xts = xts1
            tmps = Rot([mk_tmp(p1, "a%d" % i) for i in range(4)])
            fb = Rot([1, 2])
            tb = Rot([0, 4])
            vb = Rot([3, 5])
            normed = {}

            def stageN(G):
                lst = []
                for t in range(4):
                    p = G * 4 + t
                    xt, Bx = xts.next()
                    S.add('sp', lambda e, xt=xt, p=p: e.dma_start(out=xt[:], in_=xk[p * 128:(p + 1) * 128, :]), writes=[Bx], dma=Bx)
                    tmp = tmps.next()
                    yield from norm_h_g(xt[:], Bx, gam_mix[:], tmp)
                    lst.append(tmp)
                normed[G] = lst

            def stageP(G):
                issue_bg(1)
                hTG, BhTG = hTs.next()
                lst = normed.pop(G)
                for t in range(4):
                    p = G * 4 + t
                    tmp = lst[t]
                    yield from trans_g(tmp[2], tmp[3], hTG[:, :, t * 128:(t + 1) * 128], BhTG, tb.next(), eng=('act' if t % 2 == 0 else 'dve'))
                for t in range(4):
                    p = G * 4 + t
                    b3 = vb.next()
                    for k in range(8):
                        S.add('pe', lambda e, k=k, t=t, b3=b3: e.matmul(ps[b3][:, 0:384], lhsT=hTG[:, k, t * 128:(t + 1) * 128],
                                                                        rhs=wv_t[:, k, :], start=(k == 0), stop=(k == 7)),
                              reads=[BhTG, Bw[k // 4]], writes=[Bps[b3]])
                        if k % 4 == 3:
                            yield
                    S.add('act', lambda e, p=p, b3=b3: e.copy(out=Vx[:, p, :, :, 0:64],
                                                              in_=ps[b3][:, 0:384].rearrange("p (a g d) -> p a g d", a=3, g=2)),
                          reads=[Bps[b3]], writes=[BVx])
                    yield
                for kind in range(5):
                    for g in range(2):
                        b = fb.next()
                        c0 = kind * 128 + g * 64
                        for k in range(8):
                            S.add('pe', lambda e, k=k, b=b, c0=c0: e.matmul(ps[b][0:64, :], lhsT=wkf_t[:, k, c0:c0 + 64],
                                                                            rhs=hTG[:, k, :], start=(k == 0), stop=(k == 7)),
                                  reads=[BhTG, Bw[k // 4]], writes=[Bps[b]])
                            if k % 4 == 3:
                                yield
                        src = ps[b][0:64, :]
                        if kind == 0:
                            dst, Bdst = KA[0:64, g, G * 512:(G + 1) * 512], BKA
                        elif kind == 1:
                            dst, Bdst = KS[0:64, g, G * 512:(G + 1) * 512], BKS
                        elif kind == 2:
                            dst, Bdst = KW[0:64, g, G * 512:(G + 1) * 512], BKW
                        else:
                            dst, Bdst = KCr[:, kind - 3, g, :, G * 32:(G + 1) * 32].rearrange("p r n -> p n r"), BKCr
                            src = ps[b][0:64, :].rearrange("p (n r) -> p n r", r=16)
                        if (kind + g) % 2 == 0:
                            S.add('act', lambda e, src=src, dst=dst: e.copy(out=dst, in_=src), reads=[Bps[b]], writes=[Bdst])
                        else:
                            S.add('dve', lambda e, src=src, dst=dst: e.tensor_copy(out=dst, in_=src), reads=[Bps[b]], writes=[Bdst])
                        yield

            zipgens([stageN(0)])
            for G in range(8):
                zipgens([stageP(G), stageN(G + 1) if G + 1 < 8 else None])
            S.emit()
        p1w.close()

        with ExitStack() as pc:
            w1_t = sbuf(pc, "w1_t", [64, 2, 32, 256], BF)
            w2_t = sbuf(pc, "w2_t", [128, 2, 2, 64], BF)
            pos_t = sbuf(pc, "pos_t", [64, 2, 32], BF)
            hb = sbuf(pc, "hb", [128, 4], F32)
            Bw = Buf("wc")
            Bhb = Buf("hb")
            Bw1c = [[Buf("w1c%d_%d" % (kd, l4)) for l4 in range(2)] for kd in range(2)]
            for kd in range(2):
                for l4 in range(8):
                    S.add('sp', lambda e, kd=kd, l4=l4: e.dma_start(
                        out=w1_t[:, kd, l4 * 4:(l4 + 1) * 4, :], in_=w1b[kd, :, l4 * 4:(l4 + 1) * 4, :]),
                        reads=[Bpre['w1']], writes=[Bw1c[kd][l4 // 4]], dma=Bw1c[kd][l4 // 4], nowaw=True)
                S.add('pool', lambda e, kd=kd: e.dma_start(out=w2_t[:, kd, :, :], in_=w2[kd].rearrange("(c p) n -> p c n", p=128)),
                      writes=[Bw], dma=Bw)
                S.add('pool', lambda e, kd=kd: e.dma_start(out=pos_t[:, kd, :], in_=posT[kd]), writes=[Bw], dma=Bw)
            def compress_g():
                for kd in range(2):
                    for hc in range(2):
                        col = kd * 2 + hc
                        for l in range(32):
                            S.add('pe', lambda e, kd=kd, hc=hc, l=l, col=col: e.matmul(
                                ps[4][:, col:col + 1], lhsT=w1_t[:, kd, l, hc * 128:(hc + 1) * 128], rhs=pos_t[:, kd, l:l + 1],
                                start=(l == 0), stop=(l == 31), skip_group_check=True), reads=[Bw, Bw1c[kd][l // 16]], writes=[Bps[4]])
                S.add('dve', lambda e: e.tensor_copy(out=hb[:], in_=ps[4][:, 0:4]), reads=[Bps[4]], writes=[Bhb])
                yield
                gt = Rot([(sbuf(pc, "gx%d" % i, [128, 256], F32), sbuf(pc, "gu%d" % i, [128, 256], F32), Buf("gt%d" % i)) for i in range(2)])
                gel = Rot([(sbuf(pc, "gel%d" % i, [128, 2, 256], BF), Buf("gel%d" % i)) for i in range(2)])
                hbk = Rot([1, 2])
                for kd in range(2):
                    for g in range(2):
                        ge, Bge = gel.next()
                        for hc in range(2):
                            b = hbk.next()
                            col = kd * 2 + hc
                            for l in range(32):
                                rhs = KCr[:, kd, g, l % 16, (l // 16):(l // 16) + 256]
                                S.add('pe', lambda e, kd=kd, hc=hc, l=l, b=b, rhs=rhs: e.matmul(
                                    ps[b][:, 0:256], lhsT=w1_t[:, kd, l, hc * 128:(hc + 1) * 128], rhs=rhs,
                                    start=(l == 0), stop=(l == 31)), reads=[BKCr, Bw1c[kd][l // 16]], writes=[Bps[b]])
                                if l % 8 == 7:
                                    yield
                            gx, gu, Bg = gt.next()
                            S.add('dve', lambda e, b=b, gx=gx, col=col: e.tensor_scalar(out=gx[:], in0=ps[b][:, 0:256], scalar1=hb[:, col:col + 1],
                                                                                        scalar2=None, op0=ALU.add), reads=[Bps[b], Bhb], writes=[Bg])
                            S.add('act', lambda e, gx=gx, gu=gu: e.activation(out=gu[:], in_=gx[:], func=AF.Square), reads=[Bg], writes=[Bg])
                            S.add('dve', lambda e, gu=gu: e.tensor_scalar(out=gu[:], in0=gu[:], scalar1=0.044715, scalar2=1.0,
                                                                          op0=ALU.mult, op1=ALU.add), reads=[Bg], writes=[Bg])
                            S.add('dve', lambda e, gx=gx, gu=gu: e.tensor_mul(out=gu[:], in0=gu[:], in1=gx[:]), reads=[Bg], writes=[Bg])
                            S.add('act', lambda e, gu=gu: e.activation(out=gu[:], in_=gu[:], func=AF.Sigmoid, scale=1.5957691216057308),
                                  reads=[Bg], writes=[Bg])
                            S.add('dve', lambda e, gx=gx, gu=gu, ge=ge, hc=hc: e.tensor_mul(out=ge[:, hc, :], in0=gu[:], in1=gx[:]),
                                  reads=[Bg], writes=[Bge])
                            yield
                        if kd == 0:
                            for hc in range(2):
                                S.add('pe', lambda e, hc=hc, ge=ge: e.matmul(ps[5][0:64, 0:256], lhsT=w2_t[:, 0, hc, :], rhs=ge[:, hc, :],
                                                                             start=(hc == 0), stop=(hc == 1)), reads=[Bge, Bw], writes=[Bps[5]])
                            S.add('act', lambda e, g=g: e.copy(out=KCT[0:64, g, :], in_=ps[5][0:64, 0:256]), reads=[Bps[5]], writes=[BKCT])
                        else:
                            for ct in range(2):
                                for hc in range(2):
                                    S.add('pe', lambda e, hc=hc, ct=ct, ge=ge: e.matmul(
                                        ps[6][:, ct * 64:(ct + 1) * 64], lhsT=ge[:, hc, ct * 128:(ct + 1) * 128], rhs=w2_t[:, 1, hc, :],
                                        start=(hc == 0), stop=(hc == 1), skip_group_check=True), reads=[Bge, Bw], writes=[Bps[6]])
                            S.add('act', lambda e, g=g: e.copy(out=VCx[:, :, g, 0:64], in_=ps[6][:, 0:128].rearrange("p (c d) -> p c d", c=2)),
                                  reads=[Bps[6]], writes=[BVCx])

            zipgens([compress_g(), bias_chain(pc)])
            S.emit()
        kvs.close()

        with ExitStack() as pa:
            wq_t = sbuf(pa, "wq_t", [128, 8, 1024], BF)
            xts = Rot([(sbuf(pa, "xq%d" % i, [128, 1024], F32), Buf("xq%d" % i)) for i in range(2)])
            stga = xts
            scoreadd_t = sbuf(pa, "scoreadd_t", [128, NQ, 64], F32)
            allowed_t = sbuf(pa, "allowed_t", [128, NQ, 64], F32)
            Bsa = Buf("scoreadd")
            shift_t = sbuf(pa, "shift_t", [128, NQ, 256], BF)
            S.add('dve', lambda e: e.memset(shift_t[32:64, :, :], 0.0), writes=[Bsa])
            S.add('dve', lambda e: e.memset(shift_t[64:128, :, :], 0.0), writes=[Bsa])
            S.add('sp', lambda e: e.dma_start(out=shift_t[0:33, :, :], in_=shiftext), writes=[Bsa], dma=Bsa)
            S.add('sp', lambda e: e.dma_start(out=scoreadd_t[:], in_=scoreadd), writes=[Bsa], dma=Bsa)
            S.add('sp', lambda e: e.dma_start(out=allowed_t[:], in_=allowed), writes=[Bsa], dma=Bsa)
            wgn_t = sbuf(pa, "wgn_t", [128, 8, 24], BF)
            Bw = Buf("wa")
            Bwq = [Buf("wq%d" % k) for k in range(2)]
            bgq2 = []
            bgq2.append((lambda e: e.dma_start(out=wupab, in_=w_upa), 'wup'))
            bgq2.append((lambda e: e.dma_start(out=wupbb, in_=w_upb), 'wup'))
            for hf in range(2):
                bgq2.append((lambda e, hf=hf: e.dma_start(out=wob[hf * 512:(hf + 1) * 512, :], in_=w_out[hf * 512:(hf + 1) * 512, :]), 'wo'))
            for rb in range(8):
                bgq2.append((lambda e, rb=rb: e.dma_start(out=wfib[rb * 128:(rb + 1) * 128, :].rearrange("r (a b) -> r a b", b=1408),
                                                          in_=w_fi[rb * 128:(rb + 1) * 128, :].rearrange("r (a b) -> r a b", b=1408)), 'wfi'))
            for rb in range(4):
                bgq2.append((lambda e, rb=rb: e.dma_start(out=wfob[rb * 704:(rb + 1) * 704, :], in_=w_fo[rb * 704:(rb + 1) * 704, :]), 'wfo'))

            def issue_bg2(n):
                for _ in range(n):
                    if bgq2:
                        fn, nm = bgq2.pop(0)
                        S.add('pool', fn, writes=[Bpre[nm]], dma=Bpre[nm], bg=True, nowaw=True)
            for k in range(8):
                S.add('sp', lambda e, k=k: e.dma_start(out=wq_t[:, k, :], in_=wqb[k * 128:(k + 1) * 128, :]), reads=[Bpre['wq']], writes=[Bwq[k // 4]], dma=Bwq[k // 4], nowaw=True)
            S.add('pool', lambda e: e.dma_start(out=wgn_t[:], in_=w_g[:, 2048:2072].rearrange("(k p) n -> p k n", p=128)), writes=[Bw], dma=Bw)
            tmps = Rot([mk_tmp(pa, "b%d" % i) for i in range(1)])
            hTq = Rot([(sbuf(pa, "hTq%d" % i, [128, 8, 128], BF), Buf("hTq%d" % i)) for i in range(2)])
            QAs = Rot([(sbuf(pa, "QA%d" % i, [128, 2, 512], BF), Buf("QA%d" % i)) for i in range(2)])
            for i_ in range(2):
                S.add('pool', lambda e, i_=i_: e.memset(QAs.items[i_][0][64:128, :, :], 0.0), writes=[QAs.items[i_][1]])
            QSs = Rot([(sbuf(pa, "QS%d" % i, [128, 2, 512], BF), Buf("QSlo%d" % i), [Buf("QShi%d_%d" % (i, g)) for g in range(2)])
                       for i in range(2)])
            for i_ in range(2):
                S.add('pool', lambda e, i_=i_: e.memset(QSs.items[i_][0][64:128, :, :], 0.0), writes=QSs.items[i_][2])
            gns = Rot([(sbuf(pa, "gn%d" % i, [128, 24], F32), Buf("gn%d" % i)) for i in range(2)])
            negs = Rot([(sbuf(pa, "negs%d" % i, [128, 128], BF), Buf("negs%d" % i)) for i in range(2)])
            for i in range(2):
                S.add('pool', lambda e, i=i: e.memset(negs.items[i][0][:], 0.0), writes=[negs.items[i][1]])
            Pts = Rot([(sbuf(pa, "Pt%d" % i, [128, 512], BF), Buf("Pt%d" % i)) for i in range(4)])
            sbank = Rot([0, 1, 6])
            abank = Rot([2, 3, 4, 5])
            ybf = Rot([(sbuf(pa, "ybf%d" % i, [128, 512], F32), Buf("ybf%d" % i)) for i in range(1)])
            caccs = Rot([(sbuf(pa, "cacc%d" % i, [128, 4, 129], F32), Buf("cacc%d" % i)) for i in range(2)])
            yab = Rot([(sbuf(pa, "yab%d" % i, [128, 1024], BF), Buf("yab%d" % i)) for i in range(2)])
            sm = Rot([(sbuf(pa, "smA%d" % i, [128, 8], F32), sbuf(pa, "smB%d" % i, [128, 8], F32),
                       sbuf(pa, "smT%d" % i, [128, 4, 64], F32), Buf("sm%d" % i)) for i in range(4)])
            tk = Rot([(sbuf(pa, "imp%d" % i, [128, 64], F32), sbuf(pa, "sc%d" % i, [128, 64], F32), sbuf(pa, "wk%d" % i, [128, 64], F32),
                       sbuf(pa, "m8a%d" % i, [128, 8], F32), sbuf(pa, "m8b%d" % i, [128, 8], F32), Buf("tk%d" % i)) for i in range(2)])
            dbgt = Rot([(sbuf(pa, "dbgt%d" % i, [128, 1024], F32), Buf("dbgt%d" % i)) for i in range(2)]) if debug else None

            prepped = {}

            Qtok = Rot([(sbuf(pa, "Qtok%d" % i, [128, 1024], BF), Buf("Qtok%d" % i)) for i in range(2)])
            gtmp = Rot([(sbuf(pa, "gtmp%d" % i, [128, 24], F32), Buf("gtmp%d" % i)) for i in range(2)])

            def prep_g(i):
                I = 2 * i + 1
                xt, Bx = xts.next()
                S.add('sp', lambda e: e.dma_start(out=xt[:], in_=xk[I * 128:(I + 1) * 128, :]), writes=[Bx], dma=Bx)
                hT, BhT = hTq.next()
                tmp = tmps.next()
                yield from norm_h_g(xt[:], Bx, gam_mix[:], tmp, lnexp=True)
                yield from trans_g(tmp[2], tmp[3], hT[:], BhT, 7, eng='dve')
                QA, BQA = QAs.next()
                QS, BQSlo, BQShi = QSs.next()
                gn, Bgn = gns.next()
                Qt, BQt = Qtok.next()
                for m in range(2):
                    for k in range(8):
                        S.add('pe', lambda e, k=k, m=m: e.matmul(ps[7][:, :], lhsT=hT[:, k, :], rhs=wq_t[:, k, m * 512:(m + 1) * 512],
                                                                 start=(k == 0), stop=(k == 7)), reads=[BhT, Bwq[k // 4]], writes=[Bps[7]])
                        if k % 2 == 1:
                            yield
                    S.add('dve', lambda e, m=m: e.tensor_scalar(out=Qt[:, m * 512:(m + 1) * 512], in0=ps[7][:, :], scalar1=0.125, scalar2=None,
                                                                op0=ALU.mult), reads=[Bps[7]], writes=[BQt])
                    yield
                for k in range(8):
                    S.add('pe', lambda e, k=k: e.matmul(ps[7][:, 0:24], lhsT=hT[:, k, :], rhs=wgn_t[:, k, :], start=(k == 0), stop=(k == 7)),
                          reads=[BhT, Bw], writes=[Bps[7]])
                yield
                gt_, Bgt = gtmp.next()
                S.add('act', lambda e: e.activation(out=gt_[:], in_=ps[7][:, 0:24], func=AF.Exp, scale=-1.0), reads=[Bps[7]], writes=[Bgt])
                yield
                S.add('dve', lambda e: e.tensor_scalar(out=gt_[:], in0=gt_[:], scalar1=1.0, scalar2=None, op0=ALU.add), reads=[Bgt], writes=[Bgt])
                S.add('dve', lambda e: e.reciprocal(out=gn[:], in_=gt_[:]), reads=[Bgt], writes=[Bgn])
                yield
                pT = ps[7][:].bitcast(BF)
                for m in range(2):
                    for hh in range(8):
                        S.add('pe', lambda e, m=m, hh=hh: e.transpose(out=pT[0:64, hh * 128:(hh + 1) * 128],
                                                                      in_=Qt[:, m * 512 + hh * 64:m * 512 + (hh + 1) * 64], identity=ident[:]),
                              reads=[BQt, Bconst], writes=[Bps[7]])
                        if hh % 4 == 3:
                            yield
                    for g in range(2):
                        dst, Bdst = (QA[0:64, g, :], BQA) if m == 0 else (QS[0:64, g, :], BQSlo)
                        S.add('dve', lambda e, dst=dst, g=g: e.tensor_copy(out=dst, in_=pT[0:64, g * 512:(g + 1) * 512]), reads=[Bps[7]], writes=[Bdst])
                        yield
                prepped[i] = (QA, BQA, QS, BQSlo, BQShi, gn, Bgn)

            def run_steps_g(steps, dyn=None):
                n = len(steps)
                banks = [sbank.next() for _ in range(n)]

                def qk(j):
                    stp = steps[j]
                    b = banks[j]
                    l, r, rd = stp['qk']
                    has_m = stp['mask'] is not None and stp['mask'][0] == 'pe'
                    S.add('pe', lambda e: e.matmul(ps[b][:, :], lhsT=l, rhs=r, start=True, stop=not has_m), reads=rd, writes=[Bps[b]])
                    if has_m:
                        _, l2, r2, rd2 = stp['mask']
                        S.add('pe', lambda e: e.matmul(ps[b][:, :], lhsT=l2, rhs=r2, start=False, stop=True), reads=rd2, writes=[Bps[b]])
                qk(0)
                if n > 1:
                    qk(1)
                for j in range(n):
                    if j + 2 < n:
                        qk(j + 2)
                    stp = steps[j]
                    b = banks[j]
                    Pt, BPt = Pts.next()
                    if stp['abias'] is None:
                        S.add('act', lambda e, b=b, Pt=Pt: e.activation(out=Pt[:], in_=ps[b][:, :], func=AF.Exp),
                              reads=[Bps[b]], writes=[BPt])
                    else:
                        S.add('act', lambda e, b=b, Pt=Pt, stp=stp: e.activation(out=Pt[:], in_=ps[b][:, :], func=AF.Exp, bias=stp['abias']),
                              reads=[Bps[b], Bconst], writes=[BPt])
                    if stp['mask'] is not None and stp['mask'][0] == 'mul':
                        _, map_, mrd = stp['mask']
                        S.add('dve', lambda e, Pt=Pt, map_=map_: e.tensor_mul(out=Pt[:], in0=Pt[:], in1=map_), reads=[BPt] + mrd, writes=[BPt])
                    for h in range(4):
                        acc_ap, vr = stp['v'][h]
                        S.add('pe', lambda e, h=h, acc_ap=acc_ap, vr=vr, Pt=Pt, stp=stp: e.matmul(
                            acc_ap, lhsT=Pt[:, h * 128:(h + 1) * 128], rhs=vr, start=stp['first'][h], stop=stp['last'],
                            skip_group_check=True), reads=[BPt] + stp['vreads'], writes=stp['accB'])
                    if stp['post'] is not None:
                        r_ = stp['post']()
                        if r_ is not None:
                            if dyn is not None:
                                dyn.append(r_)
                            else:
                                for _ in r_:
                                    pass
                    yield

            def run_steps(steps):
                for _ in run_steps_g(steps):
                    pass

            def do_tile(i):
                I = 2 * i + 1
                if i == 0:
                    zipgens([prep_g(0)])
                QA, BQA, QS, BQSlo, BQShi, gn, Bgn = prepped.pop(i)
                ya_bf, Bya = yab.next()
                yb, Byb = ybf.next()
                steps = []
                dyn = []
                for g in range(2):
                    bX, bY = abank.next(), abank.next()
                    accs = [ps[bX][:, 0:129], ps[bX][:, 129:258], ps[bY][:, 0:129], ps[bY][:, 129:258]]

                    def post_cmp(g=g, bX=bX, bY=bY):
                        smA, smB, smT, Bsm = sm.next()
                        imp, sc, wk, m8a, m8b, Btk = tk.next()
                        ca, Bca = caccs.next()
                        for pr, bb in enumerate((bX, bY)):
                            S.add('dve', lambda e, pr=pr, bb=bb: e.tensor_copy(out=ca[:, pr * 2:pr * 2 + 2, :],
                                                                               in_=ps[bb][:, 0:258].rearrange("p (h c) -> p h c", c=129)),
                                  reads=[Bps[bb]], writes=[Bca])

                        def cmp_rest_g():
                            S.add('dve', lambda e: e.tensor_scalar(out=smA[:, 0:4], in0=ca[:, :, 64], scalar1=1e-30, scalar2=None, op0=ALU.max),
                                  reads=[Bca], writes=[Bsm])
                            yield
                            S.add('dve', lambda e: e.reciprocal(out=smA[:, 0:4], in_=smA[:, 0:4]), reads=[Bsm], writes=[Bsm])
                            yield
                            gsl = gn[:, g * 12:(g + 1) * 12].rearrange("p (h b) -> p h b", b=3)[:, :, 0]
                            S.add('dve', lambda e: e.tensor_mul(out=smB[:, 0:4], in0=smA[:, 0:4], in1=gsl), reads=[Bsm, Bgn], writes=[Bsm])
                            yield
                            S.add('dve', lambda e: e.tensor_tensor(
                                out=yb[:, g * 256:(g + 1) * 256].rearrange("p (h d) -> p h d", d=64),
                                in0=ca[:, :, 0:64], in1=bc_last(smB[:, 0:4], 64), op=ALU.mult), reads=[Bca, Bsm], writes=[Byb])
                            yield
                            for h in range(4):
                                if h == 0:
                                    S.add('dve', lambda e, h=h: e.tensor_scalar(out=imp[:], in0=ca[:, h, 65:129], scalar1=smA[:, h:h + 1],
                                                                                scalar2=None, op0=ALU.mult), reads=[Bca, Bsm], writes=[Btk])
                                else:
                                    S.add('dve', lambda e, h=h: e.scalar_tensor_tensor(out=imp[:], in0=ca[:, h, 65:129], scalar=smA[:, h:h + 1],
                                                                                       in1=imp[:], op0=ALU.mult, op1=ALU.add),
                                          reads=[Bca, Bsm, Btk], writes=[Btk])
                                yield
                            yield from topk_g()

                        def topk_g():
                            S.add('dve', lambda e: e.tensor_add(out=sc[:], in0=imp[:], in1=scoreadd_t[:, i, :]), reads=[Btk, Bsa], writes=[Btk])
                            yield
                            S.add('dve', lambda e: e.max(out=m8a[:], in_=sc[:]), reads=[Btk], writes=[Btk])
                            yield
                            S.add('dve', lambda e: e.match_replace(out=wk[:], in_to_replace=m8a[:], in_values=sc[:], imm_value=-3.0e38),
                                  reads=[Btk], writes=[Btk])
                            yield
                            S.add('dve', lambda e: e.max(out=m8b[:], in_=wk[:]), reads=[Btk], writes=[Btk])
                            yield
                            S.add('dve', lambda e: e.tensor_scalar(out=wk[:], in0=sc[:], scalar1=m8b[:, 7:8], scalar2=None, op0=ALU.is_ge),
                                  reads=[Btk], writes=[Btk])
                            yield
                            S.add('dve', lambda e: e.tensor_mul(out=wk[:], in0=wk[:], in1=allowed_t[:, i, :]), reads=[Btk, Bsa], writes=[Btk])
                            yield
                            ng, Bng = negs.next()
                            S.add('dve', lambda e: e.tensor_scalar(out=ng[:, 64:128], in0=wk[:], scalar1=-1.0, scalar2=-NEGM, op0=ALU.add, op1=ALU.mult),
                                  reads=[Btk], writes=[Bng])
                            yield
                            pT = ps[7][:].bitcast(BF)
                            S.add('pe', lambda e: e.transpose(out=pT[:, 0:128], in_=ng[:], identity=ident[:]), reads=[Bng, Bconst], writes=[Bps[7]])
                            yield
                            src = pT[64:128, 0:128]
                            srcb = bass.AP(src.tensor, src.offset, [list(src.ap[0]), [0, 4], list(src.ap[1])])
                            S.add('dve', lambda e: e.tensor_copy(out=QS[64:128, g, :].rearrange("p (h q) -> p h q", h=4), in_=srcb),
                                  reads=[Bps[7]], writes=[BQShi[g]])
                            yield

                        dyn.append(cmp_rest_g())
                    cts = [0, 1] if 8 * I + 6 >= 128 else [0]
                    for ct in cts:
                        steps.append(dict(
                            qk=(KCT[:, g, ct * 128:(ct + 1) * 128], QS[:, g, :], [BKCT, BQSlo, BQShi[g]]),
                            mask=('pe', shift_t[:, i, ct * 128:(ct + 1) * 128], bandx[:, g, :], [Bsa, Bbias]),
                            abias=None,
                            v=[(accs[h], VCx[:, ct, g, :]) for h in range(4)], vreads=[BVCx],
                            accB=[Bps[bX], Bps[bY]], first=[ct == 0 and h in (0, 2) for h in range(4)], last=(ct == cts[-1]),
                            post=post_cmp if ct == cts[-1] else None))

                def std_branch(g, Js, klhs, Bk, qrhs, Bq, K, mixer, vkind, gate_br, is_swa, is_slc, first_yb, last_yb):
                    bA = abank.next()
                    a3 = ps[bA][:, 0:260].rearrange("p (h c) -> p h c", c=65)

                    def post():
                        smA, smB, smT, Bsm = sm.next()
                        if is_swa:
                            S.add('dve', lambda e: e.tensor_add(out=smA[:, 0:4], in0=a3[:, :, 64], in1=sinkexp[:, g * 4:(g + 1) * 4]),
                                  reads=[Bps[bA], Bconst], writes=[Bsm])
                            yield
                            S.add('dve', lambda e: e.reciprocal(out=smB[:, 0:4], in_=smA[:, 0:4]), reads=[Bsm], writes=[Bsm])
                            yield
                            S.add('dve', lambda e: e.tensor_tensor(
                                out=ya_bf[:, g * 256:(g + 1) * 256].rearrange("p (h d) -> p h d", d=64),
                                in0=a3[:, :, 0:64], in1=bc_last(smB[:, 0:4], 64), op=ALU.mult), reads=[Bps[bA], Bsm], writes=[Bya])
                            yield
                            return
                        S.add('dve', lambda e: e.reciprocal(out=smA[:, 0:4], in_=a3[:, :, 64]), reads=[Bps[bA]], writes=[Bsm])
                        yield
                        gsl = gn[:, g * 12:(g + 1) * 12].rearrange("p (h b) -> p h b", b=3)[:, :, gate_br]
                        S.add('dve', lambda e: e.tensor_mul(out=smB[:, 0:4], in0=smA[:, 0:4], in1=gsl), reads=[Bsm, Bgn], writes=[Bsm])
                        yield
                        S.add('dve', lambda e: e.tensor_tensor(out=smT[:], in0=a3[:, :, 0:64], in1=bc_last(smB[:, 0:4], 64), op=ALU.mult),
                              reads=[Bps[bA], Bsm], writes=[Bsm])
                        yield
                        ybg = yb[:, g * 256:(g + 1) * 256].rearrange("p (h d) -> p h d", d=64)
                        if last_yb:
                            S.add('pool', lambda e: e.tensor_add(
                                out=ya_bf[:, 512 + g * 256:512 + (g + 1) * 256].rearrange("p (h d) -> p h d", d=64), in0=ybg, in1=smT[:]),
                                reads=[Byb, Bsm], writes=[Bya])
                        else:
                            S.add('pool', lambda e: e.tensor_add(out=ybg, in0=ybg, in1=smT[:]), reads=[Byb, Bsm], writes=[Byb])
                        yield
                    for n_, J in enumerate(Js):
                        mask = None
                        if J == I:
                            mask = ('mul', bias_t[:, mixer * 4 + 0 + g, :], [Bbias])
                        elif J == I - 1:
                            mask = ('mul', bias_t[:, mixer * 4 + 2 + g, :], [Bbias])
                        elif (not is_swa) and (not is_slc) and J == I - 4:
                            mask = ('mul', farlow_t[:], [Bconst])
                        rd = [Bk, Bq] + ([BQShi[g]] if is_slc else [])
                        steps.append(dict(
                            qk=(klhs[0:K, g, J * 128:(J + 1) * 128], qrhs[0:K, g, :], rd),
                            mask=mask, abias=(keymask_t[:, J:J + 1] if (J == 0 and not is_slc) else None),
                            v=[(a3[:, h, :], Vx[:, J, vkind, g, :]) for h in range(4)], vreads=[BVx],
                            accB=[Bps[bA]], first=[n_ == 0 and h == 0 for h in range(4)], last=(n_ == len(Js) - 1),
                            post=post if n_ == len(Js) - 1 else None))

                for g in range(2):
                    std_branch(g, list(range(max(0, I - 4), I + 1)), KW, BKW, QS, BQSlo, 128, 1, 2, 2, False, False, False, False)
                    std_branch(g, [I - 1, I], KA, BKA, QA, BQA, 128, 0, 0, 0, True, False, False, False)
                dyn.append(run_steps_g(steps, dyn))
                zipgens_dyn(dyn)
                issue_bg2(1)
                steps = []
                for g in range(2):
                    std_branch(g, list(range(0, I + 1)), KS, BKS, QS, BQSlo, 128, 1, 1, 1, False, True, False, True)
                dyn2 = []
                dyn2.append(run_steps_g(steps, dyn2))
                if i + 1 < NQ:
                    dyn2.append(prep_g(i + 1))
                zipgens_dyn(dyn2)
                S.add('sp', lambda e, ya_bf=ya_bf, i=i: e.dma_start(out=yabd[i * 128:(i + 1) * 128, :], in_=ya_bf[:]), reads=[Bya], dma=Bya)
                if debug:
                    dt_, Bdt = dbgt.next()
                    S.add('dve', lambda e, dt_=dt_, ya_bf=ya_bf: e.tensor_copy(out=dt_[:], in_=ya_bf[:]), reads=[Bya], writes=[Bdt])
                    S.add('sp', lambda e, dt_=dt_, i=i: e.dma_start(out=dbg['ya'][i * 128:(i + 1) * 128, :], in_=dt_[:, 0:512]), reads=[Bdt], dma=Bdt)
                    S.add('sp', lambda e, dt_=dt_, i=i: e.dma_start(out=dbg['yb'][i * 128:(i + 1) * 128, :], in_=dt_[:, 512:1024]), reads=[Bdt], dma=Bdt)
            for i_ in range(NQ):
                do_tile(i_)
            S.emit()
        att.close()

        hsend = sbuf(st, "hsend", [128, 8, NQ, 2], BF)
        Bhs = Buf("hsend")
        wfo_s = ExitStack()
        wfo_t = sbuf(wfo_s, "wfo_t", [128, 22, 1024], BF)
        Bwo_ffn = Buf("wfo")
        with ExitStack() as pb:
            wup_t = sbuf(pb, "wup_t", [128, 2, 4, 1024], BF)
            wo_t = sbuf(pb, "wo_t", [128, 8, 1024], BF)
            gam_ffn = sbuf(pb, "gam_ffn", [128, 1024], F32)
            Bwup = [Buf("wup%d" % m) for m in range(2)]
            Bwo = [Buf("wo%d" % k) for k in range(2)]
            S.add('sp', lambda e: e.dma_start(out=gam_ffn[:], in_=gam[1:2, :].rearrange("a n -> (a n)").partition_broadcast(128)),
                  writes=[Bgam], dma=Bgam)
            xts = Rot([(sbuf(pb, "xb%d" % i, [128, 1024], F32), Buf("xb%d" % i)) for i in range(4)])
            tmps = Rot([mk_tmp(pb, "c%d" % i) for i in range(3)])
            hTq = Rot([(sbuf(pb, "hTb%d" % i, [128, 8, 128], BF), Buf("hTb%d" % i)) for i in range(2)])
            sgs = Rot([(sbuf(pb, "sg%d" % i, [128, 2048], F32), Buf("sg%d" % i)) for i in range(2)])
            yabs = Rot([(sbuf(pb, "yabl%d" % i, [128, 1024], BF), Buf("yabl%d" % i)) for i in range(3)])
            yTs = Rot([(sbuf(pb, "yT%d" % i, [128, 8, 128], BF), Buf("yT%d" % i)) for i in range(2)])
            mgs = Rot([(sbuf(pb, "mg%d" % i, [128, 1024], F32), sbuf(pb, "mgt%d" % i, [128, 1024], F32),
                        sbuf(pb, "mgb%d" % i, [128, 1024], BF), Buf("mg%d" % i)) for i in range(2)])
            mTs = Rot([(sbuf(pb, "mT%d" % i, [128, 8, 128], BF), Buf("mT%d" % i)) for i in range(2)])
            x1s = Rot([(sbuf(pb, "x1_%d" % i, [128, 1024], F32), Buf("x1_%d" % i)) for i in range(2)])
            h2Ts = Rot([(sbuf(pb, "h2T%d" % i, [128, 8, 128], BF), Buf("h2T%d" % i)) for i in range(2)])
            gb = Rot([0, 1])
            ub = Rot([2, 3])
            ob = Rot([4, 5])
            def pb_weights():
                for c in range(4):
                    S.add('sp', lambda e, c=c: e.dma_start(out=wup_t[:, 0, c, :], in_=wupab[c * 128:(c + 1) * 128, :]), reads=[Bpre['wup']], writes=[Bwup[0]], dma=Bwup[0], nowaw=True)
                for c in range(4):
                    S.add('sp', lambda e, c=c: e.dma_start(out=wup_t[:, 1, c, :], in_=wupbb[c * 128:(c + 1) * 128, :]), reads=[Bpre['wup']], writes=[Bwup[1]], dma=Bwup[1], nowaw=True)
                for k in range(8):
                    S.add('sp', lambda e, k=k: e.dma_start(out=wo_t[:, k, :], in_=wob[k * 128:(k + 1) * 128, :]), reads=[Bpre['wo']], writes=[Bwo[k // 4]], dma=Bwo[k // 4], nowaw=True)

            wfoq = list(range(22))

            def issue_wfo(n):
                for _ in range(n):
                    if wfoq:
                        c = wfoq.pop(0)
                        S.add('sp', lambda e, c=c: e.dma_start(out=wfo_t[:, c, :], in_=wfob[c * 128:(c + 1) * 128, :]), reads=[Bpre['wfo']], writes=[Bwo_ffn], dma=Bwo_ffn, nowaw=True)
            stA, stB, stA1 = {}, {}, {}

            def stageA(i):
                I = 2 * i + 1
                xt, Bx = xts.next()
                S.add('sp', lambda e: e.dma_start(out=xt[:], in_=xk[I * 128:(I + 1) * 128, :]), writes=[Bx], dma=Bx)
                yl, Byl = yabs.next()
                S.add('sp', lambda e: e.dma_start(out=yl[:], in_=yabd[i * 128:(i + 1) * 128, :]), writes=[Byl], dma=Byl)
                hT, BhT = hTq.next()
                yield from norm_T_g(xt[:], Bx, gam_mix[:], hT[:], BhT, tmps.next(), 7)
                stA1[i] = (xt, Bx, yl, Byl, hT, BhT)

            def stageA2(i):
                xt, Bx, yl, Byl, hT, BhT = stA1.pop(i)
                sg, Bsg = sgs.next()
                for cc in range(4):
                    b = gb.next()
                    for k in range(8):
                        S.add('pe', lambda e, k=k, b=b, cc=cc: e.matmul(ps[b][:, :], lhsT=hT[:, k, :], rhs=wg_t[:, k, cc * 512:(cc + 1) * 512],
                                                                        start=(k == 0), stop=(k == 7)), reads=[BhT, Bwg], writes=[Bps[b]])
                    S.add('act', lambda e, b=b, cc=cc: e.activation(out=sg[:, cc * 512:(cc + 1) * 512], in_=ps[b][:, :], func=AF.Sigmoid),
                          reads=[Bps[b]], writes=[Bsg])
                    yield
                yT, ByT = yTs.next()
                pT = ps[6][:].bitcast(BF)
                for c in range(8):
                    S.add('pe', lambda e, c=c: e.transpose(out=pT[:, c * 128:(c + 1) * 128], in_=yl[:, c * 128:(c + 1) * 128], identity=ident[:]),
                          reads=[Byl, Bconst], writes=[Bps[6]])
                S.add('dve', lambda e: e.tensor_copy(out=yT[:], in_=pT[:, 0:1024].rearrange("p (k t) -> p k t", k=8)), reads=[Bps[6]], writes=[ByT])
                yield
                stA[i] = (xt, Bx, sg, Bsg, yT, ByT)

            def stageB(i):
                xt, Bx, sg, Bsg, yT, ByT = stA.pop(i)
                mg, mgt, mgb, Bmg = mgs.next()
                for m in range(2):
                    for hf in range(2):
                        b = ub.next()
                        for c in range(4):
                            S.add('pe', lambda e, c=c, b=b, m=m, hf=hf: e.matmul(ps[b][:, :], lhsT=yT[:, m * 4 + c, :],
                                                                                rhs=wup_t[:, m, c, hf * 512:(hf + 1) * 512],
                                                                                start=(c == 0), stop=(c == 3)), reads=[ByT, Bwup[m]], writes=[Bps[b]])
                        dst = mg if m == 0 else mgt
                        S.add('dve', lambda e, b=b, m=m, hf=hf, dst=dst: e.tensor_mul(out=dst[:, hf * 512:(hf + 1) * 512], in0=ps[b][:, :],
                                                                                     in1=sg[:, m * 1024 + hf * 512:m * 1024 + (hf + 1) * 512]),
                              reads=[Bps[b], Bsg], writes=[Bmg])
                        yield
                S.add('dve', lambda e: e.tensor_add(out=mgb[:], in0=mg[:], in1=mgt[:]), reads=[Bmg], writes=[Bmg])
                yield
                mT, BmT = mTs.next()
                pT7 = ps[7][:].bitcast(BF)
                for c in range(8):
                    S.add('pe', lambda e, c=c: e.transpose(out=pT7[:, c * 128:(c + 1) * 128], in_=mgb[:, c * 128:(c + 1) * 128], identity=ident[:]),
                          reads=[Bmg, Bconst], writes=[Bps[7]])
                S.add('act', lambda e: e.copy(out=mT[:], in_=pT7[:, 0:1024].rearrange("p (k t) -> p k t", k=8)), reads=[Bps[7]], writes=[BmT])
                yield
                stB[i] = (xt, Bx, mT, BmT)

            def stageC(i):
                xt, Bx, mT, BmT = stB.pop(i)
                x1, Bx1 = x1s.next()
                for hf in range(2):
                    b = ob.next()
                    for c in range(8):
                        S.add('pe', lambda e, c=c, b=b, hf=hf: e.matmul(ps[b][:, :], lhsT=mT[:, c, :], rhs=wo_t[:, c, hf * 512:(hf + 1) * 512],
                                                                        start=(c == 0), stop=(c == 7)), reads=[BmT, Bwo[c // 4]], writes=[Bps[b]])
                    S.add('dve', lambda e, b=b, hf=hf: e.tensor_add(out=x1[:, hf * 512:(hf + 1) * 512], in0=ps[b][:, :],
                                                                    in1=xt[:, hf * 512:(hf + 1) * 512]), reads=[Bps[b], Bx], writes=[Bx1])
                    yield
                S.add('pool', lambda e: e.dma_start(out=x1d[i * 128:(i + 1) * 128, :], in_=x1[:]), reads=[Bx1], dma=Bx1)
                if debug:
                    S.add('sp', lambda e: e.dma_start(out=dbg['x1'][i * 128:(i + 1) * 128, :], in_=x1[:]), reads=[Bx1], dma=Bx1)
                h2T, Bh2T = h2Ts.next()
                yield from norm_T_g(x1[:], Bx1, gam_ffn[:], h2T[:], Bh2T, tmps.next(), 6)
                S.add('pool', lambda e: e.dma_start(
                    out=h2Td.rearrange("p (k t) -> p k t", k=8)[:, :, i * 128:(i + 1) * 128], in_=h2T[:]), reads=[Bh2T], dma=Bh2T)
                S.add('act', lambda e: e.copy(out=hsend[:, :, i, :], in_=h2T[:, :, 126:128]), reads=[Bh2T], writes=[Bhs])

            for s_ in range(NQ + 3):
                zipgens([stageC(s_ - 3) if 0 <= s_ - 3 < NQ else None,
                         stageB(s_ - 2) if 0 <= s_ - 2 < NQ else None,
                         stageA2(s_ - 1) if 0 <= s_ - 1 < NQ else None,
                         stageA(s_) if s_ < NQ else None])
                if s_ == 0:
                    pb_weights()
                elif s_ >= 2:
                    issue_wfo(2)
            issue_wfo(22)
            S.emit()

        with ExitStack() as pf:
            wfi_t = sbuf(pf, "wfi_t", [128, 8, 5632], BF)
            cw_t = sbuf(pf, "cw_t", [128, 4, 44], F32)
            a_t = sbuf(pf, "a_t", [128, 1], F32)
            gam_fin = sbuf(pf, "gam_fin", [128, 1024], F32)
            hrecv = sbuf(pf, "hrecv", [128, 2, 8, NQ, 2], BF)
            hh = sbuf(pf, "hh", [128, 8, NQ, 2], BF)
            hcb = sbuf(pf, "hcb", [128, 44, NQ, 2], F32)
            sav = arena0[:, 15360:16064].bitcast(F32).rearrange("p (c t x) -> p c t x", c=44, t=4)
            hd = hcb[:, 0:8, :, :]
            Bsav = Buf("sav")
            Bwfi = [Buf("wfi%d" % c) for c in range(22)]
            Bcw, Bcin, Bcout, Bhr, Bhh = [Buf(n) for n in "cw cin cout hr hh".split()]
            Bhcb = [Buf("hcb%d" % c) for c in range(44)]
            S.add('sp', lambda e: e.dma_start(out=cin.ap(), in_=hsend[:].rearrange("p k t c -> p (k t c)")), reads=[Bhs], writes=[Bcin], dma=Bcin)
            S.add('pool', lambda e: e.collective_compute("AllGather", ALU.bypass, replica_groups=[[0, 1], [2, 3], [4, 5], [6, 7]],
                                                         ins=[cin.ap().opt()], outs=[cout.ap().opt()]), reads=[Bcin], writes=[Bcout], own_sem=True)
            S.add('sp', lambda e: e.dma_start(out=cw_t[:], in_=cwb), writes=[Bcw], dma=Bcw)
            S.add('sp', lambda e: e.dma_start(out=a_t[:], in_=asel), writes=[Bcw], dma=Bcw)
            S.add('sp', lambda e: e.dma_start(out=gam_fin[:], in_=gam[2:3, :].rearrange("a n -> (a n)").partition_broadcast(128)),
                  writes=[Bgam], dma=Bgam)
            aT = arena0[:, 0:11264].rearrange("p (c n) -> p c n", c=22)
            BaT = Buf("actT")
            hg = arena0[:, 11264:11264 + 4096].rearrange("p (k t) -> p k t", k=8)
            Bhg = Buf("h2g")
            tus = Rot([(sbuf(pf, "tu%d" % i, [128, 4, 128], F32), Buf("tu%d" % i)) for i in range(3)])
            tgs = Rot([(sbuf(pf, "tg%d" % i, [128, 4, 128], F32), Buf("tg%d" % i)) for i in range(3)])
            htm = Rot([(sbuf(pf, "htm%d" % i, [128, NQ], F32), Buf("htm%d" % i)) for i in range(2)])
            x1s = Rot([(sbuf(pf, "x1f%d" % i, [128, 1024], F32), Buf("x1f%d" % i)) for i in range(2)])
            fin = Rot([(sbuf(pf, "fs%d" % i, [128, 1], F32), sbuf(pf, "fr%d" % i, [128, 1], F32),
                        sbuf(pf, "fo%d" % i, [128, 1024], F32), Buf("fin%d" % i)) for i in range(1)])
            ubk = Rot([0, 1])
            gbk = Rot([2, 3])
            obk = Rot([4, 5])
            hbk = Rot([4, 5])
            S.add('sp', lambda e: e.dma_start(out=hg, in_=h2Td.rearrange("p (k t) -> p k t", k=8)[:, :, 0:512]), writes=[Bhg], dma=Bhg)
            Bwfi_h = [[Buf("wfi%d_%d" % (half, c2)) for c2 in range(6)] for half in range(2)]
            for c2 in range(11):
                for half in range(2):
                    c0 = (2 * c2 + 22 * half) * 128
                    S.add('sp', lambda e, c0=c0: e.dma_start(out=wfi_t[:, :, c0:c0 + 256],
                                                            in_=wfib[:, c0:c0 + 256].rearrange("(k p) n -> p k n", p=128)),
                          reads=[Bpre['wfi']], writes=[Bwfi_h[half][c2 // 2]], dma=Bwfi_h[half][c2 // 2], nowaw=True)
            S.add('sp', lambda e: e.dma_start(out=hrecv[:].rearrange("p r k t c -> p r (k t c)"),
                                              in_=cout.ap().rearrange("(r p) n -> p r n", p=128)), reads=[Bcout], writes=[Bhr], dma=Bhr)
            pend = []
            ew_eng = ['pool']
            ubk3 = Rot([0, 1, 6])
            gbk3 = Rot([2, 3, 7])

            def fin_pair(c, res):
                (tu, Btu), (tg, Btg) = res
                S.add('act', lambda e: e.activation(out=tg[:], in_=tg[:], func=AF.Silu), reads=[Btg], writes=[Btg])
                S.add(ew_eng[0], lambda e: e.tensor_mul(out=aT[:, c, :].rearrange("p (t n) -> p t n", t=4), in0=tu[:], in1=tg[:]),
                      reads=[Btu, Btg], writes=[BaT])
            def hh_compute():
                G0 = hrecv[:, 0]
                G1 = hrecv[:, 1]
                S.add('dve', lambda e: e.tensor_copy(out=hd[:, :, 0, :], in_=G0[:, :, 0, :]), reads=[Bhr], writes=[Bhh])
                S.add('dve', lambda e: e.tensor_sub(out=hd[:, :, 1:NQ, :], in0=G0[:, :, 1:NQ, :], in1=G1[:, :, 0:NQ - 1, :]), reads=[Bhr], writes=[Bhh])
                S.add('dve', lambda e: e.tensor_scalar(out=hh[:, :, 0, :], in0=hd[:, :, 0, :], scalar1=a_t[:, 0:1], scalar2=None, op0=ALU.mult),
                      reads=[Bhh, Bcw], writes=[Bhh])
                S.add('dve', lambda e: e.scalar_tensor_tensor(out=hh[:, :, 1:NQ, :], in0=hd[:, :, 1:NQ, :], scalar=a_t[:, 0:1], in1=G1[:, :, 0:NQ - 1, :],
                                                              op0=ALU.mult, op1=ALU.add), reads=[Bhh, Bcw, Bhr], writes=[Bhh])

            def halo_chain(cc):
                half, c = cc // 22, cc % 22
                hb_ = hbk.next()
                for k in range(8):
                    S.add('pe', lambda e, k=k: e.matmul(ps[hb_][:, 0:32], lhsT=wfi_t[:, k, cc * 128:(cc + 1) * 128],
                                                        rhs=hh[:, k, :, :].rearrange("p t c -> p (t c)"),
                                                        start=(k == 0), stop=(k == 7)), reads=[Bwfi_h[half][c // 4], Bhh], writes=[Bps[hb_]])
                p2 = ps[hb_][:, 0:32].rearrange("p (t c) -> p t c", c=2)
                ht, Bht = htm.next()
                S.add('dve', lambda e: e.tensor_scalar(out=hcb[:, cc, :, 1], in0=p2[:, :, 1], scalar1=cw_t[:, 0, cc:cc + 1],
                                                       scalar2=None, op0=ALU.mult), reads=[Bps[hb_], Bcw], writes=[Bhcb[cc]])
                S.add('dve', lambda e: e.tensor_scalar(out=ht[:], in0=p2[:, :, 0], scalar1=cw_t[:, 0, cc:cc + 1],
                                                       scalar2=None, op0=ALU.mult), reads=[Bps[hb_], Bcw], writes=[Bht])
                S.add('dve', lambda e: e.scalar_tensor_tensor(out=hcb[:, cc, :, 0], in0=p2[:, :, 1], scalar=cw_t[:, 1, cc:cc + 1],
                                                              in1=ht[:], op0=ALU.mult, op1=ALU.add),
                      reads=[Bps[hb_], Bcw, Bht], writes=[Bhcb[cc]])
            hq = [p + 22 * h_ for p in range(22) for h_ in range(2)]
            for Gq in range(4):
                if Gq > 0:
                    ew_eng[0] = 'pool'
                for c in range(22):
                    res = []
                    for half, bk, ts_ in ((0, ubk3, tus), (1, gbk3, tgs)):
                        cc = c + 22 * half
                        b = bk.next()
                        for k in range(8):
                            S.add('pe', lambda e, k=k, b=b, cc=cc: e.matmul(ps[b][:, :], lhsT=wfi_t[:, k, cc * 128:(cc + 1) * 128], rhs=hg[:, k, :],
                                                                            start=(k == 0), stop=(k == 7)), reads=[Bwfi_h[half][c // 4], Bhg], writes=[Bps[b]])
                        tt, Btt = ts_.next()
                        p3 = ps[b][:, :].rearrange("p (t n) -> p t n", t=4)
                        S.add('act', lambda e, b=b, cc=cc, tt=tt: e.activation(out=tt[:].rearrange("p t n -> p (t n)"), in_=ps[b][:, :], func=AF.Identity,
                                                                               scale=cw_t[:, 2, cc:cc + 1], bias=cw_t[:, 3, cc:cc + 1]),
                              reads=[Bps[b], Bcw], writes=[Btt])
                        S.add('dve', lambda e, p3=p3, cc=cc, tt=tt: e.scalar_tensor_tensor(out=tt[:, :, 1:128], in0=p3[:, :, 0:127], scalar=cw_t[:, 1, cc:cc + 1],
                                                                                          in1=tt[:, :, 1:128], op0=ALU.mult, op1=ALU.add),
                              reads=[Bps[b], Bcw, Btt], writes=[Btt])
                        S.add('dve', lambda e, p3=p3, cc=cc, tt=tt: e.scalar_tensor_tensor(out=tt[:, :, 2:128], in0=p3[:, :, 0:126], scalar=cw_t[:, 0, cc:cc + 1],
                                                                                          in1=tt[:, :, 2:128], op0=ALU.mult, op1=ALU.add),
                              reads=[Bps[b], Bcw, Btt], writes=[Btt])
                        if Gq == 0:
                            S.add(ew_eng[0], lambda e, cc=cc, tt=tt: e.tensor_copy(out=sav[:, cc, :, :], in_=tt[:, :, 0:2]), reads=[Btt], writes=[Bsav])
                        else:
                            S.add(ew_eng[0], lambda e, cc=cc, tt=tt, Gq=Gq: e.tensor_add(out=tt[:, :, 0:2], in0=tt[:, :, 0:2], in1=hcb[:, cc, Gq * 4:(Gq + 1) * 4, :]),
                                  reads=[Btt, Bhcb[cc]], writes=[Btt])
                        res.append((tt, Btt))
                    pend.append((c, res))
                    if len(pend) > 1:
                        fin_pair(*pend.pop(0))
                    if Gq == 0 and c >= 8:
                        if c == 8:
                            hh_compute()
                        for _ in range(3):
                            if hq:
                                halo_chain(hq.pop(0))
                while pend:
                    fin_pair(*pend.pop(0))
                if Gq == 0:
                    while hq:
                        halo_chain(hq.pop(0))
                    S.add('dve', lambda e: e.tensor_add(out=sav[:, :, :, :], in0=sav[:, :, :, :], in1=hcb[:, :, 0:4, :]), reads=[Bsav] + Bhcb, writes=[Bsav])
                    S.add('act', lambda e: e.activation(out=sav[:, 22:44, :, :], in_=sav[:, 22:44, :, :], func=AF.Silu), reads=[Bsav], writes=[Bsav])
                    S.add('dve', lambda e: e.tensor_mul(out=aT[:, :, :].rearrange("p c (t n) -> p c t n", t=4)[:, :, :, 0:2], in0=sav[:, 0:22, :, :],
                                                        in1=sav[:, 22:44, :, :]), reads=[Bsav, BaT], writes=[BaT])
                if Gq + 1 < 4:
                    S.add('sp', lambda e, Gq=Gq: e.dma_start(out=hg, in_=h2Td.rearrange("p (k t) -> p k t", k=8)[:, :, (Gq + 1) * 512:(Gq + 2) * 512]),
                          writes=[Bhg], dma=Bhg)
                for t in range(4):
                    i = Gq * 4 + t
                    x1, Bx1 = x1s.next()
                    S.add('sp', lambda e, x1=x1, i=i: e.dma_start(out=x1[:], in_=x1d[i * 128:(i + 1) * 128, :]), writes=[Bx1], dma=Bx1)
                    x2, Bx2 = x1, Bx1
                    for hf in range(2):
                        b = obk.next()
                        for c in range(22):
                            S.add('pe', lambda e, c=c, b=b, hf=hf, t=t: e.matmul(ps[b][:, :], lhsT=aT[:, c, t * 128:(t + 1) * 128],
                                                                                rhs=wfo_t[:, c, hf * 512:(hf + 1) * 512],
                                                                                start=(c == 0), stop=(c == 21)), reads=[BaT, Bwo_ffn], writes=[Bps[b]])
                        S.add('dve', lambda e, b=b, hf=hf, x1=x1, x2=x2: e.tensor_add(out=x2[:, hf * 512:(hf + 1) * 512], in0=ps[b][:, :],
                                                                                      in1=x1[:, hf * 512:(hf + 1) * 512]), reads=[Bps[b], Bx1], writes=[Bx2])
                    fs, fr, fo, Bf = fin.next()
                    S.add('act', lambda e, fo=fo, fs=fs, x2=x2: e.activation(out=fo[:], in_=x2[:], func=AF.Square, accum_out=fs[:]), reads=[Bx2], writes=[Bf])
                    S.add('dve', lambda e, fs=fs, fr=fr: e.tensor_scalar(out=fr[:], in0=fs[:], scalar1=1.0 / 1024, scalar2=1e-6, op0=ALU.mult, op1=ALU.add),
                          reads=[Bf], writes=[Bf])
                    S.add('act', lambda e, fr=fr: e.activation(out=fr[:], in_=fr[:], func=AF.Sqrt), reads=[Bf], writes=[Bf])
                    S.add('dve', lambda e, fr=fr: e.reciprocal(out=fr[:], in_=fr[:]), reads=[Bf], writes=[Bf])
                    S.add('dve', lambda e, fr=fr, fo=fo, x2=x2: e.scalar_tensor_tensor(out=fo[:], in0=x2[:], scalar=fr[:, 0:1], in1=gam_fin[:],
                                                                                      op0=ALU.mult, op1=ALU.mult), reads=[Bf, Bx2, Bgam], writes=[Bf])
                    S.add('pool', lambda e, fo=fo, i=i: e.dma_start(out=out[i * 128:(i + 1) * 128, :], in_=fo[:]), reads=[Bf], dma=Bf)
            S.emit()
        wfo_s.close()
    return nc


def _t5_bucket(d):
    d = np.maximum(d, 0)
    dd = np.maximum(d, 1).astype(np.float32)
    large = 16 + (np.log(dd / np.float32(16)) / np.float32(math.log(128 / 16)) * np.float32(16)).astype(np.int32)
    large = np.minimum(large, 31)
    return np.where(d < 16, d, large)


def _host_consts(r):
    c = {}
    km = np.zeros((128, 32), np.float32)
    cm = np.zeros((128, 2), np.float32)
    if r == 0:
        km[:, 0] = NEGM
        cm[0:8, 0] = NEGM
    cm[127, 1] = NEGM
    c['keymask'] = km
    c['cmask'] = cm
    k = np.arange(4096)
    c['emat'] = (k[None, :] // 64 == np.arange(64)[:, None]).astype(np.float32).astype(BF_NP)
    sa = np.zeros((128, NQ, 64), np.float32)
    al = np.zeros((128, NQ, 64), np.float32)
    shift = 1 - r
    for i in range(NQ):
        I = 2 * i + 1
        qpos = I * 128 + np.arange(128)
        qblk = qpos // 64
        j = np.arange(64)[None, :]
        first = 2 * shift
        forced = (j == first) | (j == qblk[:, None]) | (j == qblk[:, None] - 1)
        future = j > qblk[:, None]
        dummy = j < first
        a = np.where(forced, 1e30, 0.0)
        a = np.where(future | dummy, -1e30, a)
        sa[:, i, :] = a
        al[:, i, :] = (~(future | dummy)).astype(np.float32)
    c['scoreadd'] = sa
    c['allowed'] = al
    kk = np.arange(128)[:, None]
    qq = np.arange(128)[None, :]
    c['farlow'] = np.tile(np.where(kk > qq, 0.0, NEGM).astype(np.float32), (1, 4)).astype(BF_NP)
    se = np.zeros((33, NQ, 256), np.float32)
    for i in range(NQ):
        I = 2 * i + 1
        for m in range(32):
            cc = 8 * I - 9 + (31 - m)
            if 0 <= cc < 256:
                se[m, i, cc] = 1.0
        lo = 8 * I - 9 + 32
        se[32, i, max(lo, 0):] = 1.0
        se[32, i, 255] = 1.0
        if r == 0:
            se[32, i, 0:8] = 1.0
    c['shiftext'] = se.astype(BF_NP)
    oha = np.zeros((33, 1024), np.float32)
    ohb = np.zeros((33, 1024), np.float32)
    d = np.arange(1024) - 512
    bk = _t5_bucket(d)
    for idx in range(1024):
        if d[idx] < 0:
            oha[32, idx] = 1
            ohb[32, idx] = 1
        else:
            ohb[bk[idx], idx] = 1
            if d[idx] < 128:
                oha[bk[idx], idx] = 1
            else:
                oha[32, idx] = 1
    c['oha'] = oha
    c['ohb'] = ohb
    c['asel'] = np.full((128, 1), float(r), np.float32)
    ov = np.zeros((256, 64), np.float32)
    for j in range(64):
        for m in range(4):
            for n in range(2):
                ci = 4 * j + m - n
                if 0 <= ci < 256:
                    ov[ci, j] += 1
    c['overlap'] = np.ascontiguousarray(ov.reshape(2, 128, 64).transpose(1, 0, 2))
    return c


_NC_CACHE = {}


def run(inputs, debug=False):
    f = lambda a: np.ascontiguousarray(np.asarray(a, dtype=np.float32))
    x = f(inputs['x'])
    w_in = f(inputs['w_in'])[0]
    cs = lambda a, b: w_in[:, a:b]
    shared = {
        'w_kf': np.ascontiguousarray(np.concatenate([cs(O_KA, O_KA + 128), cs(O_KSL, O_KSL + 128), cs(O_KW, O_KW + 128),
                                                     cs(O_KC, O_KC + 128), cs(O_VC, O_VC + 128)], axis=1)),
        'w_v': np.ascontiguousarray(np.concatenate([cs(O_VA, O_VA + 128), cs(O_VSL, O_VSL + 128), cs(O_VW, O_VW + 128)], axis=1)),
        'w_q': np.ascontiguousarray(np.concatenate([cs(O_QA, O_QA + 512), cs(O_QB, O_QB + 512)], axis=1)),
        'w_g': np.ascontiguousarray(np.concatenate([cs(O_GA, O_GA + 1024), cs(O_GB, O_GB + 1024), cs(O_GN, O_GN + 24)], axis=1)),
        'gam': np.ascontiguousarray(np.stack([f(inputs['norm_mix'])[0], f(inputs['norm_ffn'])[0], f(inputs['norm_final'])])),
        'sinks': f(inputs['attn_sinks']),
        'table': f(inputs['rel_bias_table']),
        'posT': np.ascontiguousarray(np.stack([f(inputs['cmp_pos_k'])[0].T, f(inputs['cmp_pos_v'])[0].T])),
        'w1': np.ascontiguousarray(np.stack([f(inputs['cmp_w1_k'])[0], f(inputs['cmp_w1_v'])[0]])),
        'w2': np.ascontiguousarray(np.stack([f(inputs['cmp_w2_k'])[0], f(inputs['cmp_w2_v'])[0]])),
        'w_upa': f(inputs['w_up_a'])[0], 'w_upb': f(inputs['w_up_b'])[0], 'w_out': f(inputs['w_out'])[0],
        'w_fi': f(inputs['w_ffn_in'])[0], 'w_fo': f(inputs['w_ffn_out'])[0],
    }
    cw = f(inputs['conv_w'])[0]
    cb = f(inputs['conv_b'])
    cwb = np.concatenate([cw, cb], axis=0).reshape(4, 44, 128).transpose(2, 0, 1)
    shared['cwb'] = np.ascontiguousarray(cwb)
    consts = [_host_consts(0), _host_consts(1)]
    in_maps = []
    for c in range(8):
        b, r = c // 2, c % 2
        if r == 1:
            xkk = x[b]
        else:
            xkk = np.concatenate([np.zeros((128, 1024), np.float32), x[b][:3968]], axis=0)
        m = dict(shared)
        m.update(consts[r])
        m['xk'] = np.ascontiguousarray(xkk)
        in_maps.append(m)
    key = bool(debug)
    if key not in _NC_CACHE:
        _NC_CACHE[key] = build(debug)
    nc = _NC_CACHE[key]
    res = run_bass_kernel_spmd(nc, in_maps, core_ids=list(range(8)))
    outp = np.zeros((4, 4096, 1024), np.float32)
    for c in range(8):
        b, r = c // 2, c % 2
        o = np.asarray(res.results[c]['out']).reshape(NQ, 128, 1024)
        outp[b].reshape(16, 2, 128, 1024)[:, r] = o
    if debug:
        return outp, res
    return outp


def kernel(**inputs):
    return run(inputs)
```

```python
import math
from contextlib import ExitStack
import numpy as np
import ml_dtypes
BF_NP = ml_dtypes.bfloat16
import concourse.bass as bass
import concourse.mybir as mybir
from concourse.bass_utils import run_bass_kernel_spmd

F32 = mybir.dt.float32
BF = mybir.dt.bfloat16
AF = mybir.ActivationFunctionType
ALU = mybir.AluOpType
AX = mybir.AxisListType
NEGM = -30000.0
NQ = 16


class Buf:
    def __init__(self, name):
        self.name = name
        self.w = None
        self.r = []
        self.sem = None
        self.cnt = 0


class Sched:
    ENG = ['pe', 'act', 'dve', 'pool', 'sp']

    def __init__(self, nc, stack):
        self.nc = nc
        self.ops = []
        self.start = 0
        self.stack = stack
        self.esem = {e: stack.enter_context(nc.semaphore('sem_' + e)) for e in self.ENG}
        self.ecnt = {e: 0 for e in self.ENG}
        self.bar = stack.enter_context(nc.semaphore('sem_bar'))
        self.nphase = 0

    def add(self, eng, fn, reads=(), writes=(), dma=None, bg=False, nowaw=False, own_sem=False):
        i = len(self.ops)
        deps = set()
        for b in reads:
            if b.w is not None:
                deps.add(b.w)
        for b in writes:
            if b.w is not None and not nowaw:
                deps.add(b.w)
            deps.update(b.r)
        for b in reads:
            b.r.append(i)
        for b in writes:
            b.w = i
            b.r = []
        self.ops.append(dict(eng=eng, fn=fn, deps=deps, dma=dma, bg=bg, own_sem=own_sem))
        return i

    def emit(self):
        nc = self.nc
        ops = self.ops
        s0 = self.start
        for o in ops[s0:]:
            o['deps'] = {d for d in o['deps'] if d >= s0 or ops[d]['bg']}
            if o['eng'] == 'pe' and o['dma'] is None:
                o['deps'] = {d for d in o['deps'] if not (ops[d]['eng'] == 'pe' and ops[d]['dma'] is None)}
        need = [False] * len(ops)
        for o in ops[s0:]:
            for d in o['deps']:
                need[d] = True
        self.nphase += 1
        mine_last = {}
        for e in self.ENG:
            idxs = [i for i in range(s0, len(ops)) if ops[i]['eng'] == e and ops[i]['fn'] is not None and ops[i]['dma'] is None]
            mine_last[e] = idxs[-1] if idxs else None
            if idxs:
                need[idxs[-1]] = True
        alld = []
        for i in range(s0, len(ops)):
            o = ops[i]
            if o['dma'] is not None:
                b = o['dma']
                if b.sem is None:
                    b.sem = self.stack.enter_context(nc.semaphore('dsem_' + b.name))
                b.cnt += 16
                o['sig'] = (b.sem, b.cnt)
                if not o['bg']:
                    alld.append(i)
            elif o['own_sem']:
                o['sig'] = (self.stack.enter_context(nc.semaphore('osem_%d' % i)), 1)
            elif need[i] and o['fn'] is not None:
                self.ecnt[o['eng']] += 1
                o['sig'] = (self.esem[o['eng']], self.ecnt[o['eng']])
            else:
                o['sig'] = None
        with nc.Block() as block:
            reg = dict(pe=block.tensor, act=block.scalar, dve=block.vector, pool=block.gpsimd, sp=block.sync)
            for e in self.ENG:
                mine = [o for o in ops[s0:] if o['eng'] == e]

                def body(eh, mine=mine, e=e):
                    seen = {}

                    def wait_for(d):
                        if ops[d]['sig'] is None:
                            return
                        sem, val = ops[d]['sig']
                        k = id(sem)
                        if seen.get(k, 0) >= val:
                            return
                        eh.wait_ge(sem, val)
                        seen[k] = val
                    for o in mine:
                        for d in sorted(o['deps']):
                            wait_for(d)
                        if o['fn'] is None:
                            continue
                        ins = o['fn'](eh)
                        if o['sig'] is not None:
                            sem, val = o['sig']
                            ins.then_inc(sem, 16 if o['dma'] is not None else 1)
                    if e == 'sp':
                        last = {}
                        for d in alld:
                            sem, val = ops[d]['sig']
                            if id(sem) not in last or ops[last[id(sem)]]['sig'][1] < val:
                                last[id(sem)] = d
                        for d in sorted(last.values()):
                            wait_for(d)
                    if mine_last[e] is not None:
                        wait_for(mine_last[e])
                    eh.sem_inc(self.bar, 1)
                    eh.wait_ge(self.bar, 5 * self.nphase)
                reg[e](body)
        self.start = len(ops)


def bc_last(ap, n):
    return bass.AP(ap.tensor, ap.offset, [list(a) for a in ap.ap] + [[0, n]])


class Rot:
    def __init__(self, items):
        self.items = items
        self.i = 0

    def next(self):
        it = self.items[self.i % len(self.items)]
        self.i += 1
        return it


O_QA, O_KA, O_VA, O_QB, O_KC, O_VC, O_KSL, O_VSL, O_KW, O_VW, O_GN, O_GA, O_GB = (
    0, 512, 640, 768, 1280, 1408, 1536, 1664, 1792, 1920, 2048, 2072, 3096)


def build(debug=False):
    nc = bass.Bass("TRN2", target_bir_lowering=False)

    def di(n, s, dt=F32):
        return nc.dram_tensor(n, list(s), dt, kind="ExternalInput").ap()
    xk = di("xk", [4096, 1024])
    w_kf = di("w_kf", [1024, 640])
    w_v = di("w_v", [1024, 384])
    w_q = di("w_q", [1024, 1024])
    w_g = di("w_g", [1024, 2072])
    gam = di("gam", [3, 1024])
    sinks = di("sinks", [1, 8])
    table = di("table", [32, 16])
    posT = di("posT", [2, 64, 32])
    w1 = di("w1", [2, 2048, 256])
    w2 = di("w2", [2, 256, 64])
    w_upa = di("w_upa", [512, 1024])
    w_upb = di("w_upb", [512, 1024])
    w_out = di("w_out", [1024, 1024])
    w_fi = di("w_fi", [1024, 5632])
    cwb = di("cwb", [128, 4, 44])
    w_fo = di("w_fo", [2816, 1024])
    keymask = di("keymask", [128, 32])
    cmask = di("cmask", [128, 2])
    emat = di("emat", [64, 4096], BF)
    scoreadd = di("scoreadd", [128, NQ, 64])
    allowed = di("allowed", [128, NQ, 64])
    farlow = di("farlow", [128, 512], BF)
    shiftext = di("shiftext", [33, NQ, 256], BF)
    oha = di("oha", [33, 1024])
    ohb = di("ohb", [33, 1024])
    asel = di("asel", [128, 1])
    overlap = di("overlap", [128, 2, 64])
    out = nc.dram_tensor("out", [2048, 1024], F32, kind="ExternalOutput").ap()
    dbg = {}
    if debug:
        dbg['ya'] = nc.dram_tensor("dbg_ya", [2048, 512], F32, kind="ExternalOutput").ap()
        dbg['yb'] = nc.dram_tensor("dbg_yb", [2048, 512], F32, kind="ExternalOutput").ap()
        dbg['x1'] = nc.dram_tensor("dbg_x1", [2048, 1024], F32, kind="ExternalOutput").ap()
    biasd = nc.dram_tensor("biasd", [16, 1024], BF, kind="Internal").ap()
    x1d = nc.dram_tensor("x1d", [2048, 1024], F32, kind="Internal").ap()
    h2Td = nc.dram_tensor("h2Td", [128, 8 * 2048], BF, kind="Internal").ap()
    yabd = nc.dram_tensor("yabd", [2048, 1024], BF, kind="Internal").ap()
    cin = nc.dram_tensor("cin", [128, 256], BF, kind="Internal")
    w1b = nc.dram_tensor("w1b", [2, 64, 32, 256], BF, kind="Internal").ap()
    wqb = nc.dram_tensor("wqb", [1024, 1024], BF, kind="Internal").ap()
    wupab = nc.dram_tensor("wupab", [512, 1024], BF, kind="Internal").ap()
    wupbb = nc.dram_tensor("wupbb", [512, 1024], BF, kind="Internal").ap()
    wob = nc.dram_tensor("wob", [1024, 1024], BF, kind="Internal").ap()
    wfib = nc.dram_tensor("wfib", [1024, 5632], BF, kind="Internal").ap()
    wfob = nc.dram_tensor("wfob", [2816, 1024], BF, kind="Internal").ap()
    cout = nc.dram_tensor("cout", [256, 256], BF, kind="Internal")

    with ExitStack() as st:
        S = Sched(nc, st)

        def sbuf(stk, n, s, d):
            return stk.enter_context(nc.sbuf_tensor(n, list(s), d))
        ps = [st.enter_context(nc.psum_tensor("ps%d" % i, [128, 512], F32)) for i in range(8)]
        Bps = [Buf("ps%d" % i) for i in range(8)]

        ident = sbuf(st, "ident", [128, 128], BF)
        gam_mix = sbuf(st, "gam_mix", [128, 1024], F32)
        junk_t = sbuf(st, "junk_t", [128, 1024], BF)
        Bjunk = Buf("junk")
        arena0 = sbuf(st, "arena0", [128, 16384], BF)
        wg_t = arena0[:, :].rearrange("p (k n) -> p k n", k=8)
        Bwg = Buf("wg")
        Bconst = Buf("const")
        Bgam = Buf("gam")

        def norm_h_g(xt, Bx, gam_ap, tmp, lnexp=False):
            ssq, rstd, h, Bt = tmp
            if lnexp:
                S.add('act', lambda e: e.activation(out=junk_t[:], in_=xt, func=AF.Square, accum_out=ssq[:]),
                      reads=[Bx], writes=[Bt, Bjunk])
                yield
                S.add('dve', lambda e: e.tensor_scalar(out=rstd[:], in0=ssq[:], scalar1=1.0 / 1024, scalar2=1e-6,
                                                       op0=ALU.mult, op1=ALU.add), reads=[Bt], writes=[Bt])
                yield
                S.add('act', lambda e: e.activation(out=rstd[:], in_=rstd[:], func=AF.Ln), reads=[Bt], writes=[Bt])
                S.add('act', lambda e: e.activation(out=rstd[:], in_=rstd[:], func=AF.Exp, scale=-0.5), reads=[Bt], writes=[Bt])
                yield
                S.add('dve', lambda e: e.scalar_tensor_tensor(out=h[:], in0=xt, scalar=rstd[:, 0:1], in1=gam_ap,
                                                              op0=ALU.mult, op1=ALU.mult),
                      reads=[Bx, Bt, Bgam], writes=[Bt])
                yield
                return
            S.add('act', lambda e: e.activation(out=junk_t[:], in_=xt, func=AF.Square, accum_out=ssq[:]),
                  reads=[Bx], writes=[Bt, Bjunk])
            yield
            S.add('dve', lambda e: e.tensor_scalar(out=rstd[:], in0=ssq[:], scalar1=1.0 / 1024, scalar2=1e-6,
                                                   op0=ALU.mult, op1=ALU.add), reads=[Bt], writes=[Bt])
            yield
            S.add('act', lambda e: e.activation(out=rstd[:], in_=rstd[:], func=AF.Sqrt), reads=[Bt], writes=[Bt])
            yield
            S.add('dve', lambda e: e.reciprocal(out=rstd[:], in_=rstd[:]), reads=[Bt], writes=[Bt])
            yield
            S.add('dve', lambda e: e.scalar_tensor_tensor(out=h[:], in0=xt, scalar=rstd[:, 0:1], in1=gam_ap,
                                                          op0=ALU.mult, op1=ALU.mult),
                  reads=[Bx, Bt, Bgam], writes=[Bt])
            yield

        def trans_g(h, Bt, hT_out, BhT, bank, eng='act'):
            pT = ps[bank][:].bitcast(BF)
            for k in range(8):
                S.add('pe', lambda e, k=k: e.transpose(out=pT[:, k * 128:(k + 1) * 128], in_=h[:, k * 128:(k + 1) * 128],
                                                       identity=ident[:]), reads=[Bt, Bconst], writes=[Bps[bank]])
                if k % 4 == 3:
                    yield
            if eng == 'act':
                S.add('act', lambda e: e.copy(out=hT_out, in_=pT[:, 0:1024].rearrange("p (k t) -> p k t", k=8)),
                      reads=[Bps[bank]], writes=[BhT])
            else:
                S.add('dve', lambda e: e.tensor_copy(out=hT_out, in_=pT[:, 0:1024].rearrange("p (k t) -> p k t", k=8)),
                      reads=[Bps[bank]], writes=[BhT])
            yield

        def norm_T_g(xt, Bx, gam_ap, hT_out, BhT, tmp, bank):
            yield from norm_h_g(xt, Bx, gam_ap, tmp)
            yield from trans_g(tmp[2], tmp[3], hT_out, BhT, bank)

        def norm_T(xt, Bx, gam_ap, hT_out, BhT, tmp, bank):
            for _ in norm_T_g(xt, Bx, gam_ap, hT_out, BhT, tmp, bank):
                pass

        def zipgens_dyn(lst):
            while lst:
                for g in list(lst):
                    try:
                        next(g)
                    except StopIteration:
                        lst.remove(g)

        def zipgens(gens):
            gens = [g for g in gens if g is not None]
            while gens:
                alive = []
                for g in gens:
                    try:
                        next(g)
                        alive.append(g)
                    except StopIteration:
                        pass
                gens = alive

        cast_rr = [0]

        def load_cast(dst, src, stage_rot, Bdst, engs=('dve', 'act')):
            stg_t, Bst = stage_rot.next()
            n = 1
            for d_ in dst.shape[1:]:
                n *= d_
            sv = stg_t[0:dst.shape[0], 0:n]
            if len(dst.shape) == 3:
                sv = sv.rearrange("p (a b) -> p a b", a=dst.shape[1])
            S.add('sp', lambda e: e.dma_start(out=sv, in_=src), writes=[Bst], dma=Bst)
            eng = engs[cast_rr[0] % len(engs)]
            cast_rr[0] += 1
            if eng == 'act':
                S.add('act', lambda e: e.copy(out=dst, in_=sv), reads=[Bst], writes=[Bdst], nowaw=True)
            else:
                S.add(eng, lambda e: e.tensor_copy(out=dst, in_=sv), reads=[Bst], writes=[Bdst], nowaw=True)

        def mk_tmp(stk, n):
            return (sbuf(stk, "ssq" + n, [128, 1], F32),
                    sbuf(stk, "rstd" + n, [128, 1], F32), sbuf(stk, "hbf" + n, [128, 1024], BF), Buf("nt" + n))

        att = ExitStack()
        KA = sbuf(att, "KA", [128, 2, 4096], BF)
        KW = sbuf(att, "KW", [128, 2, 4096], BF)
        KS = sbuf(att, "KS", [128, 2, 4096], BF)
        Vx = sbuf(att, "Vx", [128, 32, 3, 2, 65], BF)
        KCT = sbuf(att, "KCT", [128, 2, 256], BF)
        VCx = sbuf(att, "VCx", [128, 2, 2, 129], BF)
        bias_t = sbuf(att, "bias_t", [128, 8, 512], BF)
        bandx = sbuf(att, "bandx", [128, 2, 512], BF)
        farlow_t = sbuf(att, "farlow_t", [128, 512], BF)
        keymask_t = sbuf(att, "keymask_t", [128, 32], F32)
        cmask_t = sbuf(att, "cmask_t", [128, 2], F32)
        sinkexp = sbuf(att, "sinkexp", [128, 8], F32)
        BKA, BKW, BKS, BVx, BKCT, BVCx, Bbias = [Buf(n) for n in "KA KW KS Vx KCT VCx bias".split()]

        def bias_chain(stk):
            tabx = sbuf(stk, "tabx", [33, 16], F32)
            tab31 = sbuf(stk, "tab31", [32, 16], F32)
            oh_t = sbuf(stk, "oh_t", [33, 1024], F32)
            vec_t = sbuf(stk, "vec_t", [16, 1024], BF)
            Jf = sbuf(stk, "Jf", [128, 128], F32)
            Jb = sbuf(stk, "Jb", [128, 128], BF)
            Hxs = Rot([(sbuf(stk, "Hx%d" % i, [128, 512], BF), Buf("Hx%d" % i)) for i in range(4)])
            Bp0, Bvec, Boh, Btab, Bt31, BJ = [Buf(n) for n in "bc_p0 bc_vec bc_oh bc_tab bc_t31 bc_J".split()]
            S.add('sp', lambda e: e.dma_start(out=tabx[0:32, :], in_=table), writes=[Btab], dma=Btab)
            S.add('sp', lambda e: e.dma_start(out=tab31[:], in_=table[31:32, :].rearrange("a n -> (a n)").partition_broadcast(32)),
                  writes=[Bt31], dma=Bt31)
            S.add('pool', lambda e: e.memset(tabx[32:33, :], NEGM), writes=[Btab])
            S.add('dve', lambda e: e.tensor_sub(out=tabx[0:32, :], in0=tabx[0:32, :], in1=tab31[:]), reads=[Btab, Bt31], writes=[Btab])
            yield
            for m in range(2):
                S.add('sp', lambda e, m=m: e.dma_start(out=oh_t[:, :], in_=(oha if m == 0 else ohb)), writes=[Boh], dma=Boh)
                for hf in range(2):
                    S.add('pe', lambda e, m=m, hf=hf: e.matmul(ps[0][0:8, :], lhsT=tabx[:, m * 8:(m + 1) * 8],
                                                               rhs=oh_t[:, hf * 512:(hf + 1) * 512], start=True, stop=True),
                          reads=[Btab, Boh], writes=[Bps[0]])
                    S.add('act', lambda e, m=m, hf=hf: e.copy(out=vec_t[0:8, hf * 512:(hf + 1) * 512], in_=ps[0][0:8, :]),
                          reads=[Bps[0]], writes=[Bvec])
                    S.add('sp', lambda e, m=m, hf=hf: e.dma_start(out=biasd[m * 8:(m + 1) * 8, hf * 512:(hf + 1) * 512],
                                                                   in_=vec_t[0:8, hf * 512:(hf + 1) * 512]),
                          reads=[Bvec], writes=[Bp0], dma=Bvec)
                    yield
            bt = biasd.tensor
            S.add('pool', lambda e: e.memset(Jf[:], 0.0), writes=[BJ])
            S.add('pool', lambda e: e.affine_select(out=Jf[:], in_=Jf[:], pattern=[[1, 128]], compare_op=ALU.not_equal,
                                                    fill=1.0, base=-127, channel_multiplier=1), reads=[BJ], writes=[BJ])
            S.add('dve', lambda e: e.tensor_copy(out=Jb[:], in_=Jf[:]), reads=[BJ], writes=[BJ])
            yield
            for m in range(2):
                for kind in range(2):
                    for g in range(2):
                        idx = m * 4 + kind * 2 + g
                        Hx, BHx = Hxs.next()
                        src = bass.AP(bt, (m * 8 + 4 * g) * 1024 + 512 + 128 * kind - 127, [[1, 128], [1024, 4], [1, 128]])
                        S.add('sp', lambda e, Hx=Hx, src=src: e.dma_start(
                            out=Hx[:, :].rearrange("p (h q) -> p h q", h=4), in_=src), reads=[Bp0], writes=[BHx], dma=BHx)
                        S.add('pe', lambda e, Hx=Hx: e.matmul(ps[0][:, :], lhsT=Jb[:], rhs=Hx[:, :], start=True, stop=True),
                              reads=[BJ, BHx], writes=[Bps[0]])
                        S.add('act', lambda e, idx=idx: e.activation(out=bias_t[:, idx, :], in_=ps[0][:, :], func=AF.Exp), reads=[Bps[0]], writes=[Bbias])
                        yield
            for g in range(2):
                src = bass.AP(bt, (8 + 4 * g) * 1024 + 512 - 383, [[16, 32], [1024, 4], [1, 128]])
                S.add('sp', lambda e, g=g, src=src: e.dma_start(out=bandx[0:32, g, :].rearrange("p (h q) -> p h q", h=4), in_=src),
                      reads=[Bp0], writes=[Bbias], dma=Bbias)
            yield

        Bpre = {n: Buf('pre_' + n) for n in ('w1', 'wq', 'wup', 'wo', 'wfi', 'wfo')}
        kvs = ExitStack()
        KCr = sbuf(kvs, "KCr", [64, 2, 2, 16, 260], BF)
        BKCr = Buf("KCr")
        p1w = ExitStack()
        wkf_t = sbuf(p1w, "wkf_t", [128, 8, 640], BF)
        wv_t = sbuf(p1w, "wv_t", [128, 8, 384], BF)
        Bw1p = [Buf("w1p%d" % k) for k in range(2)]
        xts1 = Rot([(sbuf(p1w, "xt%d" % i, [128, 1024], F32), Buf("xt%d" % i)) for i in range(2)])
        hTs = Rot([(sbuf(p1w, "hTG%d" % i, [128, 8, 512], BF), Buf("hTG%d" % i)) for i in range(2)])
        stg0 = Rot([(xts1.items[0][0][:, :], xts1.items[0][1]), (xts1.items[1][0][:, :], xts1.items[1][1])] +
                   [(hTs.items[i][0][:, 4 * j:4 * j + 4, :].rearrange("p a b -> p (a b)").bitcast(F32), hTs.items[i][1]) for i in range(2) for j in range(2)])
        with ExitStack() as p0:
            identf = sbuf(p1w, "identf", [128, 128], F32)
            sk = sbuf(p1w, "sk", [128, 8], F32)
            t31 = sbuf(p1w, "t31", [128, 8], F32)
            Bp0 = Buf("p0")
            Bd = [Buf("c%d" % i) for i in range(12)]
            for k in range(8):
                load_cast(wkf_t[:, k, :], w_kf[k * 128:(k + 1) * 128, :], stg0, Bw1p[k // 4], engs=(('dve',) if k < 4 else ('act',)))
                load_cast(wv_t[:, k, :], w_v[k * 128:(k + 1) * 128, :], stg0, Bw1p[k // 4], engs=(('dve',) if k < 4 else ('act',)))
            S.add('pool', lambda e: e.memset(identf[:], 0.0), writes=[Bconst])
            S.add('pool', lambda e: e.affine_select(out=identf[:], in_=identf[:], pattern=[[-1, 128]], compare_op=ALU.not_equal,
                                                    fill=1.0, base=0, channel_multiplier=1), reads=[Bconst], writes=[Bconst])
            S.add('dve', lambda e: e.tensor_copy(out=ident[:], in_=identf[:]), reads=[Bconst], writes=[Bconst])
            S.add('sp', lambda e: e.dma_start(out=gam_mix[:], in_=gam[0:1, :].rearrange("a n -> (a n)").partition_broadcast(128)),
                  writes=[Bgam], dma=Bgam)
            S.add('sp', lambda e: e.dma_start(out=keymask_t[:], in_=keymask), writes=[Bd[0]], dma=Bd[0])
            S.add('sp', lambda e: e.dma_start(out=cmask_t[:], in_=cmask), writes=[Bd[1]], dma=Bd[1])
            S.add('sp', lambda e: e.dma_start(out=farlow_t[:], in_=farlow), writes=[Bd[4]], dma=Bd[4])
            S.add('act', lambda e: e.activation(out=farlow_t[:], in_=farlow_t[:], func=AF.Exp), reads=[Bd[4]], writes=[Bconst])
            for g in range(2):
                S.add('sp', lambda e, g=g: e.dma_start(out=KS[64:128, g, :], in_=emat), writes=[Bd[6 + g]], dma=Bd[6 + g])
            S.add('sp', lambda e: e.dma_start(out=sk[:], in_=sinks.rearrange("a n -> (a n)").partition_broadcast(128)),
                  writes=[Bd[11]], dma=Bd[11])
            S.add('sp', lambda e: e.dma_start(out=t31[:], in_=table[31:32, 0:8].rearrange("a n -> (a n)").partition_broadcast(128)),
                  writes=[Bd[11]], dma=Bd[11])
            S.add('dve', lambda e: e.tensor_sub(out=sk[:], in0=sk[:], in1=t31[:]), reads=[Bd[11]], writes=[Bp0])
            S.add('act', lambda e: e.activation(out=sinkexp[:], in_=sk[:], func=AF.Exp), reads=[Bp0], writes=[Bconst])
            S.add('pool', lambda e: e.memset(bandx[32:64, :, :], 0.0), writes=[Bconst])
            S.add('pool', lambda e: e.memset(bandx[64:128, :, :], 0.0), writes=[Bconst])
            S.add('pool', lambda e: e.memset(bandx[32:33, :, :], NEGM), writes=[Bconst])
            S.add('pool', lambda e: e.memset(Vx[:, :, :, :, 64:65], 1.0), writes=[BVx])
            S.add('pool', lambda e: e.memset(VCx[:, :, :, 64:65], 1.0), writes=[BVCx])
            for g in range(2):
                S.add('pool', lambda e, g=g: e.dma_start(out=VCx[:, :, g, 65:129], in_=overlap), writes=[BVCx], dma=BVCx)

        with ExitStack() as p1:
            Bw = Bw1p
            for k in range(8):
                S.add('pool', lambda e, k=k: e.dma_start(out=wg_t[:, k, :], in_=w_g[k * 128:(k + 1) * 128, 0:2048]), writes=[Bwg], dma=Bwg)
            S.add('pool', lambda e: e.memset(KCr[:, :, :, :, 256:260], 0.0), writes=[BKCr])
            Bpad = Buf("kpad")
            S.add('pool', lambda e: e.memset(KA[64:128, :, :], 0.0), writes=[Bpad])
            S.add('pool', lambda e: e.memset(KW[64:128, :, :], 0.0), writes=[Bpad])
            S.add('pool', lambda e: e.memset(KCT[64:128, :, :], 0.0), writes=[Bpad])
            bgq = []
            for kd in range(2):
                for hf in range(2):
                    bgq.append((lambda e, kd=kd, hf=hf: e.dma_start(out=w1b[kd, :, hf * 16:(hf + 1) * 16, :], in_=w1[kd, hf * 1024:(hf + 1) * 1024, :].rearrange("(l d) n -> d l n", d=64)), 'w1'))
            for hf in range(2):
                bgq.append((lambda e, hf=hf: e.dma_start(out=wqb[hf * 512:(hf + 1) * 512, :], in_=w_q[hf * 512:(hf + 1) * 512, :]), 'wq'))

            def issue_bg(n):
                for _ in range(n):
                    if bgq:
                        fn, nm = bgq.pop(0)
                        S.add('pool', fn, writes=[Bpre[nm]], dma=Bpre[nm], bg=True, nowaw=True)
            xts = xts1
            tmps = Rot([mk_tmp(p1, "a%d" % i) for i in range(4)])
            fb = Rot([1, 2])
            tb = Rot([0, 4])
            vb = Rot([3, 5])
            normed = {}

            def stageN(G):
                lst = []
                for t in range(4):
                    p = G * 4 + t
                    xt, Bx = xts.next()
                    S.add('sp', lambda e, xt=xt, p=p: e.dma_start(out=xt[:], in_=xk[p * 128:(p + 1) * 128, :]), writes=[Bx], dma=Bx)
                    tmp = tmps.next()
                    yield from norm_h_g(xt[:], Bx, gam_mix[:], tmp)
                    lst.append(tmp)
                normed[G] = lst

            def stageP(G):
                issue_bg(1)
                hTG, BhTG = hTs.next()
                lst = normed.pop(G)
                for t in range(4):
                    p = G * 4 + t
                    tmp = lst[t]
                    yield from trans_g(tmp[2], tmp[3], hTG[:, :, t * 128:(t + 1) * 128], BhTG, tb.next(), eng=('act' if t % 2 == 0 else 'dve'))
                for t in range(4):
                    p = G * 4 + t
                    b3 = vb.next()
                    for k in range(8):
                        S.add('pe', lambda e, k=k, t=t, b3=b3: e.matmul(ps[b3][:, 0:384], lhsT=hTG[:, k, t * 128:(t + 1) * 128],
                                                                        rhs=wv_t[:, k, :], start=(k == 0), stop=(k == 7)),
                              reads=[BhTG, Bw[k // 4]], writes=[Bps[b3]])
                        if k % 4 == 3:
                            yield
                    S.add('act', lambda e, p=p, b3=b3: e.copy(out=Vx[:, p, :, :, 0:64],
                                                              in_=ps[b3][:, 0:384].rearrange("p (a g d) -> p a g d", a=3, g=2)),
                          reads=[Bps[b3]], writes=[BVx])
                    yield
                for kind in range(5):
                    for g in range(2):
                        b = fb.next()
                        c0 = kind * 128 + g * 64
                        for k in range(8):
                            S.add('pe', lambda e, k=k, b=b, c0=c0: e.matmul(ps[b][0:64, :], lhsT=wkf_t[:, k, c0:c0 + 64],
                                                                            rhs=hTG[:, k, :], start=(k == 0), stop=(k == 7)),
                                  reads=[BhTG, Bw[k // 4]], writes=[Bps[b]])
                            if k % 4 == 3:
                                yield
                        src = ps[b][0:64, :]
                        if kind == 0:
                            dst, Bdst = KA[0:64, g, G * 512:(G + 1) * 512], BKA
                        elif kind == 1:
                            dst, Bdst = KS[0:64, g, G * 512:(G + 1) * 512], BKS
                        elif kind == 2:
                            dst, Bdst = KW[0:64, g, G * 512:(G + 1) * 512], BKW
                        else:
                            dst, Bdst = KCr[:, kind - 3, g, :, G * 32:(G + 1) * 32].rearrange("p r n -> p n r"), BKCr
                            src = ps[b][0:64, :].rearrange("p (n r) -> p n r", r=16)
                        if (kind + g) % 2 == 0:
                            S.add('act', lambda e, src=src, dst=dst: e.copy(out=dst, in_=src), reads=[Bps[b]], writes=[Bdst])
                        else:
                            S.add('dve', lambda e, src=src, dst=dst: e.tensor_copy(out=dst, in_=src), reads=[Bps[b]], writes=[Bdst])
                        yield

            zipgens([stageN(0)])
            for G in range(8):
                zipgens([stageP(G), stageN(G + 1) if G + 1 < 8 else None])
            S.emit()
        p1w.close()

        with ExitStack() as pc:
            w1_t = sbuf(pc, "w1_t", [64, 2, 32, 256], BF)
            w2_t = sbuf(pc, "w2_t", [128, 2, 2, 64], BF)
            pos_t = sbuf(pc, "pos_t", [64, 2, 32], BF)
            hb = sbuf(pc, "hb", [128, 4], F32)
            Bw = Buf("wc")
            Bhb = Buf("hb")
            Bw1c = [[Buf("w1c%d_%d" % (kd, l4)) for l4 in range(2)] for kd in range(2)]
            for kd in range(2):
                for l4 in range(8):
                    S.add('sp', lambda e, kd=kd, l4=l4: e.dma_start(
                        out=w1_t[:, kd, l4 * 4:(l4 + 1) * 4, :], in_=w1b[kd, :, l4 * 4:(l4 + 1) * 4, :]),
                        reads=[Bpre['w1']], writes=[Bw1c[kd][l4 // 4]], dma=Bw1c[kd][l4 // 4], nowaw=True)
                S.add('pool', lambda e, kd=kd: e.dma_start(out=w2_t[:, kd, :, :], in_=w2[kd].rearrange("(c p) n -> p c n", p=128)),
                      writes=[Bw], dma=Bw)
                S.add('pool', lambda e, kd=kd: e.dma_start(out=pos_t[:, kd, :], in_=posT[kd]), writes=[Bw], dma=Bw)
            def compress_g():
                for kd in range(2):
                    for hc in range(2):
                        col = kd * 2 + hc
                        for l in range(32):
                            S.add('pe', lambda e, kd=kd, hc=hc, l=l, col=col: e.matmul(
                                ps[4][:, col:col + 1], lhsT=w1_t[:, kd, l, hc * 128:(hc + 1) * 128], rhs=pos_t[:, kd, l:l + 1],
                                start=(l == 0), stop=(l == 31), skip_group_check=True), reads=[Bw, Bw1c[kd][l // 16]], writes=[Bps[4]])
                S.add('dve', lambda e: e.tensor_copy(out=hb[:], in_=ps[4][:, 0:4]), reads=[Bps[4]], writes=[Bhb])
                yield
                gt = Rot([(sbuf(pc, "gx%d" % i, [128, 256], F32), sbuf(pc, "gu%d" % i, [128, 256], F32), Buf("gt%d" % i)) for i in range(2)])
                gel = Rot([(sbuf(pc, "gel%d" % i, [128, 2, 256], BF), Buf("gel%d" % i)) for i in range(2)])
                hbk = Rot([1, 2])
                for kd in range(2):
                    for g in range(2):
                        ge, Bge = gel.next()
                        for hc in range(2):
                            b = hbk.next()
                            col = kd * 2 + hc
                            for l in range(32):
                                rhs = KCr[:, kd, g, l % 16, (l // 16):(l // 16) + 256]
                                S.add('pe', lambda e, kd=kd, hc=hc, l=l, b=b, rhs=rhs: e.matmul(
                                    ps[b][:, 0:256], lhsT=w1_t[:, kd, l, hc * 128:(hc + 1) * 128], rhs=rhs,
                                    start=(l == 0), stop=(l == 31)), reads=[BKCr, Bw1c[kd][l // 16]], writes=[Bps[b]])
                                if l % 8 == 7:
                                    yield
                            gx, gu, Bg = gt.next()
                            S.add('dve', lambda e, b=b, gx=gx, col=col: e.tensor_scalar(out=gx[:], in0=ps[b][:, 0:256], scalar1=hb[:, col:col + 1],
                                                                                        scalar2=None, op0=ALU.add), reads=[Bps[b], Bhb], writes=[Bg])
                            S.add('act', lambda e, gx=gx, gu=gu: e.activation(out=gu[:], in_=gx[:], func=AF.Square), reads=[Bg], writes=[Bg])
                            S.add('dve', lambda e, gu=gu: e.tensor_scalar(out=gu[:], in0=gu[:], scalar1=0.044715, scalar2=1.0,
                                                                          op0=ALU.mult, op1=ALU.add), reads=[Bg], writes=[Bg])
                            S.add('dve', lambda e, gx=gx, gu=gu: e.tensor_mul(out=gu[:], in0=gu[:], in1=gx[:]), reads=[Bg], writes=[Bg])
                            S.add('act', lambda e, gu=gu: e.activation(out=gu[:], in_=gu[:], func=AF.Sigmoid, scale=1.5957691216057308),
                                  reads=[Bg], writes=[Bg])
                            S.add('dve', lambda e, gx=gx, gu=gu, ge=ge, hc=hc: e.tensor_mul(out=ge[:, hc, :], in0=gu[:], in1=gx[:]),
                                  reads=[Bg], writes=[Bge])
                            yield
                        if kd == 0:
                            for hc in range(2):
                                S.add('pe', lambda e, hc=hc, ge=ge: e.matmul(ps[5][0:64, 0:256], lhsT=w2_t[:, 0, hc, :], rhs=ge[:, hc, :],
                                                                             start=(hc == 0), stop=(hc == 1)), reads=[Bge, Bw], writes=[Bps[5]])
                            S.add('act', lambda e, g=g: e.copy(out=KCT[0:64, g, :], in_=ps[5][0:64, 0:256]), reads=[Bps[5]], writes=[BKCT])
                        else:
                            for ct in range(2):
                                for hc in range(2):
                                    S.add('pe', lambda e, hc=hc, ct=ct, ge=ge: e.matmul(
                                        ps[6][:, ct * 64:(ct + 1) * 64], lhsT=ge[:, hc, ct * 128:(ct + 1) * 128], rhs=w2_t[:, 1, hc, :],
                                        start=(hc == 0), stop=(hc == 1), skip_group_check=True), reads=[Bge, Bw], writes=[Bps[6]])
                            S.add('act', lambda e, g=g: e.copy(out=VCx[:, :, g, 0:64], in_=ps[6][:, 0:128].rearrange("p (c d) -> p c d", c=2)),
                                  reads=[Bps[6]], writes=[BVCx])

            zipgens([compress_g(), bias_chain(pc)])
            S.emit()
        kvs.close()

        with ExitStack() as pa:
            wq_t = sbuf(pa, "wq_t", [128, 8, 1024], BF)
            xts = Rot([(sbuf(pa, "xq%d" % i, [128, 1024], F32), Buf("xq%d" % i)) for i in range(2)])
            stga = xts
            scoreadd_t = sbuf(pa, "scoreadd_t", [128, NQ, 64], F32)
            allowed_t = sbuf(pa, "allowed_t", [128, NQ, 64], F32)
            Bsa = Buf("scoreadd")
            shift_t = sbuf(pa, "shift_t", [128, NQ, 256], BF)
            S.add('dve', lambda e: e.memset(shift_t[32:64, :, :], 0.0), writes=[Bsa])
            S.add('dve', lambda e: e.memset(shift_t[64:128, :, :], 0.0), writes=[Bsa])
            S.add('sp', lambda e: e.dma_start(out=shift_t[0:33, :, :], in_=shiftext), writes=[Bsa], dma=Bsa)
            S.add('sp', lambda e: e.dma_start(out=scoreadd_t[:], in_=scoreadd), writes=[Bsa], dma=Bsa)
            S.add('sp', lambda e: e.dma_start(out=allowed_t[:], in_=allowed), writes=[Bsa], dma=Bsa)
            wgn_t = sbuf(pa, "wgn_t", [128, 8, 24], BF)
            Bw = Buf("wa")
            Bwq = [Buf("wq%d" % k) for k in range(2)]
            bgq2 = []
            bgq2.append((lambda e: e.dma_start(out=wupab, in_=w_upa), 'wup'))
            bgq2.append((lambda e: e.dma_start(out=wupbb, in_=w_upb), 'wup'))
            for hf in range(2):
                bgq2.append((lambda e, hf=hf: e.dma_start(out=wob[hf * 512:(hf + 1) * 512, :], in_=w_out[hf * 512:(hf + 1) * 512, :]), 'wo'))
            for rb in range(8):
                bgq2.append((lambda e, rb=rb: e.dma_start(out=wfib[rb * 128:(rb + 1) * 128, :].rearrange("r (a b) -> r a b", b=1408),
                                                          in_=w_fi[rb * 128:(rb + 1) * 128, :].rearrange("r (a b) -> r a b", b=1408)), 'wfi'))
            for rb in range(4):
                bgq2.append((lambda e, rb=rb: e.dma_start(out=wfob[rb * 704:(rb + 1) * 704, :], in_=w_fo[rb * 704:(rb + 1) * 704, :]), 'wfo'))

            def issue_bg2(n):
                for _ in range(n):
                    if bgq2:
                        fn, nm = bgq2.pop(0)
                        S.add('pool', fn, writes=[Bpre[nm]], dma=Bpre[nm], bg=True, nowaw=True)
            for k in range(8):
                S.add('sp', lambda e, k=k: e.dma_start(out=wq_t[:, k, :], in_=wqb[k * 128:(k + 1) * 128, :]), reads=[Bpre['wq']], writes=[Bwq[k // 4]], dma=Bwq[k // 4], nowaw=True)
            S.add('pool', lambda e: e.dma_start(out=wgn_t[:], in_=w_g[:, 2048:2072].rearrange("(k p) n -> p k n", p=128)), writes=[Bw], dma=Bw)
            tmps = Rot([mk_tmp(pa, "b%d" % i) for i in range(1)])
            hTq = Rot([(sbuf(pa, "hTq%d" % i, [128, 8, 128], BF), Buf("hTq%d" % i)) for i in range(2)])
            QAs = Rot([(sbuf(pa, "QA%d" % i, [128, 2, 512], BF), Buf("QA%d" % i)) for i in range(2)])
            for i_ in range(2):
                S.add('pool', lambda e, i_=i_: e.memset(QAs.items[i_][0][64:128, :, :], 0.0), writes=[QAs.items[i_][1]])
            QSs = Rot([(sbuf(pa, "QS%d" % i, [128, 2, 512], BF), Buf("QSlo%d" % i), [Buf("QShi%d_%d" % (i, g)) for g in range(2)])
                       for i in range(2)])
            for i_ in range(2):
                S.add('pool', lambda e, i_=i_: e.memset(QSs.items[i_][0][64:128, :, :], 0.0), writes=QSs.items[i_][2])
            gns = Rot([(sbuf(pa, "gn%d" % i, [128, 24], F32), Buf("gn%d" % i)) for i in range(2)])
            negs = Rot([(sbuf(pa, "negs%d" % i, [128, 128], BF), Buf("negs%d" % i)) for i in range(2)])
            for i in range(2):
                S.add('pool', lambda e, i=i: e.memset(negs.items[i][0][:], 0.0), writes=[negs.items[i][1]])
            Pts = Rot([(sbuf(pa, "Pt%d" % i, [128, 512], BF), Buf("Pt%d" % i)) for i in range(4)])
            sbank = Rot([0, 1, 6])
            abank = Rot([2, 3, 4, 5])
            ybf = Rot([(sbuf(pa, "ybf%d" % i, [128, 512], F32), Buf("ybf%d" % i)) for i in range(1)])
            caccs = Rot([(sbuf(pa, "cacc%d" % i, [128, 4, 129], F32), Buf("cacc%d" % i)) for i in range(2)])
            yab = Rot([(sbuf(pa, "yab%d" % i, [128, 1024], BF), Buf("yab%d" % i)) for i in range(2)])
            sm = Rot([(sbuf(pa, "smA%d" % i, [128, 8], F32), sbuf(pa, "smB%d" % i, [128, 8], F32),
                       sbuf(pa, "smT%d" % i, [128, 4, 64], F32), Buf("sm%d" % i)) for i in range(4)])
            tk = Rot([(sbuf(pa, "imp%d" % i, [128, 64], F32), sbuf(pa, "sc%d" % i, [128, 64], F32), sbuf(pa, "wk%d" % i, [128, 64], F32),
                       sbuf(pa, "m8a%d" % i, [128, 8], F32), sbuf(pa, "m8b%d" % i, [128, 8], F32), Buf("tk%d" % i)) for i in range(2)])
            dbgt = Rot([(sbuf(pa, "dbgt%d" % i, [128, 1024], F32), Buf("dbgt%d" % i)) for i in range(2)]) if debug else None

            prepped = {}

            Qtok = Rot([(sbuf(pa, "Qtok%d" % i, [128, 1024], BF), Buf("Qtok%d" % i)) for i in range(2)])
            gtmp = Rot([(sbuf(pa, "gtmp%d" % i, [128, 24], F32), Buf("gtmp%d" % i)) for i in range(2)])

            xloaded = {}

            def prep_load(i):
                I = 2 * i + 1
                xt, Bx = xts.next()
                S.add('sp', lambda e: e.dma_start(out=xt[:], in_=xk[I * 128:(I + 1) * 128, :]), writes=[Bx], dma=Bx)
                xloaded[i] = (xt, Bx)

            def prep_g(i):
                I = 2 * i + 1
                if i not in xloaded:
                    prep_load(i)
                xt, Bx = xloaded.pop(i)
                hT, BhT = hTq.next()
                tmp = tmps.next()
                yield from norm_h_g(xt[:], Bx, gam_mix[:], tmp, lnexp=True)
                yield from trans_g(tmp[2], tmp[3], hT[:], BhT, 7, eng='dve')
                QA, BQA = QAs.next()
                QS, BQSlo, BQShi = QSs.next()
                gn, Bgn = gns.next()
                Qt, BQt = Qtok.next()
                for m in range(2):
                    for k in range(8):
                        S.add('pe', lambda e, k=k, m=m: e.matmul(ps[7][:, :], lhsT=hT[:, k, :], rhs=wq_t[:, k, m * 512:(m + 1) * 512],
                                                                 start=(k == 0), stop=(k == 7)), reads=[BhT, Bwq[k // 4]], writes=[Bps[7]])
                        if k % 2 == 1:
                            yield
                    S.add('dve', lambda e, m=m: e.tensor_scalar(out=Qt[:, m * 512:(m + 1) * 512], in0=ps[7][:, :], scalar1=0.125, scalar2=None,
                                                                op0=ALU.mult), reads=[Bps[7]], writes=[BQt])
                    yield
                for k in range(8):
                    S.add('pe', lambda e, k=k: e.matmul(ps[7][:, 0:24], lhsT=hT[:, k, :], rhs=wgn_t[:, k, :], start=(k == 0), stop=(k == 7)),
                          reads=[BhT, Bw], writes=[Bps[7]])
                yield
                gt_, Bgt = gtmp.next()
                S.add('act', lambda e: e.activation(out=gt_[:], in_=ps[7][:, 0:24], func=AF.Exp, scale=-1.0), reads=[Bps[7]], writes=[Bgt])
                yield
                S.add('dve', lambda e: e.tensor_scalar(out=gt_[:], in0=gt_[:], scalar1=1.0, scalar2=None, op0=ALU.add), reads=[Bgt], writes=[Bgt])
                S.add('dve', lambda e: e.reciprocal(out=gn[:], in_=gt_[:]), reads=[Bgt], writes=[Bgn])
                yield
                pT = ps[7][:].bitcast(BF)
                for m in range(2):
                    for hh in range(8):
                        S.add('pe', lambda e, m=m, hh=hh: e.transpose(out=pT[0:64, hh * 128:(hh + 1) * 128],
                                                                      in_=Qt[:, m * 512 + hh * 64:m * 512 + (hh + 1) * 64], identity=ident[:]),
                              reads=[BQt, Bconst], writes=[Bps[7]])
                        if hh % 4 == 3:
                            yield
                    for g in range(2):
                        dst, Bdst = (QA[0:64, g, :], BQA) if m == 0 else (QS[0:64, g, :], BQSlo)
                        S.add('dve', lambda e, dst=dst, g=g: e.tensor_copy(out=dst, in_=pT[0:64, g * 512:(g + 1) * 512]), reads=[Bps[7]], writes=[Bdst])
                        yield
                prepped[i] = (QA, BQA, QS, BQSlo, BQShi, gn, Bgn)

            def run_steps_g(steps, dyn=None):
                n = len(steps)
                banks = [sbank.next() for _ in range(n)]

                def qk(j):
                    stp = steps[j]
                    b = banks[j]
                    l, r, rd = stp['qk']
                    has_m = stp['mask'] is not None and stp['mask'][0] == 'pe'
                    S.add('pe', lambda e: e.matmul(ps[b][:, :], lhsT=l, rhs=r, start=True, stop=not has_m), reads=rd, writes=[Bps[b]])
                    if has_m:
                        _, l2, r2, rd2 = stp['mask']
                        S.add('pe', lambda e: e.matmul(ps[b][:, :], lhsT=l2, rhs=r2, start=False, stop=True), reads=rd2, writes=[Bps[b]])
                qk(0)
                if n > 1:
                    qk(1)
                for j in range(n):
                    if j + 2 < n:
                        qk(j + 2)
                    stp = steps[j]
                    b = banks[j]
                    Pt, BPt = Pts.next()
                    if stp['abias'] is None:
                        S.add('act', lambda e, b=b, Pt=Pt: e.activation(out=Pt[:], in_=ps[b][:, :], func=AF.Exp),
                              reads=[Bps[b]], writes=[BPt])
                    else:
                        S.add('act', lambda e, b=b, Pt=Pt, stp=stp: e.activation(out=Pt[:], in_=ps[b][:, :], func=AF.Exp, bias=stp['abias']),
                              reads=[Bps[b], Bconst], writes=[BPt])
                    if stp['mask'] is not None and stp['mask'][0] == 'mul':
                        _, map_, mrd = stp['mask']
                        S.add('dve', lambda e, Pt=Pt, map_=map_: e.tensor_mul(out=Pt[:], in0=Pt[:], in1=map_), reads=[BPt] + mrd, writes=[BPt])
                    for h in range(4):
                        acc_ap, vr = stp['v'][h]
                        S.add('pe', lambda e, h=h, acc_ap=acc_ap, vr=vr, Pt=Pt, stp=stp: e.matmul(
                            acc_ap, lhsT=Pt[:, h * 128:(h + 1) * 128], rhs=vr, start=stp['first'][h], stop=stp['last'],
                            skip_group_check=True), reads=[BPt] + stp['vreads'], writes=stp['accB'])
                    if stp['post'] is not None:
                        r_ = stp['post']()
                        if r_ is not None:
                            if dyn is not None:
                                dyn.append(r_)
                            else:
                                for _ in r_:
                                    pass
                    yield

            def run_steps(steps):
                for _ in run_steps_g(steps):
                    pass

            def do_tile(i):
                I = 2 * i + 1
                if i == 0:
                    zipgens([prep_g(0)])
                QA, BQA, QS, BQSlo, BQShi, gn, Bgn = prepped.pop(i)
                ya_bf, Bya = yab.next()
                yb, Byb = ybf.next()
                steps = []
                dyn = []
                for g in range(2):
                    bX, bY = abank.next(), abank.next()
                    accs = [ps[bX][:, 0:129], ps[bX][:, 129:258], ps[bY][:, 0:129], ps[bY][:, 129:258]]

                    def post_cmp(g=g, bX=bX, bY=bY):
                        smA, smB, smT, Bsm = sm.next()
                        imp, sc, wk, m8a, m8b, Btk = tk.next()
                        ca, Bca = caccs.next()
                        for pr, bb in enumerate((bX, bY)):
                            S.add('dve', lambda e, pr=pr, bb=bb: e.tensor_copy(out=ca[:, pr * 2:pr * 2 + 2, :],
                                                                               in_=ps[bb][:, 0:258].rearrange("p (h c) -> p h c", c=129)),
                                  reads=[Bps[bb]], writes=[Bca])

                        def cmp_rest_g():
                            S.add('dve', lambda e: e.tensor_scalar(out=smA[:, 0:4], in0=ca[:, :, 64], scalar1=1e-30, scalar2=None, op0=ALU.max),
                                  reads=[Bca], writes=[Bsm])
                            yield
                            S.add('dve', lambda e: e.reciprocal(out=smA[:, 0:4], in_=smA[:, 0:4]), reads=[Bsm], writes=[Bsm])
                            yield
                            gsl = gn[:, g * 12:(g + 1) * 12].rearrange("p (h b) -> p h b", b=3)[:, :, 0]
                            S.add('dve', lambda e: e.tensor_mul(out=smB[:, 0:4], in0=smA[:, 0:4], in1=gsl), reads=[Bsm, Bgn], writes=[Bsm])
                            yield
                            S.add('dve', lambda e: e.tensor_tensor(
                                out=yb[:, g * 256:(g + 1) * 256].rearrange("p (h d) -> p h d", d=64),
                                in0=ca[:, :, 0:64], in1=bc_last(smB[:, 0:4], 64), op=ALU.mult), reads=[Bca, Bsm], writes=[Byb])
                            yield
                            for h in range(4):
                                if h == 0:
                                    S.add('dve', lambda e, h=h: e.tensor_scalar(out=imp[:], in0=ca[:, h, 65:129], scalar1=smA[:, h:h + 1],
                                                                                scalar2=None, op0=ALU.mult), reads=[Bca, Bsm], writes=[Btk])
                                else:
                                    S.add('dve', lambda e, h=h: e.scalar_tensor_tensor(out=imp[:], in0=ca[:, h, 65:129], scalar=smA[:, h:h + 1],
                                                                                       in1=imp[:], op0=ALU.mult, op1=ALU.add),
                                          reads=[Bca, Bsm, Btk], writes=[Btk])
                                yield
                            yield from topk_g()

                        def topk_g():
                            S.add('dve', lambda e: e.tensor_add(out=sc[:], in0=imp[:], in1=scoreadd_t[:, i, :]), reads=[Btk, Bsa], writes=[Btk])
                            yield
                            S.add('dve', lambda e: e.max(out=m8a[:], in_=sc[:]), reads=[Btk], writes=[Btk])
                            yield
                            S.add('dve', lambda e: e.match_replace(out=wk[:], in_to_replace=m8a[:], in_values=sc[:], imm_value=-3.0e38),
                                  reads=[Btk], writes=[Btk])
                            yield
                            S.add('dve', lambda e: e.max(out=m8b[:], in_=wk[:]), reads=[Btk], writes=[Btk])
                            yield
                            S.add('dve', lambda e: e.tensor_scalar(out=wk[:], in0=sc[:], scalar1=m8b[:, 7:8], scalar2=None, op0=ALU.is_ge),
                                  reads=[Btk], writes=[Btk])
                            yield
                            S.add('dve', lambda e: e.tensor_mul(out=wk[:], in0=wk[:], in1=allowed_t[:, i, :]), reads=[Btk, Bsa], writes=[Btk])
                            yield
                            ng, Bng = negs.next()
                            S.add('dve', lambda e: e.tensor_scalar(out=ng[:, 64:128], in0=wk[:], scalar1=-1.0, scalar2=-NEGM, op0=ALU.add, op1=ALU.mult),
                                  reads=[Btk], writes=[Bng])
                            yield
                            pT = ps[7][:].bitcast(BF)
                            S.add('pe', lambda e: e.transpose(out=pT[:, 0:128], in_=ng[:], identity=ident[:]), reads=[Bng, Bconst], writes=[Bps[7]])
                            yield
                            src = pT[64:128, 0:128]
                            srcb = bass.AP(src.tensor, src.offset, [list(src.ap[0]), [0, 4], list(src.ap[1])])
                            S.add('dve', lambda e: e.tensor_copy(out=QS[64:128, g, :].rearrange("p (h q) -> p h q", h=4), in_=srcb),
                                  reads=[Bps[7]], writes=[BQShi[g]])
                            yield

                        dyn.append(cmp_rest_g())
                    cts = [0, 1] if 8 * I + 6 >= 128 else [0]
                    for ct in cts:
                        steps.append(dict(
                            qk=(KCT[:, g, ct * 128:(ct + 1) * 128], QS[:, g, :], [BKCT, BQSlo, BQShi[g]]),
                            mask=('pe', shift_t[:, i, ct * 128:(ct + 1) * 128], bandx[:, g, :], [Bsa, Bbias]),
                            abias=None,
                            v=[(accs[h], VCx[:, ct, g, :]) for h in range(4)], vreads=[BVCx],
                            accB=[Bps[bX], Bps[bY]], first=[ct == 0 and h in (0, 2) for h in range(4)], last=(ct == cts[-1]),
                            post=post_cmp if ct == cts[-1] else None))

                def std_branch(g, Js, klhs, Bk, qrhs, Bq, K, mixer, vkind, gate_br, is_swa, is_slc, first_yb, last_yb):
                    bA = abank.next()
                    a3 = ps[bA][:, 0:260].rearrange("p (h c) -> p h c", c=65)

                    def post():
                        smA, smB, smT, Bsm = sm.next()
                        if is_swa:
                            S.add('dve', lambda e: e.tensor_add(out=smA[:, 0:4], in0=a3[:, :, 64], in1=sinkexp[:, g * 4:(g + 1) * 4]),
                                  reads=[Bps[bA], Bconst], writes=[Bsm])
                            yield
                            S.add('dve', lambda e: e.reciprocal(out=smB[:, 0:4], in_=smA[:, 0:4]), reads=[Bsm], writes=[Bsm])
                            yield
                            S.add('dve', lambda e: e.tensor_tensor(
                                out=ya_bf[:, g * 256:(g + 1) * 256].rearrange("p (h d) -> p h d", d=64),
                                in0=a3[:, :, 0:64], in1=bc_last(smB[:, 0:4], 64), op=ALU.mult), reads=[Bps[bA], Bsm], writes=[Bya])
                            yield
                            return
                        S.add('dve', lambda e: e.reciprocal(out=smA[:, 0:4], in_=a3[:, :, 64]), reads=[Bps[bA]], writes=[Bsm])
                        yield
                        gsl = gn[:, g * 12:(g + 1) * 12].rearrange("p (h b) -> p h b", b=3)[:, :, gate_br]
                        S.add('dve', lambda e: e.tensor_mul(out=smB[:, 0:4], in0=smA[:, 0:4], in1=gsl), reads=[Bsm, Bgn], writes=[Bsm])
                        yield
                        S.add('dve', lambda e: e.tensor_tensor(out=smT[:], in0=a3[:, :, 0:64], in1=bc_last(smB[:, 0:4], 64), op=ALU.mult),
                              reads=[Bps[bA], Bsm], writes=[Bsm])
                        yield
                        ybg = yb[:, g * 256:(g + 1) * 256].rearrange("p (h d) -> p h d", d=64)
                        if last_yb:
                            S.add('pool', lambda e: e.tensor_add(
                                out=ya_bf[:, 512 + g * 256:512 + (g + 1) * 256].rearrange("p (h d) -> p h d", d=64), in0=ybg, in1=smT[:]),
                                reads=[Byb, Bsm], writes=[Bya])
                        else:
                            S.add('pool', lambda e: e.tensor_add(out=ybg, in0=ybg, in1=smT[:]), reads=[Byb, Bsm], writes=[Byb])
                        yield
                    for n_, J in enumerate(Js):
                        mask = None
                        if J == I:
                            mask = ('mul', bias_t[:, mixer * 4 + 0 + g, :], [Bbias])
                        elif J == I - 1:
                            mask = ('mul', bias_t[:, mixer * 4 + 2 + g, :], [Bbias])
                        elif (not is_swa) and (not is_slc) and J == I - 4:
                            mask = ('mul', farlow_t[:], [Bconst])
                        rd = [Bk, Bq] + ([BQShi[g]] if is_slc else [])
                        steps.append(dict(
                            qk=(klhs[0:K, g, J * 128:(J + 1) * 128], qrhs[0:K, g, :], rd),
                            mask=mask, abias=(keymask_t[:, J:J + 1] if (J == 0 and not is_slc) else None),
                            v=[(a3[:, h, :], Vx[:, J, vkind, g, :]) for h in range(4)], vreads=[BVx],
                            accB=[Bps[bA]], first=[n_ == 0 and h == 0 for h in range(4)], last=(n_ == len(Js) - 1),
                            post=post if n_ == len(Js) - 1 else None))

                for g in range(2):
                    std_branch(g, list(range(max(0, I - 4), I + 1)), KW, BKW, QS, BQSlo, 128, 1, 2, 2, False, False, False, False)
                    std_branch(g, [I - 1, I], KA, BKA, QA, BQA, 128, 0, 0, 0, True, False, False, False)
                dyn.append(run_steps_g(steps, dyn))
                if i + 1 < NQ:
                    prep_load(i + 1)
                zipgens_dyn(dyn)
                issue_bg2(1)
                steps = []
                for g in range(2):
                    std_branch(g, list(range(0, I + 1)), KS, BKS, QS, BQSlo, 128, 1, 1, 1, False, True, False, True)
                dyn2 = []
                dyn2.append(run_steps_g(steps, dyn2))
                if i + 1 < NQ:
                    dyn2.append(prep_g(i + 1))
                zipgens_dyn(dyn2)
                S.add('sp', lambda e, ya_bf=ya_bf, i=i: e.dma_start(out=yabd[i * 128:(i + 1) * 128, :], in_=ya_bf[:]), reads=[Bya], dma=Bya)
                if debug:
                    dt_, Bdt = dbgt.next()
                    S.add('dve', lambda e, dt_=dt_, ya_bf=ya_bf: e.tensor_copy(out=dt_[:], in_=ya_bf[:]), reads=[Bya], writes=[Bdt])
                    S.add('sp', lambda e, dt_=dt_, i=i: e.dma_start(out=dbg['ya'][i * 128:(i + 1) * 128, :], in_=dt_[:, 0:512]), reads=[Bdt], dma=Bdt)
                    S.add('sp', lambda e, dt_=dt_, i=i: e.dma_start(out=dbg['yb'][i * 128:(i + 1) * 128, :], in_=dt_[:, 512:1024]), reads=[Bdt], dma=Bdt)
            for i_ in range(NQ):
                do_tile(i_)
            S.emit()
        att.close()

        hsend = sbuf(st, "hsend", [128, 8, NQ, 2], BF)
        Bhs = Buf("hsend")
        wfo_s = ExitStack()
        wfo_t = sbuf(wfo_s, "wfo_t", [128, 22, 1024], BF)
        Bwo_ffn = Buf("wfo")
        with ExitStack() as pb:
            wup_t = sbuf(pb, "wup_t", [128, 2, 4, 1024], BF)
            wo_t = sbuf(pb, "wo_t", [128, 8, 1024], BF)
            gam_ffn = sbuf(pb, "gam_ffn", [128, 1024], F32)
            Bwup = [Buf("wup%d" % m) for m in range(2)]
            Bwo = [Buf("wo%d" % k) for k in range(2)]
            S.add('sp', lambda e: e.dma_start(out=gam_ffn[:], in_=gam[1:2, :].rearrange("a n -> (a n)").partition_broadcast(128)),
                  writes=[Bgam], dma=Bgam)
            xts = Rot([(sbuf(pb, "xb%d" % i, [128, 1024], F32), Buf("xb%d" % i)) for i in range(4)])
            tmps = Rot([mk_tmp(pb, "c%d" % i) for i in range(3)])
            hTq = Rot([(sbuf(pb, "hTb%d" % i, [128, 8, 128], BF), Buf("hTb%d" % i)) for i in range(2)])
            sgs = Rot([(sbuf(pb, "sg%d" % i, [128, 2048], F32), Buf("sg%d" % i)) for i in range(2)])
            yabs = Rot([(sbuf(pb, "yabl%d" % i, [128, 1024], BF), Buf("yabl%d" % i)) for i in range(3)])
            yTs = Rot([(sbuf(pb, "yT%d" % i, [128, 8, 128], BF), Buf("yT%d" % i)) for i in range(2)])
            mgs = Rot([(sbuf(pb, "mg%d" % i, [128, 1024], F32), sbuf(pb, "mgt%d" % i, [128, 1024], F32),
                        sbuf(pb, "mgb%d" % i, [128, 1024], BF), Buf("mg%d" % i)) for i in range(2)])
            mTs = Rot([(sbuf(pb, "mT%d" % i, [128, 8, 128], BF), Buf("mT%d" % i)) for i in range(2)])
            x1s = Rot([(sbuf(pb, "x1_%d" % i, [128, 1024], F32), Buf("x1_%d" % i)) for i in range(2)])
            h2Ts = Rot([(sbuf(pb, "h2T%d" % i, [128, 8, 128], BF), Buf("h2T%d" % i)) for i in range(2)])
            gb = Rot([0, 1])
            ub = Rot([2, 3])
            ob = Rot([4, 5])
            def pb_weights():
                for c in range(4):
                    S.add('sp', lambda e, c=c: e.dma_start(out=wup_t[:, 0, c, :], in_=wupab[c * 128:(c + 1) * 128, :]), reads=[Bpre['wup']], writes=[Bwup[0]], dma=Bwup[0], nowaw=True)
                for c in range(4):
                    S.add('sp', lambda e, c=c: e.dma_start(out=wup_t[:, 1, c, :], in_=wupbb[c * 128:(c + 1) * 128, :]), reads=[Bpre['wup']], writes=[Bwup[1]], dma=Bwup[1], nowaw=True)
                for k in range(8):
                    S.add('sp', lambda e, k=k: e.dma_start(out=wo_t[:, k, :], in_=wob[k * 128:(k + 1) * 128, :]), reads=[Bpre['wo']], writes=[Bwo[k // 4]], dma=Bwo[k // 4], nowaw=True)

            wfoq = list(range(22))

            def issue_wfo(n):
                for _ in range(n):
                    if wfoq:
                        c = wfoq.pop(0)
                        S.add('sp', lambda e, c=c: e.dma_start(out=wfo_t[:, c, :], in_=wfob[c * 128:(c + 1) * 128, :]), reads=[Bpre['wfo']], writes=[Bwo_ffn], dma=Bwo_ffn, nowaw=True)
            stA, stB, stA1 = {}, {}, {}

            def stageA(i):
                I = 2 * i + 1
                xt, Bx = xts.next()
                S.add('sp', lambda e: e.dma_start(out=xt[:], in_=xk[I * 128:(I + 1) * 128, :]), writes=[Bx], dma=Bx)
                yl, Byl = yabs.next()
                S.add('sp', lambda e: e.dma_start(out=yl[:], in_=yabd[i * 128:(i + 1) * 128, :]), writes=[Byl], dma=Byl)
                hT, BhT = hTq.next()
                yield from norm_T_g(xt[:], Bx, gam_mix[:], hT[:], BhT, tmps.next(), 7)
                stA1[i] = (xt, Bx, yl, Byl, hT, BhT)

            def stageA2(i):
                xt, Bx, yl, Byl, hT, BhT = stA1.pop(i)
                sg, Bsg = sgs.next()
                for cc in range(4):
                    b = gb.next()
                    for k in range(8):
                        S.add('pe', lambda e, k=k, b=b, cc=cc: e.matmul(ps[b][:, :], lhsT=hT[:, k, :], rhs=wg_t[:, k, cc * 512:(cc + 1) * 512],
                                                                        start=(k == 0), stop=(k == 7)), reads=[BhT, Bwg], writes=[Bps[b]])
                    S.add('act', lambda e, b=b, cc=cc: e.activation(out=sg[:, cc * 512:(cc + 1) * 512], in_=ps[b][:, :], func=AF.Sigmoid),
                          reads=[Bps[b]], writes=[Bsg])
                    yield
                yT, ByT = yTs.next()
                pT = ps[6][:].bitcast(BF)
                for c in range(8):
                    S.add('pe', lambda e, c=c: e.transpose(out=pT[:, c * 128:(c + 1) * 128], in_=yl[:, c * 128:(c + 1) * 128], identity=ident[:]),
                          reads=[Byl, Bconst], writes=[Bps[6]])
                S.add('dve', lambda e: e.tensor_copy(out=yT[:], in_=pT[:, 0:1024].rearrange("p (k t) -> p k t", k=8)), reads=[Bps[6]], writes=[ByT])
                yield
                stA[i] = (xt, Bx, sg, Bsg, yT, ByT)

            def stageB(i):
                xt, Bx, sg, Bsg, yT, ByT = stA.pop(i)
                mg, mgt, mgb, Bmg = mgs.next()
                for m in range(2):
                    for hf in range(2):
                        b = ub.next()
                        for c in range(4):
                            S.add('pe', lambda e, c=c, b=b, m=m, hf=hf: e.matmul(ps[b][:, :], lhsT=yT[:, m * 4 + c, :],
                                                                                rhs=wup_t[:, m, c, hf * 512:(hf + 1) * 512],
                                                                                start=(c == 0), stop=(c == 3)), reads=[ByT, Bwup[m]], writes=[Bps[b]])
                        dst = mg if m == 0 else mgt
                        S.add('dve', lambda e, b=b, m=m, hf=hf, dst=dst: e.tensor_mul(out=dst[:, hf * 512:(hf + 1) * 512], in0=ps[b][:, :],
                                                                                     in1=sg[:, m * 1024 + hf * 512:m * 1024 + (hf + 1) * 512]),
                              reads=[Bps[b], Bsg], writes=[Bmg])
                        yield
                S.add('dve', lambda e: e.tensor_add(out=mgb[:], in0=mg[:], in1=mgt[:]), reads=[Bmg], writes=[Bmg])
                yield
                mT, BmT = mTs.next()
                pT7 = ps[7][:].bitcast(BF)
                for c in range(8):
                    S.add('pe', lambda e, c=c: e.transpose(out=pT7[:, c * 128:(c + 1) * 128], in_=mgb[:, c * 128:(c + 1) * 128], identity=ident[:]),
                          reads=[Bmg, Bconst], writes=[Bps[7]])
                S.add('act', lambda e: e.copy(out=mT[:], in_=pT7[:, 0:1024].rearrange("p (k t) -> p k t", k=8)), reads=[Bps[7]], writes=[BmT])
                yield
                stB[i] = (xt, Bx, mT, BmT)

            def stageC(i):
                xt, Bx, mT, BmT = stB.pop(i)
                x1, Bx1 = x1s.next()
                for hf in range(2):
                    b = ob.next()
                    for c in range(8):
                        S.add('pe', lambda e, c=c, b=b, hf=hf: e.matmul(ps[b][:, :], lhsT=mT[:, c, :], rhs=wo_t[:, c, hf * 512:(hf + 1) * 512],
                                                                        start=(c == 0), stop=(c == 7)), reads=[BmT, Bwo[c // 4]], writes=[Bps[b]])
                    S.add('dve', lambda e, b=b, hf=hf: e.tensor_add(out=x1[:, hf * 512:(hf + 1) * 512], in0=ps[b][:, :],
                                                                    in1=xt[:, hf * 512:(hf + 1) * 512]), reads=[Bps[b], Bx], writes=[Bx1])
                    yield
                S.add('pool', lambda e: e.dma_start(out=x1d[i * 128:(i + 1) * 128, :], in_=x1[:]), reads=[Bx1], dma=Bx1)
                if debug:
                    S.add('sp', lambda e: e.dma_start(out=dbg['x1'][i * 128:(i + 1) * 128, :], in_=x1[:]), reads=[Bx1], dma=Bx1)
                h2T, Bh2T = h2Ts.next()
                yield from norm_T_g(x1[:], Bx1, gam_ffn[:], h2T[:], Bh2T, tmps.next(), 6)
                S.add('pool', lambda e: e.dma_start(
                    out=h2Td.rearrange("p (k t) -> p k t", k=8)[:, :, i * 128:(i + 1) * 128], in_=h2T[:]), reads=[Bh2T], dma=Bh2T)
                S.add('act', lambda e: e.copy(out=hsend[:, :, i, :], in_=h2T[:, :, 126:128]), reads=[Bh2T], writes=[Bhs])

            for s_ in range(NQ + 3):
                zipgens([stageC(s_ - 3) if 0 <= s_ - 3 < NQ else None,
                         stageB(s_ - 2) if 0 <= s_ - 2 < NQ else None,
                         stageA2(s_ - 1) if 0 <= s_ - 1 < NQ else None,
                         stageA(s_) if s_ < NQ else None])
                if s_ == 0:
                    pb_weights()
                elif s_ >= 2:
                    issue_wfo(2)
            issue_wfo(22)
            S.emit()

        with ExitStack() as pf:
            wfi_t = sbuf(pf, "wfi_t", [128, 8, 5632], BF)
            cw_t = sbuf(pf, "cw_t", [128, 4, 44], F32)
            a_t = sbuf(pf, "a_t", [128, 1], F32)
            gam_fin = sbuf(pf, "gam_fin", [128, 1024], F32)
            hrecv = sbuf(pf, "hrecv", [128, 2, 8, NQ, 2], BF)
            hh = sbuf(pf, "hh", [128, 8, NQ, 2], BF)
            hcb = sbuf(pf, "hcb", [128, 44, NQ, 2], F32)
            sav = arena0[:, 15360:16064].bitcast(F32).rearrange("p (c t x) -> p c t x", c=44, t=4)
            hd = hcb[:, 0:8, :, :]
            Bsav = Buf("sav")
            Bwfi = [Buf("wfi%d" % c) for c in range(22)]
            Bcw, Bcin, Bcout, Bhr, Bhh = [Buf(n) for n in "cw cin cout hr hh".split()]
            Bhcb = [Buf("hcb%d" % c) for c in range(44)]
            S.add('sp', lambda e: e.dma_start(out=cin.ap(), in_=hsend[:].rearrange("p k t c -> p (k t c)")), reads=[Bhs], writes=[Bcin], dma=Bcin)
            S.add('pool', lambda e: e.collective_compute("AllGather", ALU.bypass, replica_groups=[[0, 1], [2, 3], [4, 5], [6, 7]],
                                                         ins=[cin.ap().opt()], outs=[cout.ap().opt()]), reads=[Bcin], writes=[Bcout], own_sem=True)
            S.add('sp', lambda e: e.dma_start(out=cw_t[:], in_=cwb), writes=[Bcw], dma=Bcw)
            S.add('sp', lambda e: e.dma_start(out=a_t[:], in_=asel), writes=[Bcw], dma=Bcw)
            S.add('sp', lambda e: e.dma_start(out=gam_fin[:], in_=gam[2:3, :].rearrange("a n -> (a n)").partition_broadcast(128)),
                  writes=[Bgam], dma=Bgam)
            aT = arena0[:, 0:11264].rearrange("p (c n) -> p c n", c=22)
            BaT = Buf("actT")
            hg = arena0[:, 11264:11264 + 4096].rearrange("p (k t) -> p k t", k=8)
            Bhg = Buf("h2g")
            tus = Rot([(sbuf(pf, "tu%d" % i, [128, 4, 128], F32), Buf("tu%d" % i)) for i in range(3)])
            tgs = Rot([(sbuf(pf, "tg%d" % i, [128, 4, 128], F32), Buf("tg%d" % i)) for i in range(3)])
            htm = Rot([(sbuf(pf, "htm%d" % i, [128, NQ], F32), Buf("htm%d" % i)) for i in range(2)])
            x1s = Rot([(sbuf(pf, "x1f%d" % i, [128, 1024], F32), Buf("x1f%d" % i)) for i in range(2)])
            fin = Rot([(sbuf(pf, "fs%d" % i, [128, 1], F32), sbuf(pf, "fr%d" % i, [128, 1], F32),
                        sbuf(pf, "fo%d" % i, [128, 1024], F32), Buf("fin%d" % i)) for i in range(1)])
            ubk = Rot([0, 1])
            gbk = Rot([2, 3])
            obk = Rot([4, 5])
            hbk = Rot([4, 5])
            S.add('sp', lambda e: e.dma_start(out=hg, in_=h2Td.rearrange("p (k t) -> p k t", k=8)[:, :, 0:512]), writes=[Bhg], dma=Bhg)
            Bwfi_h = [[Buf("wfi%d_%d" % (half, c2)) for c2 in range(6)] for half in range(2)]
            for c2 in range(11):
                for half in range(2):
                    c0 = (2 * c2 + 22 * half) * 128
                    S.add('sp', lambda e, c0=c0: e.dma_start(out=wfi_t[:, :, c0:c0 + 256],
                                                            in_=wfib[:, c0:c0 + 256].rearrange("(k p) n -> p k n", p=128)),
                          reads=[Bpre['wfi']], writes=[Bwfi_h[half][c2 // 2]], dma=Bwfi_h[half][c2 // 2], nowaw=True)
            S.add('sp', lambda e: e.dma_start(out=hrecv[:].rearrange("p r k t c -> p r (k t c)"),
                                              in_=cout.ap().rearrange("(r p) n -> p r n", p=128)), reads=[Bcout], writes=[Bhr], dma=Bhr)
            pend = []
            ew_eng = ['pool']
            ubk3 = Rot([0, 1, 6])
            gbk3 = Rot([2, 3, 7])

            def fin_pair(c, res):
                (tu, Btu), (tg, Btg) = res
                S.add('act', lambda e: e.activation(out=tg[:], in_=tg[:], func=AF.Silu), reads=[Btg], writes=[Btg])
                S.add(ew_eng[0], lambda e: e.tensor_mul(out=aT[:, c, :].rearrange("p (t n) -> p t n", t=4), in0=tu[:], in1=tg[:]),
                      reads=[Btu, Btg], writes=[BaT])
            def hh_compute():
                G0 = hrecv[:, 0]
                G1 = hrecv[:, 1]
                S.add('dve', lambda e: e.tensor_copy(out=hd[:, :, 0, :], in_=G0[:, :, 0, :]), reads=[Bhr], writes=[Bhh])
                S.add('dve', lambda e: e.tensor_sub(out=hd[:, :, 1:NQ, :], in0=G0[:, :, 1:NQ, :], in1=G1[:, :, 0:NQ - 1, :]), reads=[Bhr], writes=[Bhh])
                S.add('dve', lambda e: e.tensor_scalar(out=hh[:, :, 0, :], in0=hd[:, :, 0, :], scalar1=a_t[:, 0:1], scalar2=None, op0=ALU.mult),
                      reads=[Bhh, Bcw], writes=[Bhh])
                S.add('dve', lambda e: e.scalar_tensor_tensor(out=hh[:, :, 1:NQ, :], in0=hd[:, :, 1:NQ, :], scalar=a_t[:, 0:1], in1=G1[:, :, 0:NQ - 1, :],
                                                              op0=ALU.mult, op1=ALU.add), reads=[Bhh, Bcw, Bhr], writes=[Bhh])

            def halo_chain(cc):
                half, c = cc // 22, cc % 22
                hb_ = hbk.next()
                for k in range(8):
                    S.add('pe', lambda e, k=k: e.matmul(ps[hb_][:, 0:32], lhsT=wfi_t[:, k, cc * 128:(cc + 1) * 128],
                                                        rhs=hh[:, k, :, :].rearrange("p t c -> p (t c)"),
                                                        start=(k == 0), stop=(k == 7)), reads=[Bwfi_h[half][c // 4], Bhh], writes=[Bps[hb_]])
                p2 = ps[hb_][:, 0:32].rearrange("p (t c) -> p t c", c=2)
                ht, Bht = htm.next()
                S.add('dve', lambda e: e.tensor_scalar(out=hcb[:, cc, :, 1], in0=p2[:, :, 1], scalar1=cw_t[:, 0, cc:cc + 1],
                                                       scalar2=None, op0=ALU.mult), reads=[Bps[hb_], Bcw], writes=[Bhcb[cc]])
                S.add('dve', lambda e: e.tensor_scalar(out=ht[:], in0=p2[:, :, 0], scalar1=cw_t[:, 0, cc:cc + 1],
                                                       scalar2=None, op0=ALU.mult), reads=[Bps[hb_], Bcw], writes=[Bht])
                S.add('dve', lambda e: e.scalar_tensor_tensor(out=hcb[:, cc, :, 0], in0=p2[:, :, 1], scalar=cw_t[:, 1, cc:cc + 1],
                                                              in1=ht[:], op0=ALU.mult, op1=ALU.add),
                      reads=[Bps[hb_], Bcw, Bht], writes=[Bhcb[cc]])
            hq = [p + 22 * h_ for p in range(22) for h_ in range(2)]
            for Gq in range(4):
                if Gq > 0:
                    ew_eng[0] = 'pool'
                for c in range(22):
                    res = []
                    for half, bk, ts_ in ((0, ubk3, tus), (1, gbk3, tgs)):
                        cc = c + 22 * half
                        b = bk.next()
                        for k in range(8):
                            S.add('pe', lambda e, k=k, b=b, cc=cc: e.matmul(ps[b][:, :], lhsT=wfi_t[:, k, cc * 128:(cc + 1) * 128], rhs=hg[:, k, :],
                                                                            start=(k == 0), stop=(k == 7)), reads=[Bwfi_h[half][c // 4], Bhg], writes=[Bps[b]])
                        tt, Btt = ts_.next()
                        p3 = ps[b][:, :].rearrange("p (t n) -> p t n", t=4)
                        S.add('act', lambda e, b=b, cc=cc, tt=tt: e.activation(out=tt[:].rearrange("p t n -> p (t n)"), in_=ps[b][:, :], func=AF.Identity,
                                                                               scale=cw_t[:, 2, cc:cc + 1], bias=cw_t[:, 3, cc:cc + 1]),
                              reads=[Bps[b], Bcw], writes=[Btt])
                        S.add('dve', lambda e, p3=p3, cc=cc, tt=tt: e.scalar_tensor_tensor(out=tt[:, :, 1:128], in0=p3[:, :, 0:127], scalar=cw_t[:, 1, cc:cc + 1],
                                                                                          in1=tt[:, :, 1:128], op0=ALU.mult, op1=ALU.add),
                              reads=[Bps[b], Bcw, Btt], writes=[Btt])
                        S.add('dve', lambda e, p3=p3, cc=cc, tt=tt: e.scalar_tensor_tensor(out=tt[:, :, 2:128], in0=p3[:, :, 0:126], scalar=cw_t[:, 0, cc:cc + 1],
                                                                                          in1=tt[:, :, 2:128], op0=ALU.mult, op1=ALU.add),
                              reads=[Bps[b], Bcw, Btt], writes=[Btt])
                        if Gq == 0:
                            S.add(ew_eng[0], lambda e, cc=cc, tt=tt: e.tensor_copy(out=sav[:, cc, :, :], in_=tt[:, :, 0:2]), reads=[Btt], writes=[Bsav])
                        else:
                            S.add(ew_eng[0], lambda e, cc=cc, tt=tt, Gq=Gq: e.tensor_add(out=tt[:, :, 0:2], in0=tt[:, :, 0:2], in1=hcb[:, cc, Gq * 4:(Gq + 1) * 4, :]),
                                  reads=[Btt, Bhcb[cc]], writes=[Btt])
                        res.append((tt, Btt))
                    pend.append((c, res))
                    if len(pend) > 1:
                        fin_pair(*pend.pop(0))
                    if Gq == 0 and c >= 8:
                        if c == 8:
                            hh_compute()
                        for _ in range(3):
                            if hq:
                                halo_chain(hq.pop(0))
                while pend:
                    fin_pair(*pend.pop(0))
                if Gq == 0:
                    while hq:
                        halo_chain(hq.pop(0))
                    S.add('dve', lambda e: e.tensor_add(out=sav[:, :, :, :], in0=sav[:, :, :, :], in1=hcb[:, :, 0:4, :]), reads=[Bsav] + Bhcb, writes=[Bsav])
                    S.add('act', lambda e: e.activation(out=sav[:, 22:44, :, :], in_=sav[:, 22:44, :, :], func=AF.Silu), reads=[Bsav], writes=[Bsav])
                    S.add('dve', lambda e: e.tensor_mul(out=aT[:, :, :].rearrange("p c (t n) -> p c t n", t=4)[:, :, :, 0:2], in0=sav[:, 0:22, :, :],
                                                        in1=sav[:, 22:44, :, :]), reads=[Bsav, BaT], writes=[BaT])
                if Gq + 1 < 4:
                    S.add('sp', lambda e, Gq=Gq: e.dma_start(out=hg, in_=h2Td.rearrange("p (k t) -> p k t", k=8)[:, :, (Gq + 1) * 512:(Gq + 2) * 512]),
                          writes=[Bhg], dma=Bhg)
                for t in range(4):
                    i = Gq * 4 + t
                    x1, Bx1 = x1s.next()
                    S.add('sp', lambda e, x1=x1, i=i: e.dma_start(out=x1[:], in_=x1d[i * 128:(i + 1) * 128, :]), writes=[Bx1], dma=Bx1)
                    x2, Bx2 = x1, Bx1
                    for hf in range(2):
                        b = obk.next()
                        for c in range(22):
                            S.add('pe', lambda e, c=c, b=b, hf=hf, t=t: e.matmul(ps[b][:, :], lhsT=aT[:, c, t * 128:(t + 1) * 128],
                                                                                rhs=wfo_t[:, c, hf * 512:(hf + 1) * 512],
                                                                                start=(c == 0), stop=(c == 21)), reads=[BaT, Bwo_ffn], writes=[Bps[b]])
                        S.add('dve', lambda e, b=b, hf=hf, x1=x1, x2=x2: e.tensor_add(out=x2[:, hf * 512:(hf + 1) * 512], in0=ps[b][:, :],
                                                                                      in1=x1[:, hf * 512:(hf + 1) * 512]), reads=[Bps[b], Bx1], writes=[Bx2])
                    fs, fr, fo, Bf = fin.next()
                    S.add('act', lambda e, fo=fo, fs=fs, x2=x2: e.activation(out=fo[:], in_=x2[:], func=AF.Square, accum_out=fs[:]), reads=[Bx2], writes=[Bf])
                    S.add('dve', lambda e, fs=fs, fr=fr: e.tensor_scalar(out=fr[:], in0=fs[:], scalar1=1.0 / 1024, scalar2=1e-6, op0=ALU.mult, op1=ALU.add),
                          reads=[Bf], writes=[Bf])
                    S.add('act', lambda e, fr=fr: e.activation(out=fr[:], in_=fr[:], func=AF.Sqrt), reads=[Bf], writes=[Bf])
                    S.add('dve', lambda e, fr=fr: e.reciprocal(out=fr[:], in_=fr[:]), reads=[Bf], writes=[Bf])
                    S.add('dve', lambda e, fr=fr, fo=fo, x2=x2: e.scalar_tensor_tensor(out=fo[:], in0=x2[:], scalar=fr[:, 0:1], in1=gam_fin[:],
                                                                                      op0=ALU.mult, op1=ALU.mult), reads=[Bf, Bx2, Bgam], writes=[Bf])
                    S.add('pool', lambda e, fo=fo, i=i: e.dma_start(out=out[i * 128:(i + 1) * 128, :], in_=fo[:]), reads=[Bf], dma=Bf)
            S.emit()
        wfo_s.close()
    return nc


def _t5_bucket(d):
    d = np.maximum(d, 0)
    dd = np.maximum(d, 1).astype(np.float32)
    large = 16 + (np.log(dd / np.float32(16)) / np.float32(math.log(128 / 16)) * np.float32(16)).astype(np.int32)
    large = np.minimum(large, 31)
    return np.where(d < 16, d, large)


def _host_consts(r):
    c = {}
    km = np.zeros((128, 32), np.float32)
    cm = np.zeros((128, 2), np.float32)
    if r == 0:
        km[:, 0] = NEGM
        cm[0:8, 0] = NEGM
    cm[127, 1] = NEGM
    c['keymask'] = km
    c['cmask'] = cm
    k = np.arange(4096)
    c['emat'] = (k[None, :] // 64 == np.arange(64)[:, None]).astype(np.float32).astype(BF_NP)
    sa = np.zeros((128, NQ, 64), np.float32)
    al = np.zeros((128, NQ, 64), np.float32)
    shift = 1 - r
    for i in range(NQ):
        I = 2 * i + 1
        qpos = I * 128 + np.arange(128)
        qblk = qpos // 64
        j = np.arange(64)[None, :]
        first = 2 * shift
        forced = (j == first) | (j == qblk[:, None]) | (j == qblk[:, None] - 1)
        future = j > qblk[:, None]
        dummy = j < first
        a = np.where(forced, 1e30, 0.0)
        a = np.where(future | dummy, -1e30, a)
        sa[:, i, :] = a
        al[:, i, :] = (~(future | dummy)).astype(np.float32)
    c['scoreadd'] = sa
    c['allowed'] = al
    kk = np.arange(128)[:, None]
    qq = np.arange(128)[None, :]
    c['farlow'] = np.tile(np.where(kk > qq, 0.0, NEGM).astype(np.float32), (1, 4)).astype(BF_NP)
    se = np.zeros((33, NQ, 256), np.float32)
    for i in range(NQ):
        I = 2 * i + 1
        for m in range(32):
            cc = 8 * I - 9 + (31 - m)
            if 0 <= cc < 256:
                se[m, i, cc] = 1.0
        lo = 8 * I - 9 + 32
        se[32, i, max(lo, 0):] = 1.0
        se[32, i, 255] = 1.0
        if r == 0:
            se[32, i, 0:8] = 1.0
    c['shiftext'] = se.astype(BF_NP)
    oha = np.zeros((33, 1024), np.float32)
    ohb = np.zeros((33, 1024), np.float32)
    d = np.arange(1024) - 512
    bk = _t5_bucket(d)
    for idx in range(1024):
        if d[idx] < 0:
            oha[32, idx] = 1
            ohb[32, idx] = 1
        else:
            ohb[bk[idx], idx] = 1
            if d[idx] < 128:
                oha[bk[idx], idx] = 1
            else:
                oha[32, idx] = 1
    c['oha'] = oha
    c['ohb'] = ohb
    c['asel'] = np.full((128, 1), float(r), np.float32)
    ov = np.zeros((256, 64), np.float32)
    for j in range(64):
        for m in range(4):
            for n in range(2):
                ci = 4 * j + m - n
                if 0 <= ci < 256:
                    ov[ci, j] += 1
    c['overlap'] = np.ascontiguousarray(ov.reshape(2, 128, 64).transpose(1, 0, 2))
    return c


_NC_CACHE = {}


def run(inputs, debug=False):
    f = lambda a: np.ascontiguousarray(np.asarray(a, dtype=np.float32))
    x = f(inputs['x'])
    w_in = f(inputs['w_in'])[0]
    cs = lambda a, b: w_in[:, a:b]
    shared = {
        'w_kf': np.ascontiguousarray(np.concatenate([cs(O_KA, O_KA + 128), cs(O_KSL, O_KSL + 128), cs(O_KW, O_KW + 128),
                                                     cs(O_KC, O_KC + 128), cs(O_VC, O_VC + 128)], axis=1)),
        'w_v': np.ascontiguousarray(np.concatenate([cs(O_VA, O_VA + 128), cs(O_VSL, O_VSL + 128), cs(O_VW, O_VW + 128)], axis=1)),
        'w_q': np.ascontiguousarray(np.concatenate([cs(O_QA, O_QA + 512), cs(O_QB, O_QB + 512)], axis=1)),
        'w_g': np.ascontiguousarray(np.concatenate([cs(O_GA, O_GA + 1024), cs(O_GB, O_GB + 1024), cs(O_GN, O_GN + 24)], axis=1)),
        'gam': np.ascontiguousarray(np.stack([f(inputs['norm_mix'])[0], f(inputs['norm_ffn'])[0], f(inputs['norm_final'])])),
        'sinks': f(inputs['attn_sinks']),
        'table': f(inputs['rel_bias_table']),
        'posT': np.ascontiguousarray(np.stack([f(inputs['cmp_pos_k'])[0].T, f(inputs['cmp_pos_v'])[0].T])),
        'w1': np.ascontiguousarray(np.stack([f(inputs['cmp_w1_k'])[0], f(inputs['cmp_w1_v'])[0]])),
        'w2': np.ascontiguousarray(np.stack([f(inputs['cmp_w2_k'])[0], f(inputs['cmp_w2_v'])[0]])),
        'w_upa': f(inputs['w_up_a'])[0], 'w_upb': f(inputs['w_up_b'])[0], 'w_out': f(inputs['w_out'])[0],
        'w_fi': f(inputs['w_ffn_in'])[0], 'w_fo': f(inputs['w_ffn_out'])[0],
    }
    cw = f(inputs['conv_w'])[0]
    cb = f(inputs['conv_b'])
    cwb = np.concatenate([cw, cb], axis=0).reshape(4, 44, 128).transpose(2, 0, 1)
    shared['cwb'] = np.ascontiguousarray(cwb)
    consts = [_host_consts(0), _host_consts(1)]
    in_maps = []
    for c in range(8):
        b, r = c // 2, c % 2
        if r == 1:
            xkk = x[b]
        else:
            xkk = np.concatenate([np.zeros((128, 1024), np.float32), x[b][:3968]], axis=0)
        m = dict(shared)
        m.update(consts[r])
        m['xk'] = np.ascontiguousarray(xkk)
        in_maps.append(m)
    key = bool(debug)
    if key not in _NC_CACHE:
        _NC_CACHE[key] = build(debug)
    nc = _NC_CACHE[key]
    res = run_bass_kernel_spmd(nc, in_maps, core_ids=list(range(8)))
    outp = np.zeros((4, 4096, 1024), np.float32)
    for c in range(8):
        b, r = c // 2, c % 2
        o = np.asarray(res.results[c]['out']).reshape(NQ, 128, 1024)
        outp[b].reshape(16, 2, 128, 1024)[:, r] = o
    if debug:
        return outp, res
    return outp


def kernel(**inputs):
    return run(inputs)
```

```python
import math
from contextlib import ExitStack
import numpy as np
import ml_dtypes
BF_NP = ml_dtypes.bfloat16
import concourse.bass as bass
import concourse.mybir as mybir
from concourse.bass_utils import run_bass_kernel_spmd

F32 = mybir.dt.float32
BF = mybir.dt.bfloat16
AF = mybir.ActivationFunctionType
ALU = mybir.AluOpType
AX = mybir.AxisListType
NEGM = -30000.0
NQ = 16


class Buf:
    def __init__(self, name):
        self.name = name
        self.w = None
        self.r = []
        self.sem = None
        self.cnt = 0


class Sched:
    ENG = ['pe', 'act', 'dve', 'pool', 'sp']

    def __init__(self, nc, stack):
        self.nc = nc
        self.ops = []
        self.start = 0
        self.stack = stack
        self.esem = {e: stack.enter_context(nc.semaphore('sem_' + e)) for e in self.ENG}
        self.ecnt = {e: 0 for e in self.ENG}
        self.bar = stack.enter_context(nc.semaphore('sem_bar'))
        self.nphase = 0

    def add(self, eng, fn, reads=(), writes=(), dma=None, bg=False, nowaw=False, own_sem=False):
        i = len(self.ops)
        deps = set()
        for b in reads:
            if b.w is not None:
                deps.add(b.w)
        for b in writes:
            if b.w is not None and not nowaw:
                deps.add(b.w)
            deps.update(b.r)
        for b in reads:
            b.r.append(i)
        for b in writes:
            b.w = i
            b.r = []
        self.ops.append(dict(eng=eng, fn=fn, deps=deps, dma=dma, bg=bg, own_sem=own_sem))
        return i

    def emit(self):
        nc = self.nc
        ops = self.ops
        s0 = self.start
        for o in ops[s0:]:
            o['deps'] = {d for d in o['deps'] if d >= s0 or ops[d]['bg']}
            if o['eng'] == 'pe' and o['dma'] is None:
                o['deps'] = {d for d in o['deps'] if not (ops[d]['eng'] == 'pe' and ops[d]['dma'] is None)}
        need = [False] * len(ops)
        for o in ops[s0:]:
            for d in o['deps']:
                need[d] = True
        self.nphase += 1
        mine_last = {}
        for e in self.ENG:
            idxs = [i for i in range(s0, len(ops)) if ops[i]['eng'] == e and ops[i]['fn'] is not None and ops[i]['dma'] is None]
            mine_last[e] = idxs[-1] if idxs else None
            if idxs:
                need[idxs[-1]] = True
        alld = []
        for i in range(s0, len(ops)):
            o = ops[i]
            if o['dma'] is not None:
                b = o['dma']
                if b.sem is None:
                    b.sem = self.stack.enter_context(nc.semaphore('dsem_' + b.name))
                b.cnt += 16
                o['sig'] = (b.sem, b.cnt)
                if not o['bg']:
                    alld.append(i)
            elif o['own_sem']:
                o['sig'] = (self.stack.enter_context(nc.semaphore('osem_%d' % i)), 1)
            elif need[i] and o['fn'] is not None:
                self.ecnt[o['eng']] += 1
                o['sig'] = (self.esem[o['eng']], self.ecnt[o['eng']])
            else:
                o['sig'] = None
        with nc.Block() as block:
            reg = dict(pe=block.tensor, act=block.scalar, dve=block.vector, pool=block.gpsimd, sp=block.sync)
            for e in self.ENG:
                mine = [o for o in ops[s0:] if o['eng'] == e]

                def body(eh, mine=mine, e=e):
                    seen = {}

                    def wait_for(d):
                        if ops[d]['sig'] is None:
                            return
                        sem, val = ops[d]['sig']
                        k = id(sem)
                        if seen.get(k, 0) >= val:
                            return
                        eh.wait_ge(sem, val)
                        seen[k] = val
                    for o in mine:
                        for d in sorted(o['deps']):
                            wait_for(d)
                        if o['fn'] is None:
                            continue
                        ins = o['fn'](eh)
                        if o['sig'] is not None:
                            sem, val = o['sig']
                            ins.then_inc(sem, 16 if o['dma'] is not None else 1)
                    if e == 'sp':
                        last = {}
                        for d in alld:
                            sem, val = ops[d]['sig']
                            if id(sem) not in last or ops[last[id(sem)]]['sig'][1] < val:
                                last[id(sem)] = d
                        for d in sorted(last.values()):
                            wait_for(d)
                    if mine_last[e] is not None:
                        wait_for(mine_last[e])
                    eh.sem_inc(self.bar, 1)
                    eh.wait_ge(self.bar, 5 * self.nphase)
                reg[e](body)
        self.start = len(ops)


def bc_last(ap, n):
    return bass.AP(ap.tensor, ap.offset, [list(a) for a in ap.ap] + [[0, n]])


class Rot:
    def __init__(self, items):
        self.items = items
        self.i = 0

    def next(self):
        it = self.items[self.i % len(self.items)]
        self.i += 1
        return it


O_QA, O_KA, O_VA, O_QB, O_KC, O_VC, O_KSL, O_VSL, O_KW, O_VW, O_GN, O_GA, O_GB = (
    0, 512, 640, 768, 1280, 1408, 1536, 1664, 1792, 1920, 2048, 2072, 3096)


def build(debug=False):
    nc = bass.Bass("TRN2", target_bir_lowering=False)

    def di(n, s, dt=F32):
        return nc.dram_tensor(n, list(s), dt, kind="ExternalInput").ap()
    xk = di("xk", [4096, 1024])
    w_kf = di("w_kf", [1024, 640])
    w_v = di("w_v", [1024, 384])
    w_q = di("w_q", [1024, 1024])
    w_g = di("w_g", [1024, 2072])
    gam = di("gam", [3, 1024])
    sinks = di("sinks", [1, 8])
    table = di("table", [32, 16])
    posT = di("posT", [2, 64, 32])
    w1 = di("w1", [2, 2048, 256])
    w2 = di("w2", [2, 256, 64])
    w_upa = di("w_upa", [512, 1024])
    w_upb = di("w_upb", [512, 1024])
    w_out = di("w_out", [1024, 1024])
    w_fi = di("w_fi", [1024, 5632])
    cwb = di("cwb", [128, 4, 44])
    w_fo = di("w_fo", [2816, 1024])
    keymask = di("keymask", [128, 32])
    cmask = di("cmask", [128, 2])
    emat = di("emat", [64, 4096], BF)
    scoreadd = di("scoreadd", [128, NQ, 64])
    allowed = di("allowed", [128, NQ, 64])
    farlow = di("farlow", [128, 512], BF)
    shiftext = di("shiftext", [33, NQ, 256], BF)
    oha = di("oha", [33, 1024])
    ohb = di("ohb", [33, 1024])
    asel = di("asel", [128, 1])
    overlap = di("overlap", [128, 2, 64])
    out = nc.dram_tensor("out", [2048, 1024], F32, kind="ExternalOutput").ap()
    dbg = {}
    if debug:
        dbg['ya'] = nc.dram_tensor("dbg_ya", [2048, 512], F32, kind="ExternalOutput").ap()
        dbg['yb'] = nc.dram_tensor("dbg_yb", [2048, 512], F32, kind="ExternalOutput").ap()
        dbg['x1'] = nc.dram_tensor("dbg_x1", [2048, 1024], F32, kind="ExternalOutput").ap()
    biasd = nc.dram_tensor("biasd", [16, 1024], BF, kind="Internal").ap()
    x1d = nc.dram_tensor("x1d", [2048, 1024], F32, kind="Internal").ap()
    h2Td = nc.dram_tensor("h2Td", [128, 8 * 2048], BF, kind="Internal").ap()
    yabd = nc.dram_tensor("yabd", [2048, 1024], BF, kind="Internal").ap()
    cin = nc.dram_tensor("cin", [128, 256], BF, kind="Internal")
    w1b = nc.dram_tensor("w1b", [2, 64, 32, 256], BF, kind="Internal").ap()
    wqb = nc.dram_tensor("wqb", [1024, 1024], BF, kind="Internal").ap()
    wupab = nc.dram_tensor("wupab", [512, 1024], BF, kind="Internal").ap()
    wupbb = nc.dram_tensor("wupbb", [512, 1024], BF, kind="Internal").ap()
    wob = nc.dram_tensor("wob", [1024, 1024], BF, kind="Internal").ap()
    wfib = nc.dram_tensor("wfib", [1024, 5632], BF, kind="Internal").ap()
    wfob = nc.dram_tensor("wfob", [2816, 1024], BF, kind="Internal").ap()
    cout = nc.dram_tensor("cout", [256, 256], BF, kind="Internal")

    with ExitStack() as st:
        S = Sched(nc, st)

        def sbuf(stk, n, s, d):
            return stk.enter_context(nc.sbuf_tensor(n, list(s), d))
        ps = [st.enter_context(nc.psum_tensor("ps%d" % i, [128, 512], F32)) for i in range(8)]
        Bps = [Buf("ps%d" % i) for i in range(8)]

        ident = sbuf(st, "ident", [128, 128], BF)
        gam_mix = sbuf(st, "gam_mix", [128, 1024], F32)
        junk_t = sbuf(st, "junk_t", [128, 1024], BF)
        Bjunk = Buf("junk")
        arena0 = sbuf(st, "arena0", [128, 16384], BF)
        wg_t = arena0[:, :].rearrange("p (k n) -> p k n", k=8)
        Bwg = Buf("wg")
        Bconst = Buf("const")
        Bgam = Buf("gam")

        def norm_h_g(xt, Bx, gam_ap, tmp, lnexp=False):
            ssq, rstd, h, Bt = tmp
            if lnexp:
                S.add('act', lambda e: e.activation(out=junk_t[:], in_=xt, func=AF.Square, accum_out=ssq[:]),
                      reads=[Bx], writes=[Bt, Bjunk])
                yield
                S.add('dve', lambda e: e.tensor_scalar(out=rstd[:], in0=ssq[:], scalar1=1.0 / 1024, scalar2=1e-6,
                                                       op0=ALU.mult, op1=ALU.add), reads=[Bt], writes=[Bt])
                yield
                S.add('act', lambda e: e.activation(out=rstd[:], in_=rstd[:], func=AF.Ln), reads=[Bt], writes=[Bt])
                S.add('act', lambda e: e.activation(out=rstd[:], in_=rstd[:], func=AF.Exp, scale=-0.5), reads=[Bt], writes=[Bt])
                yield
                S.add('dve', lambda e: e.scalar_tensor_tensor(out=h[:], in0=xt, scalar=rstd[:, 0:1], in1=gam_ap,
                                                              op0=ALU.mult, op1=ALU.mult),
                      reads=[Bx, Bt, Bgam], writes=[Bt])
                yield
                return
            S.add('act', lambda e: e.activation(out=junk_t[:], in_=xt, func=AF.Square, accum_out=ssq[:]),
                  reads=[Bx], writes=[Bt, Bjunk])
            yield
            S.add('dve', lambda e: e.tensor_scalar(out=rstd[:], in0=ssq[:], scalar1=1.0 / 1024, scalar2=1e-6,
                                                   op0=ALU.mult, op1=ALU.add), reads=[Bt], writes=[Bt])
            yield
            S.add('act', lambda e: e.activation(out=rstd[:], in_=rstd[:], func=AF.Sqrt), reads=[Bt], writes=[Bt])
            yield
            S.add('dve', lambda e: e.reciprocal(out=rstd[:], in_=rstd[:]), reads=[Bt], writes=[Bt])
            yield
            S.add('dve', lambda e: e.scalar_tensor_tensor(out=h[:], in0=xt, scalar=rstd[:, 0:1], in1=gam_ap,
                                                          op0=ALU.mult, op1=ALU.mult),
                  reads=[Bx, Bt, Bgam], writes=[Bt])
            yield

        def trans_g(h, Bt, hT_out, BhT, bank, eng='act'):
            pT = ps[bank][:].bitcast(BF)
            for k in range(8):
                S.add('pe', lambda e, k=k: e.transpose(out=pT[:, k * 128:(k + 1) * 128], in_=h[:, k * 128:(k + 1) * 128],
                                                       identity=ident[:]), reads=[Bt, Bconst], writes=[Bps[bank]])
                if k % 4 == 3:
                    yield
            if eng == 'act':
                S.add('act', lambda e: e.copy(out=hT_out, in_=pT[:, 0:1024].rearrange("p (k t) -> p k t", k=8)),
                      reads=[Bps[bank]], writes=[BhT])
            else:
                S.add('dve', lambda e: e.tensor_copy(out=hT_out, in_=pT[:, 0:1024].rearrange("p (k t) -> p k t", k=8)),
                      reads=[Bps[bank]], writes=[BhT])
            yield

        def norm_T_g(xt, Bx, gam_ap, hT_out, BhT, tmp, bank):
            yield from norm_h_g(xt, Bx, gam_ap, tmp)
            yield from trans_g(tmp[2], tmp[3], hT_out, BhT, bank)

        def norm_T(xt, Bx, gam_ap, hT_out, BhT, tmp, bank):
            for _ in norm_T_g(xt, Bx, gam_ap, hT_out, BhT, tmp, bank):
                pass

        def zipgens_dyn(lst):
            while lst:
                for g in list(lst):
                    try:
                        next(g)
                    except StopIteration:
                        lst.remove(g)

        def zipgens(gens):
            gens = [g for g in gens if g is not None]
            while gens:
                alive = []
                for g in gens:
                    try:
                        next(g)
                        alive.append(g)
                    except StopIteration:
                        pass
                gens = alive

        cast_rr = [0]

        def load_cast(dst, src, stage_rot, Bdst, engs=('dve', 'act')):
            stg_t, Bst = stage_rot.next()
            n = 1
            for d_ in dst.shape[1:]:
                n *= d_
            sv = stg_t[0:dst.shape[0], 0:n]
            if len(dst.shape) == 3:
                sv = sv.rearrange("p (a b) -> p a b", a=dst.shape[1])
            S.add('sp', lambda e: e.dma_start(out=sv, in_=src), writes=[Bst], dma=Bst)
            eng = engs[cast_rr[0] % len(engs)]
            cast_rr[0] += 1
            if eng == 'act':
                S.add('act', lambda e: e.copy(out=dst, in_=sv), reads=[Bst], writes=[Bdst], nowaw=True)
            else:
                S.add(eng, lambda e: e.tensor_copy(out=dst, in_=sv), reads=[Bst], writes=[Bdst], nowaw=True)

        def mk_tmp(stk, n):
            return (sbuf(stk, "ssq" + n, [128, 1], F32),
                    sbuf(stk, "rstd" + n, [128, 1], F32), sbuf(stk, "hbf" + n, [128, 1024], BF), Buf("nt" + n))

        att = ExitStack()
        KA = sbuf(att, "KA", [128, 2, 4096], BF)
        KW = sbuf(att, "KW", [128, 2, 4096], BF)
        KS = sbuf(att, "KS", [128, 2, 4096], BF)
        Vx = sbuf(att, "Vx", [128, 32, 3, 2, 65], BF)
        KCT = sbuf(att, "KCT", [128, 2, 256], BF)
        VCx = sbuf(att, "VCx", [128, 2, 2, 129], BF)
        bias_t = sbuf(att, "bias_t", [128, 8, 512], BF)
        bandx = sbuf(att, "bandx", [128, 2, 512], BF)
        farlow_t = sbuf(att, "farlow_t", [128, 512], BF)
        keymask_t = sbuf(att, "keymask_t", [128, 32], F32)
        cmask_t = sbuf(att, "cmask_t", [128, 2], F32)
        sinkexp = sbuf(att, "sinkexp", [128, 8], F32)
        BKA, BKW, BKS, BVx, BKCT, BVCx, Bbias = [Buf(n) for n in "KA KW KS Vx KCT VCx bias".split()]

        def bias_chain(stk):
            tabx = sbuf(stk, "tabx", [33, 16], F32)
            tab31 = sbuf(stk, "tab31", [32, 16], F32)
            oh_t = sbuf(stk, "oh_t", [33, 1024], F32)
            vec_t = sbuf(stk, "vec_t", [16, 1024], BF)
            Jf = sbuf(stk, "Jf", [128, 128], F32)
            Jb = sbuf(stk, "Jb", [128, 128], BF)
            Hxs = Rot([(sbuf(stk, "Hx%d" % i, [128, 512], BF), Buf("Hx%d" % i)) for i in range(4)])
            Bp0, Bvec, Boh, Btab, Bt31, BJ = [Buf(n) for n in "bc_p0 bc_vec bc_oh bc_tab bc_t31 bc_J".split()]
            S.add('sp', lambda e: e.dma_start(out=tabx[0:32, :], in_=table), writes=[Btab], dma=Btab)
            S.add('sp', lambda e: e.dma_start(out=tab31[:], in_=table[31:32, :].rearrange("a n -> (a n)").partition_broadcast(32)),
                  writes=[Bt31], dma=Bt31)
            S.add('pool', lambda e: e.memset(tabx[32:33, :], NEGM), writes=[Btab])
            S.add('dve', lambda e: e.tensor_sub(out=tabx[0:32, :], in0=tabx[0:32, :], in1=tab31[:]), reads=[Btab, Bt31], writes=[Btab])
            yield
            for m in range(2):
                S.add('sp', lambda e, m=m: e.dma_start(out=oh_t[:, :], in_=(oha if m == 0 else ohb)), writes=[Boh], dma=Boh)
                for hf in range(2):
                    S.add('pe', lambda e, m=m, hf=hf: e.matmul(ps[0][0:8, :], lhsT=tabx[:, m * 8:(m + 1) * 8],
                                                               rhs=oh_t[:, hf * 512:(hf + 1) * 512], start=True, stop=True),
                          reads=[Btab, Boh], writes=[Bps[0]])
                    S.add('act', lambda e, m=m, hf=hf: e.copy(out=vec_t[0:8, hf * 512:(hf + 1) * 512], in_=ps[0][0:8, :]),
                          reads=[Bps[0]], writes=[Bvec])
                    S.add('sp', lambda e, m=m, hf=hf: e.dma_start(out=biasd[m * 8:(m + 1) * 8, hf * 512:(hf + 1) * 512],
                                                                   in_=vec_t[0:8, hf * 512:(hf + 1) * 512]),
                          reads=[Bvec], writes=[Bp0], dma=Bvec)
                    yield
            bt = biasd.tensor
            S.add('pool', lambda e: e.memset(Jf[:], 0.0), writes=[BJ])
            S.add('pool', lambda e: e.affine_select(out=Jf[:], in_=Jf[:], pattern=[[1, 128]], compare_op=ALU.not_equal,
                                                    fill=1.0, base=-127, channel_multiplier=1), reads=[BJ], writes=[BJ])
            S.add('dve', lambda e: e.tensor_copy(out=Jb[:], in_=Jf[:]), reads=[BJ], writes=[BJ])
            yield
            for m in range(2):
                for kind in range(2):
                    for g in range(2):
                        idx = m * 4 + kind * 2 + g
                        Hx, BHx = Hxs.next()
                        src = bass.AP(bt, (m * 8 + 4 * g) * 1024 + 512 + 128 * kind - 127, [[1, 128], [1024, 4], [1, 128]])
                        S.add('sp', lambda e, Hx=Hx, src=src: e.dma_start(
                            out=Hx[:, :].rearrange("p (h q) -> p h q", h=4), in_=src), reads=[Bp0], writes=[BHx], dma=BHx)
                        S.add('pe', lambda e, Hx=Hx: e.matmul(ps[0][:, :], lhsT=Jb[:], rhs=Hx[:, :], start=True, stop=True),
                              reads=[BJ, BHx], writes=[Bps[0]])
                        S.add('act', lambda e, idx=idx: e.copy(out=bias_t[:, idx, :], in_=ps[0][:, :]), reads=[Bps[0]], writes=[Bbias])
                        yield
            for g in range(2):
                src = bass.AP(bt, (8 + 4 * g) * 1024 + 512 - 383, [[16, 32], [1024, 4], [1, 128]])
                S.add('sp', lambda e, g=g, src=src: e.dma_start(out=bandx[0:32, g, :].rearrange("p (h q) -> p h q", h=4), in_=src),
                      reads=[Bp0], writes=[Bbias], dma=Bbias)
            yield

        Bpre = {n: Buf('pre_' + n) for n in ('w1', 'wq', 'wup', 'wo', 'wfi', 'wfo')}
        kvs = ExitStack()
        KCr = sbuf(kvs, "KCr", [64, 2, 2, 16, 260], BF)
        BKCr = Buf("KCr")
        p1w = ExitStack()
        wkf_t = sbuf(p1w, "wkf_t", [128, 8, 640], BF)
        wv_t = sbuf(p1w, "wv_t", [128, 8, 384], BF)
        Bw1p = [Buf("w1p%d" % k) for k in range(2)]
        xts1 = Rot([(sbuf(p1w, "xt%d" % i, [128, 1024], F32), Buf("xt%d" % i)) for i in range(2)])
        hTs = Rot([(sbuf(p1w, "hTG%d" % i, [128, 8, 512], BF), Buf("hTG%d" % i)) for i in range(2)])
        stg0 = Rot([(xts1.items[0][0][:, :], xts1.items[0][1]), (xts1.items[1][0][:, :], xts1.items[1][1])] +
                   [(hTs.items[i][0][:, 4 * j:4 * j + 4, :].rearrange("p a b -> p (a b)").bitcast(F32), hTs.items[i][1]) for i in range(2) for j in range(2)])
        with ExitStack() as p0:
            identf = sbuf(p1w, "identf", [128, 128], F32)
            sk = sbuf(p1w, "sk", [128, 8], F32)
            t31 = sbuf(p1w, "t31", [128, 8], F32)
            Bp0 = Buf("p0")
            Bd = [Buf("c%d" % i) for i in range(12)]
            for k in range(8):
                load_cast(wkf_t[:, k, :], w_kf[k * 128:(k + 1) * 128, :], stg0, Bw1p[k // 4], engs=(('dve',) if k < 4 else ('act',)))
                load_cast(wv_t[:, k, :], w_v[k * 128:(k + 1) * 128, :], stg0, Bw1p[k // 4], engs=(('dve',) if k < 4 else ('act',)))
            S.add('pool', lambda e: e.memset(identf[:], 0.0), writes=[Bconst])
            S.add('pool', lambda e: e.affine_select(out=identf[:], in_=identf[:], pattern=[[-1, 128]], compare_op=ALU.not_equal,
                                                    fill=1.0, base=0, channel_multiplier=1), reads=[Bconst], writes=[Bconst])
            S.add('dve', lambda e: e.tensor_copy(out=ident[:], in_=identf[:]), reads=[Bconst], writes=[Bconst])
            S.add('sp', lambda e: e.dma_start(out=gam_mix[:], in_=gam[0:1, :].rearrange("a n -> (a n)").partition_broadcast(128)),
                  writes=[Bgam], dma=Bgam)
            S.add('sp', lambda e: e.dma_start(out=keymask_t[:], in_=keymask), writes=[Bd[0]], dma=Bd[0])
            S.add('sp', lambda e: e.dma_start(out=cmask_t[:], in_=cmask), writes=[Bd[1]], dma=Bd[1])
            S.add('sp', lambda e: e.dma_start(out=farlow_t[:], in_=farlow), writes=[Bd[4]], dma=Bd[4])
            Bd4 = Bd[4]
            for g in range(2):
                S.add('sp', lambda e, g=g: e.dma_start(out=KS[64:128, g, :], in_=emat), writes=[Bd[6 + g]], dma=Bd[6 + g])
            S.add('sp', lambda e: e.dma_start(out=sk[:], in_=sinks.rearrange("a n -> (a n)").partition_broadcast(128)),
                  writes=[Bd[11]], dma=Bd[11])
            S.add('sp', lambda e: e.dma_start(out=t31[:], in_=table[31:32, 0:8].rearrange("a n -> (a n)").partition_broadcast(128)),
                  writes=[Bd[11]], dma=Bd[11])
            S.add('dve', lambda e: e.tensor_sub(out=sk[:], in0=sk[:], in1=t31[:]), reads=[Bd[11]], writes=[Bp0])
            S.add('act', lambda e: e.activation(out=sinkexp[:], in_=sk[:], func=AF.Exp), reads=[Bp0], writes=[Bconst])
            S.add('pool', lambda e: e.memset(bandx[32:64, :, :], 0.0), writes=[Bconst])
            S.add('pool', lambda e: e.memset(bandx[64:128, :, :], 0.0), writes=[Bconst])
            S.add('pool', lambda e: e.memset(bandx[32:33, :, :], NEGM), writes=[Bconst])
            S.add('pool', lambda e: e.memset(Vx[:, :, :, :, 64:65], 1.0), writes=[BVx])
            S.add('pool', lambda e: e.memset(VCx[:, :, :, 64:65], 1.0), writes=[BVCx])
            for g in range(2):
                S.add('pool', lambda e, g=g: e.dma_start(out=VCx[:, :, g, 65:129], in_=overlap), writes=[BVCx], dma=BVCx)

        with ExitStack() as p1:
            Bw = Bw1p
            for k in range(8):
                S.add('pool', lambda e, k=k: e.dma_start(out=wg_t[:, k, :], in_=w_g[k * 128:(k + 1) * 128, 0:2048]), writes=[Bwg], dma=Bwg)
            S.add('pool', lambda e: e.memset(KCr[:, :, :, :, 256:260], 0.0), writes=[BKCr])
            Bpad = Buf("kpad")
            S.add('pool', lambda e: e.memset(KA[64:128, :, :], 0.0), writes=[Bpad])
            S.add('pool', lambda e: e.memset(KW[64:128, :, :], 0.0), writes=[Bpad])
            S.add('pool', lambda e: e.memset(KCT[64:128, :, :], 0.0), writes=[Bpad])
            bgq = []
            for kd in range(2):
                for hf in range(2):
                    bgq.append((lambda e, kd=kd, hf=hf: e.dma_start(out=w1b[kd, :, hf * 16:(hf + 1) * 16, :], in_=w1[kd, hf * 1024:(hf + 1) * 1024, :].rearrange("(l d) n -> d l n", d=64)), 'w1'))
            for hf in range(2):
                bgq.append((lambda e, hf=hf: e.dma_start(out=wqb[hf * 512:(hf + 1) * 512, :], in_=w_q[hf * 512:(hf + 1) * 512, :]), 'wq'))

            def issue_bg(n):
                for _ in range(n):
                    if bgq:
                        fn, nm = bgq.pop(0)
                        S.add('pool', fn, writes=[Bpre[nm]], dma=Bpre[nm], bg=True, nowaw=True)
            xts = xts1
            tmps = Rot([mk_tmp(p1, "a%d" % i) for i in range(4)])
            fb = Rot([1, 2])
            tb = Rot([0, 4])
            vb = Rot([3, 5])
            normed = {}

            def stageN(G):
                lst = []
                for t in range(4):
                    p = G * 4 + t
                    xt, Bx = xts.next()
                    S.add('sp', lambda e, xt=xt, p=p: e.dma_start(out=xt[:], in_=xk[p * 128:(p + 1) * 128, :]), writes=[Bx], dma=Bx)
                    tmp = tmps.next()
                    yield from norm_h_g(xt[:], Bx, gam_mix[:], tmp)
                    lst.append(tmp)
                normed[G] = lst

            def stageP(G):
                issue_bg(1)
                hTG, BhTG = hTs.next()
                lst = normed.pop(G)
                for t in range(4):
                    p = G * 4 + t
                    tmp = lst[t]
                    yield from trans_g(tmp[2], tmp[3], hTG[:, :, t * 128:(t + 1) * 128], BhTG, tb.next(), eng=('act' if t % 2 == 0 else 'dve'))
                for t in range(4):
                    p = G * 4 + t
                    b3 = vb.next()
                    for k in range(8):
                        S.add('pe', lambda e, k=k, t=t, b3=b3: e.matmul(ps[b3][:, 0:384], lhsT=hTG[:, k, t * 128:(t + 1) * 128],
                                                                        rhs=wv_t[:, k, :], start=(k == 0), stop=(k == 7)),
                              reads=[BhTG, Bw[k // 4]], writes=[Bps[b3]])
                        if k % 4 == 3:
                            yield
                    S.add('act', lambda e, p=p, b3=b3: e.copy(out=Vx[:, p, :, :, 0:64],
                                                              in_=ps[b3][:, 0:384].rearrange("p (a g d) -> p a g d", a=3, g=2)),
                          reads=[Bps[b3]], writes=[BVx])
                    yield
                for kind in range(5):
                    for g in range(2):
                        b = fb.next()
                        c0 = kind * 128 + g * 64
                        for k in range(8):
                            S.add('pe', lambda e, k=k, b=b, c0=c0: e.matmul(ps[b][0:64, :], lhsT=wkf_t[:, k, c0:c0 + 64],
                                                                            rhs=hTG[:, k, :], start=(k == 0), stop=(k == 7)),
                                  reads=[BhTG, Bw[k // 4]], writes=[Bps[b]])
                            if k % 4 == 3:
                                yield
                        src = ps[b][0:64, :]
                        if kind == 0:
                            dst, Bdst = KA[0:64, g, G * 512:(G + 1) * 512], BKA
                        elif kind == 1:
                            dst, Bdst = KS[0:64, g, G * 512:(G + 1) * 512], BKS
                        elif kind == 2:
                            dst, Bdst = KW[0:64, g, G * 512:(G + 1) * 512], BKW
                        else:
                            dst, Bdst = KCr[:, kind - 3, g, :, G * 32:(G + 1) * 32].rearrange("p r n -> p n r"), BKCr
                            src = ps[b][0:64, :].rearrange("p (n r) -> p n r", r=16)
                        if (kind + g) % 2 == 0:
                            S.add('act', lambda e, src=src, dst=dst: e.copy(out=dst, in_=src), reads=[Bps[b]], writes=[Bdst])
                        else:
                            S.add('dve', lambda e, src=src, dst=dst: e.tensor_copy(out=dst, in_=src), reads=[Bps[b]], writes=[Bdst])
                        yield

            zipgens([stageN(0)])
            for G in range(8):
                zipgens([stageP(G), stageN(G + 1) if G + 1 < 8 else None])
            S.emit()
        p1w.close()

        with ExitStack() as pc:
            w1_t = sbuf(pc, "w1_t", [64, 2, 32, 256], BF)
            w2_t = sbuf(pc, "w2_t", [128, 2, 2, 64], BF)
            pos_t = sbuf(pc, "pos_t", [64, 2, 32], BF)
            hb = sbuf(pc, "hb", [128, 4], F32)
            Bw = Buf("wc")
            Bhb = Buf("hb")
            Bw1c = [[Buf("w1c%d_%d" % (kd, l4)) for l4 in range(2)] for kd in range(2)]
            for kd in range(2):
                for l4 in range(8):
                    S.add('sp', lambda e, kd=kd, l4=l4: e.dma_start(
                        out=w1_t[:, kd, l4 * 4:(l4 + 1) * 4, :], in_=w1b[kd, :, l4 * 4:(l4 + 1) * 4, :]),
                        reads=[Bpre['w1']], writes=[Bw1c[kd][l4 // 4]], dma=Bw1c[kd][l4 // 4], nowaw=True)
                S.add('pool', lambda e, kd=kd: e.dma_start(out=w2_t[:, kd, :, :], in_=w2[kd].rearrange("(c p) n -> p c n", p=128)),
                      writes=[Bw], dma=Bw)
                S.add('pool', lambda e, kd=kd: e.dma_start(out=pos_t[:, kd, :], in_=posT[kd]), writes=[Bw], dma=Bw)
            def compress_g():
                for kd in range(2):
                    for hc in range(2):
                        col = kd * 2 + hc
                        for l in range(32):
                            S.add('pe', lambda e, kd=kd, hc=hc, l=l, col=col: e.matmul(
                                ps[4][:, col:col + 1], lhsT=w1_t[:, kd, l, hc * 128:(hc + 1) * 128], rhs=pos_t[:, kd, l:l + 1],
                                start=(l == 0), stop=(l == 31), skip_group_check=True), reads=[Bw, Bw1c[kd][l // 16]], writes=[Bps[4]])
                S.add('dve', lambda e: e.tensor_copy(out=hb[:], in_=ps[4][:, 0:4]), reads=[Bps[4]], writes=[Bhb])
                yield
                gt = Rot([(sbuf(pc, "gx%d" % i, [128, 256], F32), sbuf(pc, "gu%d" % i, [128, 256], F32), Buf("gt%d" % i)) for i in range(2)])
                gel = Rot([(sbuf(pc, "gel%d" % i, [128, 2, 256], BF), Buf("gel%d" % i)) for i in range(2)])
                hbk = Rot([1, 2])
                for kd in range(2):
                    for g in range(2):
                        ge, Bge = gel.next()
                        for hc in range(2):
                            b = hbk.next()
                            col = kd * 2 + hc
                            for l in range(32):
                                rhs = KCr[:, kd, g, l % 16, (l // 16):(l // 16) + 256]
                                S.add('pe', lambda e, kd=kd, hc=hc, l=l, b=b, rhs=rhs: e.matmul(
                                    ps[b][:, 0:256], lhsT=w1_t[:, kd, l, hc * 128:(hc + 1) * 128], rhs=rhs,
                                    start=(l == 0), stop=(l == 31)), reads=[BKCr, Bw1c[kd][l // 16]], writes=[Bps[b]])
                                if l % 8 == 7:
                                    yield
                            gx, gu, Bg = gt.next()
                            S.add('dve', lambda e, b=b, gx=gx, col=col: e.tensor_scalar(out=gx[:], in0=ps[b][:, 0:256], scalar1=hb[:, col:col + 1],
                                                                                        scalar2=None, op0=ALU.add), reads=[Bps[b], Bhb], writes=[Bg])
                            S.add('act', lambda e, gx=gx, gu=gu: e.activation(out=gu[:], in_=gx[:], func=AF.Square), reads=[Bg], writes=[Bg])
                            S.add('dve', lambda e, gu=gu: e.tensor_scalar(out=gu[:], in0=gu[:], scalar1=0.044715, scalar2=1.0,
                                                                          op0=ALU.mult, op1=ALU.add), reads=[Bg], writes=[Bg])
                            S.add('dve', lambda e, gx=gx, gu=gu: e.tensor_mul(out=gu[:], in0=gu[:], in1=gx[:]), reads=[Bg], writes=[Bg])
                            S.add('act', lambda e, gu=gu: e.activation(out=gu[:], in_=gu[:], func=AF.Sigmoid, scale=1.5957691216057308),
                                  reads=[Bg], writes=[Bg])
                            S.add('dve', lambda e, gx=gx, gu=gu, ge=ge, hc=hc: e.tensor_mul(out=ge[:, hc, :], in0=gu[:], in1=gx[:]),
                                  reads=[Bg], writes=[Bge])
                            yield
                        if kd == 0:
                            for hc in range(2):
                                S.add('pe', lambda e, hc=hc, ge=ge: e.matmul(ps[5][0:64, 0:256], lhsT=w2_t[:, 0, hc, :], rhs=ge[:, hc, :],
                                                                             start=(hc == 0), stop=(hc == 1)), reads=[Bge, Bw], writes=[Bps[5]])
                            S.add('act', lambda e, g=g: e.copy(out=KCT[0:64, g, :], in_=ps[5][0:64, 0:256]), reads=[Bps[5]], writes=[BKCT])
                        else:
                            for ct in range(2):
                                for hc in range(2):
                                    S.add('pe', lambda e, hc=hc, ct=ct, ge=ge: e.matmul(
                                        ps[6][:, ct * 64:(ct + 1) * 64], lhsT=ge[:, hc, ct * 128:(ct + 1) * 128], rhs=w2_t[:, 1, hc, :],
                                        start=(hc == 0), stop=(hc == 1), skip_group_check=True), reads=[Bge, Bw], writes=[Bps[6]])
                            S.add('act', lambda e, g=g: e.copy(out=VCx[:, :, g, 0:64], in_=ps[6][:, 0:128].rearrange("p (c d) -> p c d", c=2)),
                                  reads=[Bps[6]], writes=[BVCx])

            zipgens([compress_g(), bias_chain(pc)])
            S.emit()
        kvs.close()

        with ExitStack() as pa:
            wq_t = sbuf(pa, "wq_t", [128, 8, 1024], BF)
            xts = Rot([(sbuf(pa, "xq%d" % i, [128, 1024], F32), Buf("xq%d" % i)) for i in range(2)])
            stga = xts
            scoreadd_t = sbuf(pa, "scoreadd_t", [128, NQ, 64], F32)
            allowed_t = sbuf(pa, "allowed_t", [128, NQ, 64], F32)
            Bsa = Buf("scoreadd")
            shift_t = sbuf(pa, "shift_t", [128, NQ, 256], BF)
            S.add('dve', lambda e: e.memset(shift_t[32:64, :, :], 0.0), writes=[Bsa])
            S.add('dve', lambda e: e.memset(shift_t[64:128, :, :], 0.0), writes=[Bsa])
            S.add('sp', lambda e: e.dma_start(out=shift_t[0:33, :, :], in_=shiftext), writes=[Bsa], dma=Bsa)
            S.add('sp', lambda e: e.dma_start(out=scoreadd_t[:], in_=scoreadd), writes=[Bsa], dma=Bsa)
            S.add('sp', lambda e: e.dma_start(out=allowed_t[:], in_=allowed), writes=[Bsa], dma=Bsa)
            wgn_t = sbuf(pa, "wgn_t", [128, 8, 24], BF)
            Bw = Buf("wa")
            Bwq = [Buf("wq%d" % k) for k in range(2)]
            bgq2 = []
            bgq2.append((lambda e: e.dma_start(out=wupab, in_=w_upa), 'wup'))
            bgq2.append((lambda e: e.dma_start(out=wupbb, in_=w_upb), 'wup'))
            for hf in range(2):
                bgq2.append((lambda e, hf=hf: e.dma_start(out=wob[hf * 512:(hf + 1) * 512, :], in_=w_out[hf * 512:(hf + 1) * 512, :]), 'wo'))
            for rb in range(8):
                bgq2.append((lambda e, rb=rb: e.dma_start(out=wfib[rb * 128:(rb + 1) * 128, :].rearrange("r (a b) -> r a b", b=1408),
                                                          in_=w_fi[rb * 128:(rb + 1) * 128, :].rearrange("r (a b) -> r a b", b=1408)), 'wfi'))
            for rb in range(4):
                bgq2.append((lambda e, rb=rb: e.dma_start(out=wfob[rb * 704:(rb + 1) * 704, :], in_=w_fo[rb * 704:(rb + 1) * 704, :]), 'wfo'))

            def issue_bg2(n):
                for _ in range(n):
                    if bgq2:
                        fn, nm = bgq2.pop(0)
                        S.add('pool', fn, writes=[Bpre[nm]], dma=Bpre[nm], bg=True, nowaw=True)
            for k in range(8):
                S.add('sp', lambda e, k=k: e.dma_start(out=wq_t[:, k, :], in_=wqb[k * 128:(k + 1) * 128, :]), reads=[Bpre['wq']], writes=[Bwq[k // 4]], dma=Bwq[k // 4], nowaw=True)
            S.add('pool', lambda e: e.dma_start(out=wgn_t[:], in_=w_g[:, 2048:2072].rearrange("(k p) n -> p k n", p=128)), writes=[Bw], dma=Bw)
            tmps = Rot([mk_tmp(pa, "b%d" % i) for i in range(1)])
            hTq = Rot([(sbuf(pa, "hTq%d" % i, [128, 8, 128], BF), Buf("hTq%d" % i)) for i in range(2)])
            QAs = Rot([(sbuf(pa, "QA%d" % i, [128, 2, 512], BF), Buf("QA%d" % i)) for i in range(2)])
            for i_ in range(2):
                S.add('pool', lambda e, i_=i_: e.memset(QAs.items[i_][0][64:128, :, :], 0.0), writes=[QAs.items[i_][1]])
            QSs = Rot([(sbuf(pa, "QS%d" % i, [128, 2, 512], BF), Buf("QSlo%d" % i), [Buf("QShi%d_%d" % (i, g)) for g in range(2)])
                       for i in range(2)])
            for i_ in range(2):
                S.add('pool', lambda e, i_=i_: e.memset(QSs.items[i_][0][64:128, :, :], 0.0), writes=QSs.items[i_][2])
            gns = Rot([(sbuf(pa, "gn%d" % i, [128, 24], F32), Buf("gn%d" % i)) for i in range(2)])
            negs = Rot([(sbuf(pa, "negs%d" % i, [128, 128], BF), Buf("negs%d" % i)) for i in range(2)])
            for i in range(2):
                S.add('pool', lambda e, i=i: e.memset(negs.items[i][0][:], 0.0), writes=[negs.items[i][1]])
            Pts = Rot([(sbuf(pa, "Pt%d" % i, [128, 512], BF), Buf("Pt%d" % i)) for i in range(4)])
            sbank = Rot([0, 1, 6])
            abank = Rot([2, 3, 4, 5])
            ybf = Rot([(sbuf(pa, "ybf%d" % i, [128, 512], F32), Buf("ybf%d" % i)) for i in range(1)])
            caccs = Rot([(sbuf(pa, "cacc%d" % i, [128, 4, 129], F32), Buf("cacc%d" % i)) for i in range(2)])
            yab = Rot([(sbuf(pa, "yab%d" % i, [128, 1024], BF), Buf("yab%d" % i)) for i in range(2)])
            sm = Rot([(sbuf(pa, "smA%d" % i, [128, 8], F32), sbuf(pa, "smB%d" % i, [128, 8], F32),
                       sbuf(pa, "smT%d" % i, [128, 4, 64], F32), Buf("sm%d" % i)) for i in range(4)])
            tk = Rot([(sbuf(pa, "imp%d" % i, [128, 64], F32), sbuf(pa, "sc%d" % i, [128, 64], F32), sbuf(pa, "wk%d" % i, [128, 64], F32),
                       sbuf(pa, "m8a%d" % i, [128, 8], F32), sbuf(pa, "m8b%d" % i, [128, 8], F32), Buf("tk%d" % i)) for i in range(2)])
            dbgt = Rot([(sbuf(pa, "dbgt%d" % i, [128, 1024], F32), Buf("dbgt%d" % i)) for i in range(2)]) if debug else None

            prepped = {}

            Qtok = Rot([(sbuf(pa, "Qtok%d" % i, [128, 1024], BF), Buf("Qtok%d" % i)) for i in range(2)])
            gtmp = Rot([(sbuf(pa, "gtmp%d" % i, [128, 24], F32), Buf("gtmp%d" % i)) for i in range(2)])

            def prep_g(i):
                I = 2 * i + 1
                xt, Bx = xts.next()
                S.add('sp', lambda e: e.dma_start(out=xt[:], in_=xk[I * 128:(I + 1) * 128, :]), writes=[Bx], dma=Bx)
                hT, BhT = hTq.next()
                tmp = tmps.next()
                yield from norm_h_g(xt[:], Bx, gam_mix[:], tmp, lnexp=True)
                yield from trans_g(tmp[2], tmp[3], hT[:], BhT, 7, eng='dve')
                QA, BQA = QAs.next()
                QS, BQSlo, BQShi = QSs.next()
                gn, Bgn = gns.next()
                Qt, BQt = Qtok.next()
                for m in range(2):
                    for k in range(8):
                        S.add('pe', lambda e, k=k, m=m: e.matmul(ps[7][:, :], lhsT=hT[:, k, :], rhs=wq_t[:, k, m * 512:(m + 1) * 512],
                                                                 start=(k == 0), stop=(k == 7)), reads=[BhT, Bwq[k // 4]], writes=[Bps[7]])
                        if k % 2 == 1:
                            yield
                    S.add('dve', lambda e, m=m: e.tensor_scalar(out=Qt[:, m * 512:(m + 1) * 512], in0=ps[7][:, :], scalar1=0.125, scalar2=None,
                                                                op0=ALU.mult), reads=[Bps[7]], writes=[BQt])
                    yield
                for k in range(8):
                    S.add('pe', lambda e, k=k: e.matmul(ps[7][:, 0:24], lhsT=hT[:, k, :], rhs=wgn_t[:, k, :], start=(k == 0), stop=(k == 7)),
                          reads=[BhT, Bw], writes=[Bps[7]])
                yield
                gt_, Bgt = gtmp.next()
                S.add('act', lambda e: e.activation(out=gt_[:], in_=ps[7][:, 0:24], func=AF.Exp, scale=-1.0), reads=[Bps[7]], writes=[Bgt])
                yield
                S.add('dve', lambda e: e.tensor_scalar(out=gt_[:], in0=gt_[:], scalar1=1.0, scalar2=None, op0=ALU.add), reads=[Bgt], writes=[Bgt])
                S.add('dve', lambda e: e.reciprocal(out=gn[:], in_=gt_[:]), reads=[Bgt], writes=[Bgn])
                yield
                pT = ps[7][:].bitcast(BF)
                for m in range(2):
                    for hh in range(8):
                        S.add('pe', lambda e, m=m, hh=hh: e.transpose(out=pT[0:64, hh * 128:(hh + 1) * 128],
                                                                      in_=Qt[:, m * 512 + hh * 64:m * 512 + (hh + 1) * 64], identity=ident[:]),
                              reads=[BQt, Bconst], writes=[Bps[7]])
                        if hh % 4 == 3:
                            yield
                    for g in range(2):
                        dst, Bdst = (QA[0:64, g, :], BQA) if m == 0 else (QS[0:64, g, :], BQSlo)
                        S.add('dve', lambda e, dst=dst, g=g: e.tensor_copy(out=dst, in_=pT[0:64, g * 512:(g + 1) * 512]), reads=[Bps[7]], writes=[Bdst])
                        yield
                prepped[i] = (QA, BQA, QS, BQSlo, BQShi, gn, Bgn)

            def run_steps_g(steps, dyn=None):
                n = len(steps)
                banks = [sbank.next() for _ in range(n)]

                def qk(j):
                    stp = steps[j]
                    b = banks[j]
                    l, r, rd = stp['qk']
                    has_m = stp['mask'] is not None and stp['mask'][0] == 'pe'
                    S.add('pe', lambda e: e.matmul(ps[b][:, :], lhsT=l, rhs=r, start=True, stop=not has_m), reads=rd, writes=[Bps[b]])
                    if has_m:
                        _, l2, r2, rd2 = stp['mask']
                        S.add('pe', lambda e: e.matmul(ps[b][:, :], lhsT=l2, rhs=r2, start=False, stop=True), reads=rd2, writes=[Bps[b]])
                qk(0)
                if n > 1:
                    qk(1)
                for j in range(n):
                    if j + 2 < n:
                        qk(j + 2)
                    stp = steps[j]
                    b = banks[j]
                    Pt, BPt = Pts.next()
                    if stp['abias'] is None:
                        S.add('act', lambda e, b=b, Pt=Pt: e.activation(out=Pt[:], in_=ps[b][:, :], func=AF.Exp),
                              reads=[Bps[b]], writes=[BPt])
                    else:
                        S.add('act', lambda e, b=b, Pt=Pt, stp=stp: e.activation(out=Pt[:], in_=ps[b][:, :], func=AF.Exp, bias=stp['abias']),
                              reads=[Bps[b], Bconst], writes=[BPt])
                    if stp['mask'] is not None and stp['mask'][0] == 'mul':
                        _, map_, mrd = stp['mask']
                        S.add('dve', lambda e, Pt=Pt, map_=map_: e.tensor_mul(out=Pt[:], in0=Pt[:], in1=map_), reads=[BPt] + mrd, writes=[BPt])
                    for h in range(4):
                        acc_ap, vr = stp['v'][h]
                        S.add('pe', lambda e, h=h, acc_ap=acc_ap, vr=vr, Pt=Pt, stp=stp: e.matmul(
                            acc_ap, lhsT=Pt[:, h * 128:(h + 1) * 128], rhs=vr, start=stp['first'][h], stop=stp['last'],
                            skip_group_check=True), reads=[BPt] + stp['vreads'], writes=stp['accB'])
                    if stp['post'] is not None:
                        r_ = stp['post']()
                        if r_ is not None:
                            if dyn is not None:
                                dyn.append(r_)
                            else:
                                for _ in r_:
                                    pass
                    yield

            def run_steps(steps):
                for _ in run_steps_g(steps):
                    pass

            def do_tile(i):
                I = 2 * i + 1
                if i == 0:
                    zipgens([prep_g(0)])
                QA, BQA, QS, BQSlo, BQShi, gn, Bgn = prepped.pop(i)
                ya_bf, Bya = yab.next()
                yb, Byb = ybf.next()
                steps = []
                dyn = []
                for g in range(2):
                    bX, bY = abank.next(), abank.next()
                    accs = [ps[bX][:, 0:129], ps[bX][:, 129:258], ps[bY][:, 0:129], ps[bY][:, 129:258]]

                    def post_cmp(g=g, bX=bX, bY=bY):
                        smA, smB, smT, Bsm = sm.next()
                        imp, sc, wk, m8a, m8b, Btk = tk.next()
                        ca, Bca = caccs.next()
                        for pr, bb in enumerate((bX, bY)):
                            S.add('dve', lambda e, pr=pr, bb=bb: e.tensor_copy(out=ca[:, pr * 2:pr * 2 + 2, :],
                                                                               in_=ps[bb][:, 0:258].rearrange("p (h c) -> p h c", c=129)),
                                  reads=[Bps[bb]], writes=[Bca])

                        def cmp_rest_g():
                            S.add('dve', lambda e: e.tensor_scalar(out=smA[:, 0:4], in0=ca[:, :, 64], scalar1=1e-30, scalar2=None, op0=ALU.max),
                                  reads=[Bca], writes=[Bsm])
                            yield
                            S.add('dve', lambda e: e.reciprocal(out=smA[:, 0:4], in_=smA[:, 0:4]), reads=[Bsm], writes=[Bsm])
                            yield
                            gsl = gn[:, g * 12:(g + 1) * 12].rearrange("p (h b) -> p h b", b=3)[:, :, 0]
                            S.add('dve', lambda e: e.tensor_mul(out=smB[:, 0:4], in0=smA[:, 0:4], in1=gsl), reads=[Bsm, Bgn], writes=[Bsm])
                            yield
                            S.add('dve', lambda e: e.tensor_tensor(
                                out=yb[:, g * 256:(g + 1) * 256].rearrange("p (h d) -> p h d", d=64),
                                in0=ca[:, :, 0:64], in1=bc_last(smB[:, 0:4], 64), op=ALU.mult), reads=[Bca, Bsm], writes=[Byb])
                            yield
                            for h in range(4):
                                if h == 0:
                                    S.add('dve', lambda e, h=h: e.tensor_scalar(out=imp[:], in0=ca[:, h, 65:129], scalar1=smA[:, h:h + 1],
                                                                                scalar2=None, op0=ALU.mult), reads=[Bca, Bsm], writes=[Btk])
                                else:
                                    S.add('dve', lambda e, h=h: e.scalar_tensor_tensor(out=imp[:], in0=ca[:, h, 65:129], scalar=smA[:, h:h + 1],
                                                                                       in1=imp[:], op0=ALU.mult, op1=ALU.add),
                                          reads=[Bca, Bsm, Btk], writes=[Btk])
                                yield
                            yield from topk_g()

                        def topk_g():
                            S.add('dve', lambda e: e.tensor_add(out=sc[:], in0=imp[:], in1=scoreadd_t[:, i, :]), reads=[Btk, Bsa], writes=[Btk])
                            yield
                            S.add('dve', lambda e: e.max(out=m8a[:], in_=sc[:]), reads=[Btk], writes=[Btk])
                            yield
                            S.add('dve', lambda e: e.match_replace(out=wk[:], in_to_replace=m8a[:], in_values=sc[:], imm_value=-3.0e38),
                                  reads=[Btk], writes=[Btk])
                            yield
                            S.add('dve', lambda e: e.max(out=m8b[:], in_=wk[:]), reads=[Btk], writes=[Btk])
                            yield
                            S.add('dve', lambda e: e.tensor_scalar(out=wk[:], in0=sc[:], scalar1=m8b[:, 7:8], scalar2=None, op0=ALU.is_ge),
                                  reads=[Btk], writes=[Btk])
                            yield
                            S.add('dve', lambda e: e.tensor_mul(out=wk[:], in0=wk[:], in1=allowed_t[:, i, :]), reads=[Btk, Bsa], writes=[Btk])
                            yield
                            ng, Bng = negs.next()
                            S.add('dve', lambda e: e.tensor_scalar(out=ng[:, 64:128], in0=wk[:], scalar1=-1.0, scalar2=-NEGM, op0=ALU.add, op1=ALU.mult),
                                  reads=[Btk], writes=[Bng])
                            yield
                            pT = ps[7][:].bitcast(BF)
                            S.add('pe', lambda e: e.transpose(out=pT[:, 0:128], in_=ng[:], identity=ident[:]), reads=[Bng, Bconst], writes=[Bps[7]])
                            yield
                            src = pT[64:128, 0:128]
                            srcb = bass.AP(src.tensor, src.offset, [list(src.ap[0]), [0, 4], list(src.ap[1])])
                            S.add('dve', lambda e: e.tensor_copy(out=QS[64:128, g, :].rearrange("p (h q) -> p h q", h=4), in_=srcb),
                                  reads=[Bps[7]], writes=[BQShi[g]])
                            yield

                        dyn.append(cmp_rest_g())
                    cts = [0, 1] if 8 * I + 6 >= 128 else [0]
                    for ct in cts:
                        steps.append(dict(
                            qk=(KCT[:, g, ct * 128:(ct + 1) * 128], QS[:, g, :], [BKCT, BQSlo, BQShi[g]]),
                            mask=('pe', shift_t[:, i, ct * 128:(ct + 1) * 128], bandx[:, g, :], [Bsa, Bbias]),
                            abias=None,
                            v=[(accs[h], VCx[:, ct, g, :]) for h in range(4)], vreads=[BVCx],
                            accB=[Bps[bX], Bps[bY]], first=[ct == 0 and h in (0, 2) for h in range(4)], last=(ct == cts[-1]),
                            post=post_cmp if ct == cts[-1] else None))

                def std_branch(g, Js, klhs, Bk, qrhs, Bq, K, mixer, vkind, gate_br, is_swa, is_slc, first_yb, last_yb):
                    bA = abank.next()
                    a3 = ps[bA][:, 0:260].rearrange("p (h c) -> p h c", c=65)

                    def post():
                        smA, smB, smT, Bsm = sm.next()
                        if is_swa:
                            S.add('dve', lambda e: e.tensor_add(out=smA[:, 0:4], in0=a3[:, :, 64], in1=sinkexp[:, g * 4:(g + 1) * 4]),
                                  reads=[Bps[bA], Bconst], writes=[Bsm])
                            yield
                            S.add('dve', lambda e: e.reciprocal(out=smB[:, 0:4], in_=smA[:, 0:4]), reads=[Bsm], writes=[Bsm])
                            yield
                            S.add('dve', lambda e: e.tensor_tensor(
                                out=ya_bf[:, g * 256:(g + 1) * 256].rearrange("p (h d) -> p h d", d=64),
                                in0=a3[:, :, 0:64], in1=bc_last(smB[:, 0:4], 64), op=ALU.mult), reads=[Bps[bA], Bsm], writes=[Bya])
                            yield
                            return
                        S.add('dve', lambda e: e.reciprocal(out=smA[:, 0:4], in_=a3[:, :, 64]), reads=[Bps[bA]], writes=[Bsm])
                        yield
                        gsl = gn[:, g * 12:(g + 1) * 12].rearrange("p (h b) -> p h b", b=3)[:, :, gate_br]
                        S.add('dve', lambda e: e.tensor_mul(out=smB[:, 0:4], in0=smA[:, 0:4], in1=gsl), reads=[Bsm, Bgn], writes=[Bsm])
                        yield
                        S.add('dve', lambda e: e.tensor_tensor(out=smT[:], in0=a3[:, :, 0:64], in1=bc_last(smB[:, 0:4], 64), op=ALU.mult),
                              reads=[Bps[bA], Bsm], writes=[Bsm])
                        yield
                        ybg = yb[:, g * 256:(g + 1) * 256].rearrange("p (h d) -> p h d", d=64)
                        if last_yb:
                            S.add('pool', lambda e: e.tensor_add(
                                out=ya_bf[:, 512 + g * 256:512 + (g + 1) * 256].rearrange("p (h d) -> p h d", d=64), in0=ybg, in1=smT[:]),
                                reads=[Byb, Bsm], writes=[Bya])
                        else:
                            S.add('pool', lambda e: e.tensor_add(out=ybg, in0=ybg, in1=smT[:]), reads=[Byb, Bsm], writes=[Byb])
                        yield
                    for n_, J in enumerate(Js):
                        mask = None
                        if J == I:
                            mask = ('pe', ident[:], bias_t[:, mixer * 4 + 0 + g, :], [Bconst, Bbias])
                        elif J == I - 1:
                            mask = ('pe', ident[:], bias_t[:, mixer * 4 + 2 + g, :], [Bconst, Bbias])
                        elif (not is_swa) and (not is_slc) and J == I - 4:
                            mask = ('pe', ident[:], farlow_t[:], [Bconst, Bd4])
                        rd = [Bk, Bq] + ([BQShi[g]] if is_slc else [])
                        steps.append(dict(
                            qk=(klhs[0:K, g, J * 128:(J + 1) * 128], qrhs[0:K, g, :], rd),
                            mask=mask, abias=(keymask_t[:, J:J + 1] if (J == 0 and not is_slc) else None),
                            v=[(a3[:, h, :], Vx[:, J, vkind, g, :]) for h in range(4)], vreads=[BVx],
                            accB=[Bps[bA]], first=[n_ == 0 and h == 0 for h in range(4)], last=(n_ == len(Js) - 1),
                            post=post if n_ == len(Js) - 1 else None))

                for g in range(2):
                    std_branch(g, list(range(max(0, I - 4), I + 1)), KW, BKW, QS, BQSlo, 128, 1, 2, 2, False, False, False, False)
                    std_branch(g, [I - 1, I], KA, BKA, QA, BQA, 128, 0, 0, 0, True, False, False, False)
                dyn.append(run_steps_g(steps, dyn))
                zipgens_dyn(dyn)
                issue_bg2(1)
                steps = []
                for g in range(2):
                    std_branch(g, list(range(0, I + 1)), KS, BKS, QS, BQSlo, 128, 1, 1, 1, False, True, False, True)
                dyn2 = []
                dyn2.append(run_steps_g(steps, dyn2))
                if i + 1 < NQ:
                    dyn2.append(prep_g(i + 1))
                zipgens_dyn(dyn2)
                S.add('sp', lambda e, ya_bf=ya_bf, i=i: e.dma_start(out=yabd[i * 128:(i + 1) * 128, :], in_=ya_bf[:]), reads=[Bya], dma=Bya)
                if debug:
                    dt_, Bdt = dbgt.next()
                    S.add('dve', lambda e, dt_=dt_, ya_bf=ya_bf: e.tensor_copy(out=dt_[:], in_=ya_bf[:]), reads=[Bya], writes=[Bdt])
                    S.add('sp', lambda e, dt_=dt_, i=i: e.dma_start(out=dbg['ya'][i * 128:(i + 1) * 128, :], in_=dt_[:, 0:512]), reads=[Bdt], dma=Bdt)
                    S.add('sp', lambda e, dt_=dt_, i=i: e.dma_start(out=dbg['yb'][i * 128:(i + 1) * 128, :], in_=dt_[:, 512:1024]), reads=[Bdt], dma=Bdt)
            for i_ in range(NQ):
                do_tile(i_)
            S.emit()
        att.close()

        hsend = sbuf(st, "hsend", [128, 8, NQ, 2], BF)
        Bhs = Buf("hsend")
        wfo_s = ExitStack()
        wfo_t = sbuf(wfo_s, "wfo_t", [128, 22, 1024], BF)
        Bwo_ffn = Buf("wfo")
        with ExitStack() as pb:
            wup_t = sbuf(pb, "wup_t", [128, 2, 4, 1024], BF)
            wo_t = sbuf(pb, "wo_t", [128, 8, 1024], BF)
            gam_ffn = sbuf(pb, "gam_ffn", [128, 1024], F32)
            Bwup = [Buf("wup%d" % m) for m in range(2)]
            Bwo = [Buf("wo%d" % k) for k in range(2)]
            S.add('sp', lambda e: e.dma_start(out=gam_ffn[:], in_=gam[1:2, :].rearrange("a n -> (a n)").partition_broadcast(128)),
                  writes=[Bgam], dma=Bgam)
            xts = Rot([(sbuf(pb, "xb%d" % i, [128, 1024], F32), Buf("xb%d" % i)) for i in range(4)])
            tmps = Rot([mk_tmp(pb, "c%d" % i) for i in range(3)])
            hTq = Rot([(sbuf(pb, "hTb%d" % i, [128, 8, 128], BF), Buf("hTb%d" % i)) for i in range(2)])
            sgs = Rot([(sbuf(pb, "sg%d" % i, [128, 2048], F32), Buf("sg%d" % i)) for i in range(2)])
            yabs = Rot([(sbuf(pb, "yabl%d" % i, [128, 1024], BF), Buf("yabl%d" % i)) for i in range(3)])
            yTs = Rot([(sbuf(pb, "yT%d" % i, [128, 8, 128], BF), Buf("yT%d" % i)) for i in range(2)])
            mgs = Rot([(sbuf(pb, "mg%d" % i, [128, 1024], F32), sbuf(pb, "mgt%d" % i, [128, 1024], F32),
                        sbuf(pb, "mgb%d" % i, [128, 1024], BF), Buf("mg%d" % i)) for i in range(2)])
            mTs = Rot([(sbuf(pb, "mT%d" % i, [128, 8, 128], BF), Buf("mT%d" % i)) for i in range(2)])
            x1s = Rot([(sbuf(pb, "x1_%d" % i, [128, 1024], F32), Buf("x1_%d" % i)) for i in range(2)])
            h2Ts = Rot([(sbuf(pb, "h2T%d" % i, [128, 8, 128], BF), Buf("h2T%d" % i)) for i in range(2)])
            gb = Rot([0, 1])
            ub = Rot([2, 3])
            ob = Rot([4, 5])
            def pb_weights():
                for c in range(4):
                    S.add('sp', lambda e, c=c: e.dma_start(out=wup_t[:, 0, c, :], in_=wupab[c * 128:(c + 1) * 128, :]), reads=[Bpre['wup']], writes=[Bwup[0]], dma=Bwup[0], nowaw=True)
                for c in range(4):
                    S.add('sp', lambda e, c=c: e.dma_start(out=wup_t[:, 1, c, :], in_=wupbb[c * 128:(c + 1) * 128, :]), reads=[Bpre['wup']], writes=[Bwup[1]], dma=Bwup[1], nowaw=True)
                for k in range(8):
                    S.add('sp', lambda e, k=k: e.dma_start(out=wo_t[:, k, :], in_=wob[k * 128:(k + 1) * 128, :]), reads=[Bpre['wo']], writes=[Bwo[k // 4]], dma=Bwo[k // 4], nowaw=True)

            wfoq = list(range(22))

            def issue_wfo(n):
                for _ in range(n):
                    if wfoq:
                        c = wfoq.pop(0)
                        S.add('sp', lambda e, c=c: e.dma_start(out=wfo_t[:, c, :], in_=wfob[c * 128:(c + 1) * 128, :]), reads=[Bpre['wfo']], writes=[Bwo_ffn], dma=Bwo_ffn, nowaw=True)
            stA, stB, stA1 = {}, {}, {}

            def stageA(i):
                I = 2 * i + 1
                xt, Bx = xts.next()
                S.add('sp', lambda e: e.dma_start(out=xt[:], in_=xk[I * 128:(I + 1) * 128, :]), writes=[Bx], dma=Bx)
                yl, Byl = yabs.next()
                S.add('sp', lambda e: e.dma_start(out=yl[:], in_=yabd[i * 128:(i + 1) * 128, :]), writes=[Byl], dma=Byl)
                hT, BhT = hTq.next()
                yield from norm_T_g(xt[:], Bx, gam_mix[:], hT[:], BhT, tmps.next(), 7)
                stA1[i] = (xt, Bx, yl, Byl, hT, BhT)

            def stageA2(i):
                xt, Bx, yl, Byl, hT, BhT = stA1.pop(i)
                sg, Bsg = sgs.next()
                for cc in range(4):
                    b = gb.next()
                    for k in range(8):
                        S.add('pe', lambda e, k=k, b=b, cc=cc: e.matmul(ps[b][:, :], lhsT=hT[:, k, :], rhs=wg_t[:, k, cc * 512:(cc + 1) * 512],
                                                                        start=(k == 0), stop=(k == 7)), reads=[BhT, Bwg], writes=[Bps[b]])
                    S.add('act', lambda e, b=b, cc=cc: e.activation(out=sg[:, cc * 512:(cc + 1) * 512], in_=ps[b][:, :], func=AF.Sigmoid),
                          reads=[Bps[b]], writes=[Bsg])
                    yield
                yT, ByT = yTs.next()
                pT = ps[6][:].bitcast(BF)
                for c in range(8):
                    S.add('pe', lambda e, c=c: e.transpose(out=pT[:, c * 128:(c + 1) * 128], in_=yl[:, c * 128:(c + 1) * 128], identity=ident[:]),
                          reads=[Byl, Bconst], writes=[Bps[6]])
                S.add('dve', lambda e: e.tensor_copy(out=yT[:], in_=pT[:, 0:1024].rearrange("p (k t) -> p k t", k=8)), reads=[Bps[6]], writes=[ByT])
                yield
                stA[i] = (xt, Bx, sg, Bsg, yT, ByT)

            def stageB(i):
                xt, Bx, sg, Bsg, yT, ByT = stA.pop(i)
                mg, mgt, mgb, Bmg = mgs.next()
                for m in range(2):
                    for hf in range(2):
                        b = ub.next()
                        for c in range(4):
                            S.add('pe', lambda e, c=c, b=b, m=m, hf=hf: e.matmul(ps[b][:, :], lhsT=yT[:, m * 4 + c, :],
                                                                                rhs=wup_t[:, m, c, hf * 512:(hf + 1) * 512],
                                                                                start=(c == 0), stop=(c == 3)), reads=[ByT, Bwup[m]], writes=[Bps[b]])
                        dst = mg if m == 0 else mgt
                        S.add('dve', lambda e, b=b, m=m, hf=hf, dst=dst: e.tensor_mul(out=dst[:, hf * 512:(hf + 1) * 512], in0=ps[b][:, :],
                                                                                     in1=sg[:, m * 1024 + hf * 512:m * 1024 + (hf + 1) * 512]),
                              reads=[Bps[b], Bsg], writes=[Bmg])
                        yield
                S.add('dve', lambda e: e.tensor_add(out=mgb[:], in0=mg[:], in1=mgt[:]), reads=[Bmg], writes=[Bmg])
                yield
                mT, BmT = mTs.next()
                pT7 = ps[7][:].bitcast(BF)
                for c in range(8):
                    S.add('pe', lambda e, c=c: e.transpose(out=pT7[:, c * 128:(c + 1) * 128], in_=mgb[:, c * 128:(c + 1) * 128], identity=ident[:]),
                          reads=[Bmg, Bconst], writes=[Bps[7]])
                S.add('act', lambda e: e.copy(out=mT[:], in_=pT7[:, 0:1024].rearrange("p (k t) -> p k t", k=8)), reads=[Bps[7]], writes=[BmT])
                yield
                stB[i] = (xt, Bx, mT, BmT)

            def stageC(i):
                xt, Bx, mT, BmT = stB.pop(i)
                x1, Bx1 = x1s.next()
                for hf in range(2):
                    b = ob.next()
                    for c in range(8):
                        S.add('pe', lambda e, c=c, b=b, hf=hf: e.matmul(ps[b][:, :], lhsT=mT[:, c, :], rhs=wo_t[:, c, hf * 512:(hf + 1) * 512],
                                                                        start=(c == 0), stop=(c == 7)), reads=[BmT, Bwo[c // 4]], writes=[Bps[b]])
                    S.add('dve', lambda e, b=b, hf=hf: e.tensor_add(out=x1[:, hf * 512:(hf + 1) * 512], in0=ps[b][:, :],
                                                                    in1=xt[:, hf * 512:(hf + 1) * 512]), reads=[Bps[b], Bx], writes=[Bx1])
                    yield
                S.add('pool', lambda e: e.dma_start(out=x1d[i * 128:(i + 1) * 128, :], in_=x1[:]), reads=[Bx1], dma=Bx1)
                if debug:
                    S.add('sp', lambda e: e.dma_start(out=dbg['x1'][i * 128:(i + 1) * 128, :], in_=x1[:]), reads=[Bx1], dma=Bx1)
                h2T, Bh2T = h2Ts.next()
                yield from norm_T_g(x1[:], Bx1, gam_ffn[:], h2T[:], Bh2T, tmps.next(), 6)
                S.add('pool', lambda e: e.dma_start(
                    out=h2Td.rearrange("p (k t) -> p k t", k=8)[:, :, i * 128:(i + 1) * 128], in_=h2T[:]), reads=[Bh2T], dma=Bh2T)
                S.add('act', lambda e: e.copy(out=hsend[:, :, i, :], in_=h2T[:, :, 126:128]), reads=[Bh2T], writes=[Bhs])

            for s_ in range(NQ + 3):
                zipgens([stageC(s_ - 3) if 0 <= s_ - 3 < NQ else None,
                         stageB(s_ - 2) if 0 <= s_ - 2 < NQ else None,
                         stageA2(s_ - 1) if 0 <= s_ - 1 < NQ else None,
                         stageA(s_) if s_ < NQ else None])
                if s_ == 0:
                    pb_weights()
                elif s_ >= 2:
                    issue_wfo(2)
            issue_wfo(22)
            S.emit()

        with ExitStack() as pf:
            wfi_t = sbuf(pf, "wfi_t", [128, 8, 5632], BF)
            cw_t = sbuf(pf, "cw_t", [128, 4, 44], F32)
            a_t = sbuf(pf, "a_t", [128, 1], F32)
            gam_fin = sbuf(pf, "gam_fin", [128, 1024], F32)
            hrecv = sbuf(pf, "hrecv", [128, 2, 8, NQ, 2], BF)
            hh = sbuf(pf, "hh", [128, 8, NQ, 2], BF)
            hcb = sbuf(pf, "hcb", [128, 44, NQ, 2], F32)
            sav = arena0[:, 15360:16064].bitcast(F32).rearrange("p (c t x) -> p c t x", c=44, t=4)
            hd = hcb[:, 0:8, :, :]
            Bsav = Buf("sav")
            Bwfi = [Buf("wfi%d" % c) for c in range(22)]
            Bcw, Bcin, Bcout, Bhr, Bhh = [Buf(n) for n in "cw cin cout hr hh".split()]
            Bhcb = [Buf("hcb%d" % c) for c in range(44)]
            S.add('sp', lambda e: e.dma_start(out=cin.ap(), in_=hsend[:].rearrange("p k t c -> p (k t c)")), reads=[Bhs], writes=[Bcin], dma=Bcin)
            S.add('pool', lambda e: e.collective_compute("AllGather", ALU.bypass, replica_groups=[[0, 1], [2, 3], [4, 5], [6, 7]],
                                                         ins=[cin.ap().opt()], outs=[cout.ap().opt()]), reads=[Bcin], writes=[Bcout], own_sem=True)
            S.add('sp', lambda e: e.dma_start(out=cw_t[:], in_=cwb), writes=[Bcw], dma=Bcw)
            S.add('sp', lambda e: e.dma_start(out=a_t[:], in_=asel), writes=[Bcw], dma=Bcw)
            S.add('sp', lambda e: e.dma_start(out=gam_fin[:], in_=gam[2:3, :].rearrange("a n -> (a n)").partition_broadcast(128)),
                  writes=[Bgam], dma=Bgam)
            aT = arena0[:, 0:11264].rearrange("p (c n) -> p c n", c=22)
            BaT = Buf("actT")
            hg = arena0[:, 11264:11264 + 4096].rearrange("p (k t) -> p k t", k=8)
            Bhg = Buf("h2g")
            tus = Rot([(sbuf(pf, "tu%d" % i, [128, 4, 128], F32), Buf("tu%d" % i)) for i in range(3)])
            tgs = Rot([(sbuf(pf, "tg%d" % i, [128, 4, 128], F32), Buf("tg%d" % i)) for i in range(3)])
            htm = Rot([(sbuf(pf, "htm%d" % i, [128, NQ], F32), Buf("htm%d" % i)) for i in range(2)])
            x1s = Rot([(sbuf(pf, "x1f%d" % i, [128, 1024], F32), Buf("x1f%d" % i)) for i in range(2)])
            fin = Rot([(sbuf(pf, "fs%d" % i, [128, 1], F32), sbuf(pf, "fr%d" % i, [128, 1], F32),
                        sbuf(pf, "fo%d" % i, [128, 1024], F32), Buf("fin%d" % i)) for i in range(1)])
            ubk = Rot([0, 1])
            gbk = Rot([2, 3])
            obk = Rot([4, 5])
            hbk = Rot([4, 5])
            S.add('sp', lambda e: e.dma_start(out=hg, in_=h2Td.rearrange("p (k t) -> p k t", k=8)[:, :, 0:512]), writes=[Bhg], dma=Bhg)
            Bwfi_h = [[Buf("wfi%d_%d" % (half, c2)) for c2 in range(6)] for half in range(2)]
            for c2 in range(11):
                for half in range(2):
                    c0 = (2 * c2 + 22 * half) * 128
                    S.add('sp', lambda e, c0=c0: e.dma_start(out=wfi_t[:, :, c0:c0 + 256],
                                                            in_=wfib[:, c0:c0 + 256].rearrange("(k p) n -> p k n", p=128)),
                          reads=[Bpre['wfi']], writes=[Bwfi_h[half][c2 // 2]], dma=Bwfi_h[half][c2 // 2], nowaw=True)
            S.add('sp', lambda e: e.dma_start(out=hrecv[:].rearrange("p r k t c -> p r (k t c)"),
                                              in_=cout.ap().rearrange("(r p) n -> p r n", p=128)), reads=[Bcout], writes=[Bhr], dma=Bhr)
            pend = []
            ew_eng = ['pool']
            ubk3 = Rot([0, 1, 6])
            gbk3 = Rot([2, 3, 7])

            def fin_pair(c, res):
                (tu, Btu), (tg, Btg) = res
                S.add('act', lambda e: e.activation(out=tg[:], in_=tg[:], func=AF.Silu), reads=[Btg], writes=[Btg])
                S.add(ew_eng[0], lambda e: e.tensor_mul(out=aT[:, c, :].rearrange("p (t n) -> p t n", t=4), in0=tu[:], in1=tg[:]),
                      reads=[Btu, Btg], writes=[BaT])
            def hh_compute():
                G0 = hrecv[:, 0]
                G1 = hrecv[:, 1]
                S.add('dve', lambda e: e.tensor_copy(out=hd[:, :, 0, :], in_=G0[:, :, 0, :]), reads=[Bhr], writes=[Bhh])
                S.add('dve', lambda e: e.tensor_sub(out=hd[:, :, 1:NQ, :], in0=G0[:, :, 1:NQ, :], in1=G1[:, :, 0:NQ - 1, :]), reads=[Bhr], writes=[Bhh])
                S.add('dve', lambda e: e.tensor_scalar(out=hh[:, :, 0, :], in0=hd[:, :, 0, :], scalar1=a_t[:, 0:1], scalar2=None, op0=ALU.mult),
                      reads=[Bhh, Bcw], writes=[Bhh])
                S.add('dve', lambda e: e.scalar_tensor_tensor(out=hh[:, :, 1:NQ, :], in0=hd[:, :, 1:NQ, :], scalar=a_t[:, 0:1], in1=G1[:, :, 0:NQ - 1, :],
                                                              op0=ALU.mult, op1=ALU.add), reads=[Bhh, Bcw, Bhr], writes=[Bhh])

            def halo_chain(cc):
                half, c = cc // 22, cc % 22
                hb_ = hbk.next()
                for k in range(8):
                    S.add('pe', lambda e, k=k: e.matmul(ps[hb_][:, 0:32], lhsT=wfi_t[:, k, cc * 128:(cc + 1) * 128],
                                                        rhs=hh[:, k, :, :].rearrange("p t c -> p (t c)"),
                                                        start=(k == 0), stop=(k == 7)), reads=[Bwfi_h[half][c // 4], Bhh], writes=[Bps[hb_]])
                p2 = ps[hb_][:, 0:32].rearrange("p (t c) -> p t c", c=2)
                ht, Bht = htm.next()
                S.add('dve', lambda e: e.tensor_scalar(out=hcb[:, cc, :, 1], in0=p2[:, :, 1], scalar1=cw_t[:, 0, cc:cc + 1],
                                                       scalar2=None, op0=ALU.mult), reads=[Bps[hb_], Bcw], writes=[Bhcb[cc]])
                S.add('dve', lambda e: e.tensor_scalar(out=ht[:], in0=p2[:, :, 0], scalar1=cw_t[:, 0, cc:cc + 1],
                                                       scalar2=None, op0=ALU.mult), reads=[Bps[hb_], Bcw], writes=[Bht])
                S.add('dve', lambda e: e.scalar_tensor_tensor(out=hcb[:, cc, :, 0], in0=p2[:, :, 1], scalar=cw_t[:, 1, cc:cc + 1],
                                                              in1=ht[:], op0=ALU.mult, op1=ALU.add),
                      reads=[Bps[hb_], Bcw, Bht], writes=[Bhcb[cc]])
            hq = [p + 22 * h_ for p in range(22) for h_ in range(2)]
            for Gq in range(4):
                if Gq > 0:
                    ew_eng[0] = 'pool'
                for c in range(22):
                    res = []
                    for half, bk, ts_ in ((0, ubk3, tus), (1, gbk3, tgs)):
                        cc = c + 22 * half
                        b = bk.next()
                        for k in range(8):
                            S.add('pe', lambda e, k=k, b=b, cc=cc: e.matmul(ps[b][:, :], lhsT=wfi_t[:, k, cc * 128:(cc + 1) * 128], rhs=hg[:, k, :],
                                                                            start=(k == 0), stop=(k == 7)), reads=[Bwfi_h[half][c // 4], Bhg], writes=[Bps[b]])
                        tt, Btt = ts_.next()
                        p3 = ps[b][:, :].rearrange("p (t n) -> p t n", t=4)
                        S.add('act', lambda e, b=b, cc=cc, tt=tt: e.activation(out=tt[:].rearrange("p t n -> p (t n)"), in_=ps[b][:, :], func=AF.Identity,
                                                                               scale=cw_t[:, 2, cc:cc + 1], bias=cw_t[:, 3, cc:cc + 1]),
                              reads=[Bps[b], Bcw], writes=[Btt])
                        S.add('dve', lambda e, p3=p3, cc=cc, tt=tt: e.scalar_tensor_tensor(out=tt[:, :, 1:128], in0=p3[:, :, 0:127], scalar=cw_t[:, 1, cc:cc + 1],
                                                                                          in1=tt[:, :, 1:128], op0=ALU.mult, op1=ALU.add),
                              reads=[Bps[b], Bcw, Btt], writes=[Btt])
                        S.add('dve', lambda e, p3=p3, cc=cc, tt=tt: e.scalar_tensor_tensor(out=tt[:, :, 2:128], in0=p3[:, :, 0:126], scalar=cw_t[:, 0, cc:cc + 1],
                                                                                          in1=tt[:, :, 2:128], op0=ALU.mult, op1=ALU.add),
                              reads=[Bps[b], Bcw, Btt], writes=[Btt])
                        if Gq == 0:
                            S.add(ew_eng[0], lambda e, cc=cc, tt=tt: e.tensor_copy(out=sav[:, cc, :, :], in_=tt[:, :, 0:2]), reads=[Btt], writes=[Bsav])
                        else:
                            S.add(ew_eng[0], lambda e, cc=cc, tt=tt, Gq=Gq: e.tensor_add(out=tt[:, :, 0:2], in0=tt[:, :, 0:2], in1=hcb[:, cc, Gq * 4:(Gq + 1) * 4, :]),
                                  reads=[Btt, Bhcb[cc]], writes=[Btt])
                        res.append((tt, Btt))
                    pend.append((c, res))
                    if len(pend) > 1:
                        fin_pair(*pend.pop(0))
                    if Gq == 0 and c >= 8:
                        if c == 8:
                            hh_compute()
                        for _ in range(3):
                            if hq:
                                halo_chain(hq.pop(0))
                while pend:
                    fin_pair(*pend.pop(0))
                if Gq == 0:
                    while hq:
                        halo_chain(hq.pop(0))
                    S.add('dve', lambda e: e.tensor_add(out=sav[:, :, :, :], in0=sav[:, :, :, :], in1=hcb[:, :, 0:4, :]), reads=[Bsav] + Bhcb, writes=[Bsav])
                    S.add('act', lambda e: e.activation(out=sav[:, 22:44, :, :], in_=sav[:, 22:44, :, :], func=AF.Silu), reads=[Bsav], writes=[Bsav])
                    S.add('dve', lambda e: e.tensor_mul(out=aT[:, :, :].rearrange("p c (t n) -> p c t n", t=4)[:, :, :, 0:2], in0=sav[:, 0:22, :, :],
                                                        in1=sav[:, 22:44, :, :]), reads=[Bsav, BaT], writes=[BaT])
                if Gq + 1 < 4:
                    S.add('sp', lambda e, Gq=Gq: e.dma_start(out=hg, in_=h2Td.rearrange("p (k t) -> p k t", k=8)[:, :, (Gq + 1) * 512:(Gq + 2) * 512]),
                          writes=[Bhg], dma=Bhg)
                for t in range(4):
                    i = Gq * 4 + t
                    x1, Bx1 = x1s.next()
                    S.add('sp', lambda e, x1=x1, i=i: e.dma_start(out=x1[:], in_=x1d[i * 128:(i + 1) * 128, :]), writes=[Bx1], dma=Bx1)
                    x2, Bx2 = x1, Bx1
                    for hf in range(2):
                        b = obk.next()
                        for c in range(22):
                            S.add('pe', lambda e, c=c, b=b, hf=hf, t=t: e.matmul(ps[b][:, :], lhsT=aT[:, c, t * 128:(t + 1) * 128],
                                                                                rhs=wfo_t[:, c, hf * 512:(hf + 1) * 512],
                                                                                start=(c == 0), stop=(c == 21)), reads=[BaT, Bwo_ffn], writes=[Bps[b]])
                        S.add('dve', lambda e, b=b, hf=hf, x1=x1, x2=x2: e.tensor_add(out=x2[:, hf * 512:(hf + 1) * 512], in0=ps[b][:, :],
                                                                                      in1=x1[:, hf * 512:(hf + 1) * 512]), reads=[Bps[b], Bx1], writes=[Bx2])
                    fs, fr, fo, Bf = fin.next()
                    S.add('act', lambda e, fo=fo, fs=fs, x2=x2: e.activation(out=fo[:], in_=x2[:], func=AF.Square, accum_out=fs[:]), reads=[Bx2], writes=[Bf])
                    S.add('dve', lambda e, fs=fs, fr=fr: e.tensor_scalar(out=fr[:], in0=fs[:], scalar1=1.0 / 1024, scalar2=1e-6, op0=ALU.mult, op1=ALU.add),
                          reads=[Bf], writes=[Bf])
                    S.add('act', lambda e, fr=fr: e.activation(out=fr[:], in_=fr[:], func=AF.Sqrt), reads=[Bf], writes=[Bf])
                    S.add('dve', lambda e, fr=fr: e.reciprocal(out=fr[:], in_=fr[:]), reads=[Bf], writes=[Bf])
                    S.add('dve', lambda e, fr=fr, fo=fo, x2=x2: e.scalar_tensor_tensor(out=fo[:], in0=x2[:], scalar=fr[:, 0:1], in1=gam_fin[:],
                                                                                      op0=ALU.mult, op1=ALU.mult), reads=[Bf, Bx2, Bgam], writes=[Bf])
                    S.add('pool', lambda e, fo=fo, i=i: e.dma_start(out=out[i * 128:(i + 1) * 128, :], in_=fo[:]), reads=[Bf], dma=Bf)
            S.emit()
        wfo_s.close()
    return nc


def _t5_bucket(d):
    d = np.maximum(d, 0)
    dd = np.maximum(d, 1).astype(np.float32)
    large = 16 + (np.log(dd / np.float32(16)) / np.float32(math.log(128 / 16)) * np.float32(16)).astype(np.int32)
    large = np.minimum(large, 31)
    return np.where(d < 16, d, large)


def _host_consts(r):
    c = {}
    km = np.zeros((128, 32), np.float32)
    cm = np.zeros((128, 2), np.float32)
    if r == 0:
        km[:, 0] = NEGM
        cm[0:8, 0] = NEGM
    cm[127, 1] = NEGM
    c['keymask'] = km
    c['cmask'] = cm
    k = np.arange(4096)
    c['emat'] = (k[None, :] // 64 == np.arange(64)[:, None]).astype(np.float32).astype(BF_NP)
    sa = np.zeros((128, NQ, 64), np.float32)
    al = np.zeros((128, NQ, 64), np.float32)
    shift = 1 - r
    for i in range(NQ):
        I = 2 * i + 1
        qpos = I * 128 + np.arange(128)
        qblk = qpos // 64
        j = np.arange(64)[None, :]
        first = 2 * shift
        forced = (j == first) | (j == qblk[:, None]) | (j == qblk[:, None] - 1)
        future = j > qblk[:, None]
        dummy = j < first
        a = np.where(forced, 1e30, 0.0)
        a = np.where(future | dummy, -1e30, a)
        sa[:, i, :] = a
        al[:, i, :] = (~(future | dummy)).astype(np.float32)
    c['scoreadd'] = sa
    c['allowed'] = al
    kk = np.arange(128)[:, None]
    qq = np.arange(128)[None, :]
    c['farlow'] = np.tile(np.where(kk > qq, 0.0, NEGM).astype(np.float32), (1, 4)).astype(BF_NP)
    se = np.zeros((33, NQ, 256), np.float32)
    for i in range(NQ):
        I = 2 * i + 1
        for m in range(32):
            cc = 8 * I - 9 + (31 - m)
            if 0 <= cc < 256:
                se[m, i, cc] = 1.0
        lo = 8 * I - 9 + 32
        se[32, i, max(lo, 0):] = 1.0
        se[32, i, 255] = 1.0
        if r == 0:
            se[32, i, 0:8] = 1.0
    c['shiftext'] = se.astype(BF_NP)
    oha = np.zeros((33, 1024), np.float32)
    ohb = np.zeros((33, 1024), np.float32)
    d = np.arange(1024) - 512
    bk = _t5_bucket(d)
    for idx in range(1024):
        if d[idx] < 0:
            oha[32, idx] = 1
            ohb[32, idx] = 1
        else:
            ohb[bk[idx], idx] = 1
            if d[idx] < 128:
                oha[bk[idx], idx] = 1
            else:
                oha[32, idx] = 1
    c['oha'] = oha
    c['ohb'] = ohb
    c['asel'] = np.full((128, 1), float(r), np.float32)
    ov = np.zeros((256, 64), np.float32)
    for j in range(64):
        for m in range(4):
            for n in range(2):
                ci = 4 * j + m - n
                if 0 <= ci < 256:
                    ov[ci, j] += 1
    c['overlap'] = np.ascontiguousarray(ov.reshape(2, 128, 64).transpose(1, 0, 2))
    return c


_NC_CACHE = {}


def run(inputs, debug=False):
    f = lambda a: np.ascontiguousarray(np.asarray(a, dtype=np.float32))
    x = f(inputs['x'])
    w_in = f(inputs['w_in'])[0]
    cs = lambda a, b: w_in[:, a:b]
    shared = {
        'w_kf': np.ascontiguousarray(np.concatenate([cs(O_KA, O_KA + 128), cs(O_KSL, O_KSL + 128), cs(O_KW, O_KW + 128),
                                                     cs(O_KC, O_KC + 128), cs(O_VC, O_VC + 128)], axis=1)),
        'w_v': np.ascontiguousarray(np.concatenate([cs(O_VA, O_VA + 128), cs(O_VSL, O_VSL + 128), cs(O_VW, O_VW + 128)], axis=1)),
        'w_q': np.ascontiguousarray(np.concatenate([cs(O_QA, O_QA + 512), cs(O_QB, O_QB + 512)], axis=1)),
        'w_g': np.ascontiguousarray(np.concatenate([cs(O_GA, O_GA + 1024), cs(O_GB, O_GB + 1024), cs(O_GN, O_GN + 24)], axis=1)),
        'gam': np.ascontiguousarray(np.stack([f(inputs['norm_mix'])[0], f(inputs['norm_ffn'])[0], f(inputs['norm_final'])])),
        'sinks': f(inputs['attn_sinks']),
        'table': f(inputs['rel_bias_table']),
        'posT': np.ascontiguousarray(np.stack([f(inputs['cmp_pos_k'])[0].T, f(inputs['cmp_pos_v'])[0].T])),
        'w1': np.ascontiguousarray(np.stack([f(inputs['cmp_w1_k'])[0], f(inputs['cmp_w1_v'])[0]])),
        'w2': np.ascontiguousarray(np.stack([f(inputs['cmp_w2_k'])[0], f(inputs['cmp_w2_v'])[0]])),
        'w_upa': f(inputs['w_up_a'])[0], 'w_upb': f(inputs['w_up_b'])[0], 'w_out': f(inputs['w_out'])[0],
        'w_fi': f(inputs['w_ffn_in'])[0], 'w_fo': f(inputs['w_ffn_out'])[0],
    }
    cw = f(inputs['conv_w'])[0]
    cb = f(inputs['conv_b'])
    cwb = np.concatenate([cw, cb], axis=0).reshape(4, 44, 128).transpose(2, 0, 1)
    shared['cwb'] = np.ascontiguousarray(cwb)
    consts = [_host_consts(0), _host_consts(1)]
    in_maps = []
    for c in range(8):
        b, r = c // 2, c % 2
        if r == 1:
            xkk = x[b]
        else:
            xkk = np.concatenate([np.zeros((128, 1024), np.float32), x[b][:3968]], axis=0)
        m = dict(shared)
        m.update(consts[r])
        m['xk'] = np.ascontiguousarray(xkk)
        in_maps.append(m)
    key = bool(debug)
    if key not in _NC_CACHE:
        _NC_CACHE[key] = build(debug)
    nc = _NC_CACHE[key]
    res = run_bass_kernel_spmd(nc, in_maps, core_ids=list(range(8)))
    outp = np.zeros((4, 4096, 1024), np.float32)
    for c in range(8):
        b, r = c // 2, c % 2
        o = np.asarray(res.results[c]['out']).reshape(NQ, 128, 1024)
        outp[b].reshape(16, 2, 128, 1024)[:, r] = o
    if debug:
        return outp, res
    return outp


def kernel(**inputs):
    return run(inputs)
```
